# Optimizing a Trainium2 kernel written in Bass

```python
import math
import jax, jax.numpy as jnp
from jax import lax
import numpy as np

D_MODEL = 1024
BATCH = 8
SEQ = 4096
DEPTH = 1

GDN_HEADS = 8
GDN_HEAD_DIM = 64
GDN_WIDTH = GDN_HEADS * GDN_HEAD_DIM
GDN_CONV = 4
GDN_CHUNK = 64
RWKV_HEADS = 8
RWKV_HEAD_DIM = 64
RWKV_WIDTH = RWKV_HEADS * RWKV_HEAD_DIM
DECAY_LORA = 64
ICLR_LORA = 64
GATE_LORA = 160
RWKV_COLS = 3 * RWKV_WIDTH + DECAY_LORA + ICLR_LORA + GATE_LORA
IN_SIZES = (3 * GDN_WIDTH, GDN_WIDTH, GDN_HEADS, GDN_HEADS, RWKV_COLS, D_MODEL, D_MODEL)
D_IN = 3 * GDN_WIDTH + GDN_WIDTH + 2 * GDN_HEADS + RWKV_COLS + 2 * D_MODEL
D_FF = 2816
FFN_CONV = 3
NORM_EPS = 1e-6
LNX_EPS = 64e-5

kernel_name = "hybrid_gdn_rwkv7_convglu_adaln"


def rms_norm(x, w, eps=NORM_EPS):
    xf = x.astype(jnp.float32)
    y = xf * lax.rsqrt(jnp.mean(xf * xf, axis=-1, keepdims=True) + eps)
    return (y * w.astype(jnp.float32)).astype(x.dtype)


def l2_normalize(x, eps=1e-6):
    xf = x.astype(jnp.float32)
    return xf * lax.rsqrt(jnp.sum(xf * xf, axis=-1, keepdims=True) + eps)


def split_cols(p, sizes):
    out, start = [], 0
    for s in sizes:
        out.append(p[..., start:start + s])
        start += s
    return out


def token_shift(p):
    return jnp.pad(p, ((0, 0), (1, 0), (0, 0)))[:, :-1]


def causal_depthwise_conv(x, w):
    k = w.shape[0]
    return lax.conv_general_dilated(
        x, w[:, None, :].astype(x.dtype), window_strides=(1,), padding=((k - 1, 0),),
        dimension_numbers=('NWC', 'WIO', 'NWC'), feature_group_count=x.shape[-1])


def chunk_gated_delta_rule(q, k, v, g, beta):
    B, T, H, dk = q.shape
    dv = v.shape[-1]
    C = GDN_CHUNK
    n = T // C
    to_chunks = lambda t: t.reshape(B, n, C, H, -1).transpose(0, 3, 1, 2, 4)
    q, k, v = to_chunks(q), to_chunks(k), to_chunks(v)
    g = g.reshape(B, n, C, H).transpose(0, 3, 1, 2)
    beta = beta.reshape(B, n, C, H).transpose(0, 3, 1, 2)
    g_cum = jnp.cumsum(g, axis=-1)
    causal = jnp.tril(jnp.ones((C, C), dtype=bool))
    strict = jnp.tril(jnp.ones((C, C), dtype=bool), -1)
    diff = g_cum[..., :, None] - g_cum[..., None, :]
    decay = jnp.where(causal, jnp.exp(jnp.where(causal, diff, 0.0)), 0.0)
    k_beta = k * beta[..., None]
    v_beta = v * beta[..., None]
    lower = jnp.where(strict, jnp.einsum('bhncd,bhnsd->bhncs', k_beta, k) * decay, 0.0)
    eye = jnp.eye(C, dtype=q.dtype)
    t_mat = lax.linalg.triangular_solve(lower + eye, jnp.broadcast_to(eye, lower.shape),
                                        left_side=True, lower=True, unit_diagonal=True)
    u = jnp.matmul(t_mat, v_beta)
    w = jnp.matmul(t_mat, k_beta * jnp.exp(g_cum)[..., None])
    attn = jnp.where(causal, jnp.einsum('bhncd,bhnsd->bhncs', q, k) * decay, 0.0)

    def step(S, inp):
        q_i, k_i, u_i, w_i, a_i, gc_i = inp
        v_new = u_i - jnp.matmul(w_i, S)
        o_i = jnp.matmul(q_i * jnp.exp(gc_i)[..., None], S) + jnp.matmul(a_i, v_new)
        g_last = gc_i[..., -1]
        k_dec = k_i * jnp.exp(g_last[..., None] - gc_i)[..., None]
        S = S * jnp.exp(g_last)[..., None, None] + jnp.einsum('bhcd,bhce->bhde', k_dec, v_new)
        return S, o_i

    mv = lambda t: jnp.moveaxis(t, 2, 0)
    S0 = jnp.zeros((B, H, dk, dv), dtype=q.dtype)
    _, o = lax.scan(step, S0, (mv(q), mv(k), mv(u), mv(w), mv(attn), mv(g_cum)))
    return o.transpose(1, 0, 3, 2, 4).reshape(B, T, H, dv)


def gated_deltanet(qkv, z, b_logit, a_logit, conv_w, a_log, dt_bias, onorm_w):
    B, T, _ = qkv.shape
    qkv = jax.nn.silu(causal_depthwise_conv(qkv, conv_w)).astype(jnp.float32)
    q, k, v = jnp.split(qkv, 3, axis=-1)
    heads = lambda t: t.reshape(B, T, GDN_HEADS, GDN_HEAD_DIM)
    q = l2_normalize(heads(q)) * (GDN_HEAD_DIM ** -0.5)
    k = l2_normalize(heads(k))
    v = heads(v)
    beta = jax.nn.sigmoid(b_logit.astype(jnp.float32))
    g = -jnp.exp(a_log.astype(jnp.float32)) * jax.nn.softplus(
        a_logit.astype(jnp.float32) + dt_bias.astype(jnp.float32))
    o = chunk_gated_delta_rule(q, k, v, g, beta)
    o = rms_norm(o, onorm_w) * jax.nn.silu(heads(z).astype(jnp.float32))
    return o.reshape(B, T, GDN_WIDTH)


def rwkv7_scan(r, decay, k, v, a, b):
    B, T, H, N = r.shape

    def step(S, inp):
        r_t, w_t, k_t, v_t, a_t, b_t = inp
        sa = jnp.einsum('bhvk,bhk->bhv', S, a_t)
        S = S * w_t[:, :, None, :] + sa[..., None] * b_t[:, :, None, :] + v_t[..., None] * k_t[:, :, None, :]
        return S, jnp.einsum('bhvk,bhk->bhv', S, r_t)

    tm = lambda t: jnp.moveaxis(t, 1, 0)
    S0 = jnp.zeros((B, H, N, N), dtype=jnp.float32)
    _, y = lax.scan(step, S0, (tm(r), tm(decay), tm(k), tm(v), tm(a), tm(b)))
    return jnp.moveaxis(y, 0, 1)


def rwkv7_time_mix(pb, mu, w0, w2, a0, a2, g2, k_k, k_a, r_k, lnx_w, lnx_b):
    f32 = lambda t: t.astype(jnp.float32)
    B, T, _ = pb.shape
    pb = f32(pb)
    pb = pb + (token_shift(pb) - pb) * f32(mu)
    r, k, v, w_lo, a_lo, g_lo = split_cols(
        pb, (RWKV_WIDTH, RWKV_WIDTH, RWKV_WIDTH, DECAY_LORA, ICLR_LORA, GATE_LORA))
    w = -jax.nn.softplus(-(f32(w0) + jnp.tanh(w_lo) @ f32(w2))) - 0.5
    decay = jnp.exp(-jnp.exp(w))
    a = jax.nn.sigmoid(f32(a0) + a_lo @ f32(a2))
    g = jax.nn.sigmoid(g_lo) @ f32(g2)
    heads = lambda t: t.reshape(B, T, RWKV_HEADS, RWKV_HEAD_DIM)
    kk = l2_normalize(heads(k * f32(k_k)))
    k = k * (1.0 + (a - 1.0) * f32(k_a))
    r, k, v, a, decay = heads(r), heads(k), heads(v), heads(a), heads(decay)
    y = rwkv7_scan(r, decay, k, v, -kk, kk * a)
    mean = jnp.mean(y, axis=-1, keepdims=True)
    var = jnp.mean(jnp.square(y - mean), axis=-1, keepdims=True)
    y = ((y - mean) * lax.rsqrt(var + LNX_EPS)).reshape(B, T, RWKV_WIDTH) * f32(lnx_w) + f32(lnx_b)
    bonus = jnp.sum(r * k * f32(r_k), axis=-1, keepdims=True) * v
    return (y + bonus.reshape(B, T, RWKV_WIDTH)) * g


def conv_glu(h, w_in, conv_w, w_out):
    gate, up = jnp.split(h @ w_in, 2, axis=-1)
    gate = causal_depthwise_conv(gate, conv_w)
    return (jax.nn.silu(gate) * up) @ w_out


def setup_inputs(seed: int = 0) -> dict:
    key = jax.random.key(seed)
    ks = iter(jax.random.split(key, 40))
    nrm = lambda shape, scale: jax.random.normal(next(ks), shape, jnp.float32) * scale
    uni = lambda shape, lo, hi: jax.random.uniform(next(ks), shape, jnp.float32, minval=lo, maxval=hi)
    L = DEPTH
    x = nrm((BATCH, SEQ, D_MODEL), 1.0)
    c = nrm((BATCH, D_MODEL), 1.0)
    w_ada = nrm((L, D_MODEL, 6 * D_MODEL), D_MODEL ** -0.5)
    b_ada = nrm((L, 6 * D_MODEL), 0.02)
    norm1_w = 1.0 + nrm((L, D_MODEL), 0.02)
    w_in = nrm((L, D_MODEL, D_IN), D_MODEL ** -0.5)
    conv_gdn = nrm((L, GDN_CONV, 3 * GDN_WIDTH), GDN_CONV ** -0.5)
    a_log = jnp.log(uni((L, GDN_HEADS), 1.0, 16.0))
    dt = jnp.exp(uni((L, GDN_HEADS), math.log(1e-3), math.log(1e-1)))
    dt_bias = dt + jnp.log(-jnp.expm1(-dt))
    onorm_gdn = 1.0 + nrm((L, GDN_HEAD_DIM), 0.02)
    w_branch_gdn = nrm((L, GDN_WIDTH, D_MODEL), GDN_WIDTH ** -0.5)
    mu_rwkv = uni((L, RWKV_COLS), 0.0, 1.0)
    w0 = uni((L, RWKV_WIDTH), -6.5, -1.5)
    w2 = nrm((L, DECAY_LORA, RWKV_WIDTH), 0.5 * DECAY_LORA ** -0.5)
    a0 = nrm((L, RWKV_WIDTH), 0.1)
    a2 = nrm((L, ICLR_LORA, RWKV_WIDTH), 0.5 * ICLR_LORA ** -0.5)
    g2 = nrm((L, GATE_LORA, RWKV_WIDTH), GATE_LORA ** -0.5)
    k_k = 0.85 + nrm((L, RWKV_WIDTH), 0.02)
    k_a = 1.0 + nrm((L, RWKV_WIDTH), 0.02)
    r_k = nrm((L, RWKV_HEADS, RWKV_HEAD_DIM), 0.1)
    lnx_w = 1.0 + nrm((L, RWKV_WIDTH), 0.02)
    lnx_b = nrm((L, RWKV_WIDTH), 0.02)
    w_branch_rwkv = nrm((L, RWKV_WIDTH, D_MODEL), RWKV_WIDTH ** -0.5)
    w_out = nrm((L, D_MODEL, D_MODEL), D_MODEL ** -0.5)
    norm2_w = 1.0 + nrm((L, D_MODEL), 0.02)
    w_ffn_in = nrm((L, D_MODEL, 2 * D_FF), D_MODEL ** -0.5)
    conv_ffn = nrm((L, FFN_CONV, D_FF), FFN_CONV ** -0.5)
    w_ffn_out = nrm((L, D_FF, D_MODEL), D_FF ** -0.5)
    norm_f_w = 1.0 + nrm((D_MODEL,), 0.02)
    return {"x": x, "c": c, "w_ada": w_ada, "b_ada": b_ada, "norm1_w": norm1_w, "w_in": w_in,
            "conv_gdn": conv_gdn, "a_log": a_log, "dt_bias": dt_bias, "onorm_gdn": onorm_gdn,
            "w_branch_gdn": w_branch_gdn, "mu_rwkv": mu_rwkv, "w0": w0, "w2": w2, "a0": a0, "a2": a2,
            "g2": g2, "k_k": k_k, "k_a": k_a, "r_k": r_k, "lnx_w": lnx_w, "lnx_b": lnx_b,
            "w_branch_rwkv": w_branch_rwkv, "w_out": w_out, "norm2_w": norm2_w, "w_ffn_in": w_ffn_in,
            "conv_ffn": conv_ffn, "w_ffn_out": w_ffn_out, "norm_f_w": norm_f_w}


def reference(x, c, w_ada, b_ada, norm1_w, w_in, conv_gdn, a_log, dt_bias, onorm_gdn, w_branch_gdn,
              mu_rwkv, w0, w2, a0, a2, g2, k_k, k_a, r_k, lnx_w, lnx_b, w_branch_rwkv, w_out,
              norm2_w, w_ffn_in, conv_ffn, w_ffn_out, norm_f_w):
    cond = jax.nn.silu(c)
    for i in range(DEPTH):
        mod = cond @ w_ada[i] + b_ada[i]
        shift1, scale1, gate1, shift2, scale2, gate2 = jnp.split(mod[:, None, :], 6, axis=-1)
        h = rms_norm(x, norm1_w[i]) * (1.0 + scale1) + shift1
        p = h @ w_in[i]
        qkv_a, z_a, b_a, a_a, p_b, gl_a, gl_b = split_cols(p, IN_SIZES)
        y_a = gated_deltanet(qkv_a, z_a, b_a, a_a, conv_gdn[i], a_log[i], dt_bias[i], onorm_gdn[i])
        y_b = rwkv7_time_mix(p_b, mu_rwkv[i], w0[i], w2[i], a0[i], a2[i], g2[i], k_k[i], k_a[i],
                             r_k[i], lnx_w[i], lnx_b[i])
        y_a = y_a.astype(x.dtype) @ w_branch_gdn[i]
        y_b = y_b.astype(x.dtype) @ w_branch_rwkv[i]
        merged = jax.nn.sigmoid(gl_a) * y_a + jax.nn.sigmoid(gl_b) * y_b
        x = x + gate1 * (merged @ w_out[i])
        h = rms_norm(x, norm2_w[i]) * (1.0 + scale2) + shift2
        x = x + gate2 * conv_glu(h, w_ffn_in[i], conv_ffn[i], w_ffn_out[i])
    return rms_norm(x, norm_f_w)
```

```python
import heapq
import numpy as np
from contextlib import ExitStack
import concourse.bass as bass
import concourse.mybir as mybir
from concourse.bass_utils import run_bass_kernel_spmd

F32 = mybir.dt.float32
BF16 = mybir.dt.bfloat16
AF = mybir.ActivationFunctionType
ALU = mybir.AluOpType
AX = mybir.AxisListType

D = 1024
D_IN = 5936
D_FF = 2816
NEG = -30000.0
ENGS = ("pe", "act", "dve", "pool", "sp")
SEM_LAT = 180.0
DMA_ISSUE = 80.0
DMA_LAT = 2000.0
ACT_TABLE_NS = 1283.0
ACT_LOOKAHEAD = 1
DMA_BPNS = 250.0


class Op:
    __slots__ = ("eng", "fns", "deps", "bdeps", "cost", "idx", "kind", "nbytes", "pos", "slot", "val", "done", "out_ap", "in_ap", "kw", "tag", "crit", "start", "why", "tset")

    def __init__(self, eng, kind, idx):
        self.eng = eng
        self.kind = kind
        self.idx = idx
        self.fns = []
        self.deps = set()
        self.bdeps = set()
        self.cost = 100.0
        self.pos = 0
        self.tag = ""
        self.crit = None
        self.why = {}
        self.tset = None


class Prog:
    def __init__(self, nc, sems, dma_sems, reorder=True):
        self.nc = nc
        self.sem = sems
        self.dma_sems = dma_sems
        self.reorder = reorder
        self.dma_val = [0] * len(dma_sems)
        self.dma_free = [0.0] * len(dma_sems)
        self.dma_last = [None] * len(dma_sems)
        self.cnt = {e: 0 for e in ENGS}
        self.waited = {e: {} for e in ENGS}
        self.seq = []
        self.last_w = {}
        self.readers = {}
        self.aliases = {}
        self.consts = set()
        self.bank_last = {}
        self.tag = ""
        self._pend = None

    def _slots(self, eng):
        n = len(self.dma_sems)
        return range(n - 6, n) if eng == "pool" else range(0, n - 6)

    def alias(self, *keys):
        for k in keys:
            self.aliases.setdefault(k, set()).update(x for x in keys if x != k)

    def const(self, *keys):
        self.consts.update(keys)

    def _expand(self, writes):
        out = list(writes)
        for w in writes:
            for a in self.aliases.get(w, ()):
                if a not in out:
                    out.append(a)
        return out

    def _track(self, op, reads, writes):
        for k in list(reads) + list(writes):
            if isinstance(k, str) and k.startswith("pb"):
                bank = k[:3]
                prev = self.bank_last.get(bank)
                if prev is not None and prev is not op:
                    op.bdeps.add(prev)
                    op.why[prev] = ("bank", bank)
                self.bank_last[bank] = op
        writes = self._expand(writes)
        for b in reads:
            w = self.last_w.get(b)
            if w is not None:
                op.deps.add(w)
                op.why[w] = ("RAW", b)
        for b in writes:
            w = self.last_w.get(b)
            if w is not None:
                op.deps.add(w)
                op.why[w] = ("WAW", b)
            for r in self.readers.get(b, ()):
                op.deps.add(r)
                op.why[r] = ("WAR", b)
        op.deps.discard(op)
        for b in writes:
            self.last_w[b] = op
            self.readers[b] = []
        for b in reads:
            if b in self.consts:
                continue
            self.readers.setdefault(b, []).append(op)

    def emit(self, eng, fn, reads=(), writes=(), inc=True, cost=100.0, tset=None):
        if eng == "pe":
            if self._pend is None:
                self._pend = Op("pe", "c", len(self.seq))
                self._pend.cost = 0.0
                self._pend.tag = self.tag
            op = self._pend
            op.fns.append(fn)
            op.cost += cost
            self._track(op, reads, writes)
            if inc:
                self.seq.append(op)
                self._pend = None
            return op
        assert self._pend is None, "non-PE emit inside an open PE accumulation bundle"
        op = Op(eng, "c", len(self.seq))
        op.tag = self.tag
        op.tset = tset
        op.fns.append(fn)
        op.cost = cost
        self._track(op, reads, writes)
        self.seq.append(op)
        return op

    def dma(self, eng, out_ap, in_ap, reads=(), writes=(), nbytes=65536, **kw):
        assert self._pend is None
        op = Op(eng, "dma", len(self.seq))
        op.tag = self.tag
        op.out_ap, op.in_ap, op.kw = out_ap, in_ap, kw
        op.nbytes = nbytes
        op.cost = DMA_ISSUE
        self._track(op, reads, writes)
        self.seq.append(op)
        return op

    def _schedule(self):
        ops = self.seq
        n = len(ops)
        for i, op in enumerate(ops):
            op.idx = i
        inphase = set(ops)
        succs = [[] for _ in range(n)]
        ndeps = [0] * n
        for op in ops:
            op.deps = {d for d in op.deps if d in inphase}
            op.bdeps = {d for d in op.bdeps if d in inphase and d not in op.deps}
            alld = op.deps | op.bdeps
            ndeps[op.idx] = len(alld)
            for d in alld:
                succs[d.idx].append(op)
        order = {e: [] for e in ENGS}
        if not self.reorder:
            t = 0.0
            for op in ops:
                if op.kind == "dma":
                    s = min(self._slots(op.eng), key=lambda i: self.dma_free[i])
                    op.slot = s
                    prev = self.dma_last[s]
                    if prev is not None:
                        op.deps.add(prev)
                    self.dma_last[s] = op
                    self.dma_free[s] = t = t + 1.0
                order[op.eng].append(op)
            return order
        bl = [0.0] * n
        for op in reversed(ops):
            m = 0.0
            for s_ in succs[op.idx]:
                v = bl[s_.idx] + SEM_LAT
                if v > m:
                    m = v
            bl[op.idx] = m + (op.cost if op.kind == "c" else DMA_ISSUE + DMA_LAT + op.nbytes / DMA_BPNS)
        import os
        use_bl = os.environ.get("SCHED_PRIO", "bl") == "bl"
        prio = [(-bl[i] if use_bl else i) for i in range(n)]
        act_set = None
        n_tload = 0
        ready_t = [0.0] * n
        ready_by = [None] * n
        eng_last = {e: None for e in ENGS}
        eng_free = {e: 0.0 for e in ENGS}
        heaps = {e: [] for e in ENGS}
        avail = {e: [] for e in ENGS}
        for op in ops:
            if ndeps[op.idx] == 0:
                heapq.heappush(heaps[op.eng], (0.0, op.idx))
        done_n = 0
        while done_n < n:
            best = None
            for e in ENGS:
                h, a = heaps[e], avail[e]
                ef = eng_free[e]
                while h and h[0][0] <= ef:
                    j_ = heapq.heappop(h)[1]
                    heapq.heappush(a, (prio[j_], j_))
                if a:
                    c = (ef, a[0][0], e, True, a[0][1])
                elif h:
                    c = (h[0][0], prio[h[0][1]], e, False, h[0][1])
                else:
                    continue
                if best is None or c[:2] < best[:2]:
                    best = c
            start, _p, e, from_avail, idx = best
            if from_avail:
                if e == "act":
                    cand = [heapq.heappop(avail[e]) for _ in range(min(ACT_LOOKAHEAD, len(avail[e])))]
                    pick = 0
                    for ci, (_pp, jj) in enumerate(cand):
                        ts_ = ops[jj].tset
                        if ts_ is None or ts_ == act_set or (ts_ == "T" and act_set in ("E", "S", "U")):
                            pick = ci
                            break
                    idx = cand[pick][1]
                    for ci, it_ in enumerate(cand):
                        if ci != pick:
                            heapq.heappush(avail[e], it_)
                else:
                    heapq.heappop(avail[e])
            else:
                heapq.heappop(heaps[e])
            if e == "act":
                ts_ = ops[idx].tset
                if ts_ is not None and not (ts_ == act_set or (ts_ == "T" and act_set in ("E", "S", "U"))):
                    act_set = ts_
                    start += ACT_TABLE_NS
                    n_tload += 1
            op = ops[idx]
            if op.kind == "dma":
                s = min(self._slots(op.eng), key=lambda i: self.dma_free[i])
                start = max(start, self.dma_free[s])
                op.slot = s
                prev = self.dma_last[s]
                if prev is not None:
                    op.deps.add(prev)
                self.dma_last[s] = op
                done = start + DMA_ISSUE + DMA_LAT + op.nbytes / DMA_BPNS
                self.dma_free[s] = done
                eng_free[e] = start + DMA_ISSUE
            else:
                done = start + op.cost
                eng_free[e] = done
            op.done = done
            op.start = start
            op.crit = ready_by[idx] if (ready_t[idx] >= start - 1e-6 and ready_by[idx] is not None) else eng_last[e]
            eng_last[e] = op
            order[e].append(op)
            for s_ in succs[idx]:
                j = s_.idx
                ndeps[j] -= 1
                rt = done + SEM_LAT
                if rt > ready_t[j]:
                    ready_t[j] = rt
                    ready_by[j] = op
                if ndeps[j] == 0:
                    heapq.heappush(heaps[s_.eng], (ready_t[j], j))
            done_n += 1
        self.sim_end = max(eng_free.values())
        self.n_tload = n_tload
        import os
        if os.environ.get("SCHED_DEBUG") and n > 10000:
            last = max(ops, key=lambda o: o.done)
            agg = {}
            cur = last
            while cur is not None:
                prev = cur.crit
                t0 = prev.done if prev is not None else 0.0
                key = (cur.tag.split(":")[-1], cur.eng, "wait" if (prev is not None and prev.eng != cur.eng) else "fifo")
                agg[key] = agg.get(key, 0.0) + (cur.done - t0)
                cur = prev
            tot = sum(agg.values())
            print("CRITICAL PATH total %.1f us" % (tot / 1e3))
            cur = last
            hops = []
            while cur is not None:
                prev = cur.crit
                if prev is not None and prev.tag != cur.tag:
                    hops.append((prev.done / 1e3, prev.tag, prev.eng, cur.tag, cur.eng, cur.why.get(prev, ("fifo", ""))))
                cur = prev
            hops.reverse()
            mid = [x for x in hops if 3000 < x[0] < 3500]
            for x in mid:
                print("   HOP t=%.1f  %s/%s -> %s/%s  %s" % x)
            for k, v in sorted(agg.items(), key=lambda kv: -kv[1])[:40]:
                print("   %-28s %8.1f us" % (str(k), v / 1e3))
        return order

    def _verify(self, order):
        ptr = {e: 0 for e in ENGS}
        done = set()
        total = sum(len(v) for v in order.values())
        n = 0
        while n < total:
            prog = False
            for e in ENGS:
                q = order[e]
                while ptr[e] < len(q):
                    op = q[ptr[e]]
                    if all(d in done for d in op.deps) and all(d in done for d in op.bdeps):
                        done.add(op)
                        ptr[e] += 1
                        n += 1
                        prog = True
                    else:
                        break
            if not prog:
                raise RuntimeError("schedule deadlock: " + str({e: ptr[e] for e in ENGS}))

    def replay(self, barrier=True):
        assert self._pend is None
        order = self._schedule()
        self._verify(order)
        import os
        if os.environ.get("SCHED_DEBUG"):
            busy = {e: sum((o.cost if o.kind == "c" else DMA_ISSUE) for o in order[e]) for e in ENGS}
            sb = {}
            for e in ENGS:
                for o in order[e]:
                    k_ = (o.tag.split(":")[-1], e)
                    sb[k_] = sb.get(k_, 0.0) + (o.cost if o.kind == "c" else DMA_ISSUE)
            if len(self.seq) > 10000:
                print("STAGE BUSY (us/tile):", {k_: round(v / 32e3, 1) for k_, v in sorted(sb.items()) if v / 32e3 > 0.5})
            print("SCHED n=%d sim_end=%.1f us tloads=%d busy(us)=%s" % (len(self.seq), getattr(self, "sim_end", 0) / 1e3, getattr(self, "n_tload", -1),
                  {e: round(v / 1e3, 1) for e, v in busy.items()}), flush=True)
        for e in ENGS:
            c = self.cnt[e]
            for op in order[e]:
                if op.kind == "c":
                    c += 1
                    op.pos = c
            self.cnt[e] = c
        for op in sorted((o for o in self.seq if o.kind == "dma"), key=lambda o: (o.slot, o.done if self.reorder else o.idx)):
            self.dma_val[op.slot] += 16
            op.val = self.dma_val[op.slot]
        sem, dsem, waited = self.sem, self.dma_sems, self.waited
        final_cnt = dict(self.cnt)
        final_dma = list(self.dma_val)

        def body(e_name):
            def run(e):
                wd = waited[e_name]
                for op in order[e_name]:
                    need = {}
                    for d in list(op.deps) + [b for b in op.bdeps if b.eng != e_name]:
                        if d.kind == "dma":
                            k, v = ("dma", d.slot), d.val
                        else:
                            if d.eng == "pe" and e_name == "pe":
                                continue
                            k, v = d.eng, d.pos
                        if need.get(k, 0) < v:
                            need[k] = v
                    for k, v in need.items():
                        if wd.get(k, 0) >= v:
                            continue
                        wd[k] = v
                        e.wait_ge(dsem[k[1]] if isinstance(k, tuple) else sem[k], v)
                    if op.kind == "dma":
                        e.dma_start(out=op.out_ap, in_=op.in_ap, **op.kw).then_inc(dsem[op.slot], 16)
                    else:
                        ins = None
                        for f in op.fns:
                            ins = f(e)
                        ins.then_inc(sem[e_name], 1)
                if barrier:
                    for k, v in final_cnt.items():
                        if k != e_name and v > wd.get(k, 0):
                            wd[k] = v
                            e.wait_ge(sem[k], v)
                    for i, v in enumerate(final_dma):
                        if v > wd.get(("dma", i), 0):
                            wd[("dma", i)] = v
                            e.wait_ge(dsem[i], v)
            return run

        with self.nc.Block() as block:
            block.tensor(body("pe"))
            block.scalar(body("act"))
            block.vector(body("dve"))
            block.gpsimd(body("pool"))
            block.sync(body("sp"))
        self.seq = []
        self.last_w = {}
        self.readers = {}
        self.bank_last = {}
        self.dma_last = [None] * len(self.dma_sems)
        self.dma_free = [0.0] * len(self.dma_sems)

    def barrier(self):
        pass


def _consts():
    c = {}
    s = np.arange(128)[:, None]
    t = np.arange(128)[None, :]
    c["ident"] = np.eye(128, dtype=np.float32)
    up_s = (s < t)
    up_i = (s <= t)
    lo_s = (t < s)
    nm = lambda m: np.where(m, 0.0, NEG).astype(np.float32)
    c["negmask3"] = np.concatenate([nm(up_i), nm(up_s), nm(lo_s)], axis=1)
    f = lambda m: m.astype(np.float32)
    c["mask4"] = np.concatenate([f(up_i), f(up_s), f(up_i), f(up_s)], axis=1)
    c["ltri"] = f(up_i)
    c["lrev"] = f(lo_s)
    c["bd"] = f((s // 64) == (t // 64))
    hsel = np.zeros((1, 2, 128), np.float32)
    hsel[0, 0, 0:64] = 1.0
    hsel[0, 1, 64:128] = 1.0
    c["halfsel"] = hsel.reshape(1, 256)
    hs = np.zeros((128, 4, 8), np.float32)
    for f_ in range(4):
        hs[0:64, f_, 2 * f_] = 1.0
        hs[64:128, f_, 2 * f_ + 1] = 1.0
    c["headsel"] = hs.reshape(128, 32)
    c["ones"] = np.ones((128, 128), np.float32)
    sel8 = np.zeros((8, 8, 128), np.float32)
    for h_ in range(8):
        sel8[h_, h_, :] = 1.0
    c["sel8"] = sel8.reshape(8, 1024)
    selp = np.zeros((8, 4, 128), np.float32)
    for f_ in range(4):
        selp[2 * f_, f_, 0:64] = 1.0
        selp[2 * f_ + 1, f_, 64:128] = 1.0
    c["selp8"] = selp.reshape(8, 512)
    blk = lambda n: f((s // n) == (t // n))
    c["imask"] = np.concatenate([blk(16), blk(32) - blk(16), blk(64) - blk(32), blk(128) - blk(64)], axis=1)
    return c


def _fm(v, nch):
    v = np.asarray(v, np.float32).reshape(-1)
    pad = nch * 128 - v.shape[0]
    if pad:
        v = np.concatenate([v, np.zeros(pad, np.float32)])
    return np.ascontiguousarray(v.reshape(nch, 128).T)


def _bc(v, n=128):
    v = np.asarray(v, np.float32).reshape(1, -1)
    return np.ascontiguousarray(np.broadcast_to(v, (n, v.shape[1])))


C05 = 0.6065306597126334


def build_nc(T, mode="full", taps=None, reorder=True):
    NT = T // 128
    nc = bass.Bass("TRN2", target_bir_lowering=False)
    di = lambda name, shape: nc.dram_tensor(name, list(shape), F32, kind="ExternalInput").ap()
    x_d = di("x", [T, D])
    cT_d = di("cT", [128, 8])
    wada_d = di("w_ada", [D, 6 * D])
    badaT_d = di("badaT", [128, 48])
    bgate_d = di("bgate", [128, 2 * D])
    n1T_d = di("n1T", [128, 8])
    n2T_d = di("n2T", [128, 8])
    nf_d = di("nf_bc", [128, D])
    win_d = di("w_in", [D, D_IN])
    cgT_d = di("cgT", [128, 48])
    gpar_d = di("gpar_bc", [128, 16])
    onorm_d = di("onorm_bc", [128, 64])
    wbg_d = di("wbg", [512, D])
    wbr_d = di("wbr", [512, D])
    wout_d = di("wout", [D, D])
    muT_d = di("muT", [128, 15])
    rwp_d = di("rwp", [128, 20])
    lnx_d = di("lnx_bc", [128, 1024])
    lora_d = di("lora", [128, 512])
    g2_d = di("g2", [160, 512])
    wfi_d = di("w_ffn_in", [D, 2 * D_FF])
    cfT_d = di("cfT", [128, 66])
    wfo_d = di("w_ffn_out", [D_FF, D])
    ident_d = di("ident", [128, 128])
    negmask3_d = di("negmask3", [128, 384])
    mask4_d = di("mask4", [128, 512])
    ltri_d = di("ltri", [128, 128])
    lrev_d = di("lrev", [128, 128])
    bd_d = di("bd", [128, 128])
    halfsel_d = di("halfsel", [1, 256])
    headsel_d = di("headsel", [128, 32])
    ones_d = di("ones", [128, 128])
    imask_d = di("imask", [128, 512])
    sel8_d = di("sel8", [8, 1024])
    selp8_d = di("selp8", [8, 512])
    y_d = nc.dram_tensor("y", [T, D], F32, kind="ExternalOutput").ap()
    x1_d = nc.dram_tensor("x1_scratch", [T, D], F32).ap()
    gscr_d = nc.dram_tensor("gate_scratch", [128, 2 * D], F32).ap()
    dbg_d = nc.dram_tensor("dbg", [128, 16384], F32, kind="ExternalOutput").ap() if taps is not None else None
    tap_off = [0]

    with ExitStack() as es:
        sems = {e: es.enter_context(nc.semaphore("s_" + e)) for e in ENGS}
        dsems = [es.enter_context(nc.semaphore(f"dq{i}")) for i in range(24)]
        P = Prog(nc, sems, dsems, reorder=reorder)

        def sbt(scope, name, shape, dt):
            return scope.enter_context(nc.sbuf_tensor("s_" + name, list(shape), dt))

        PBall = es.enter_context(nc.psum_tensor("pball", [128, 8, 512], F32))
        PB = [PBall[:, i, :] for i in range(8)]
        PBall_b = PBall[:, :, :].rearrange("p b c -> p (b c)").bitcast(BF16).rearrange("p (b c) -> p b c", b=8)

        def pbb(i):
            return PB[i][:, :].bitcast(BF16)

        modT = sbt(es, "modT", [128, 32], F32)
        g1T = sbt(es, "g1T", [128, 8], F32)
        g2T = sbt(es, "g2T", [128, 8], F32)
        ident_f = sbt(es, "ident_f", [128, 128], F32)
        ident_b = sbt(es, "ident_b", [128, 128], BF16)
        small = sbt(es, "small", [128, 16], F32)
        cst = sbt(es, "cst", [128, 4], F32)

        P.dma("sp", ident_f[:], ident_d, writes=["ident_f"])
        P.dma("pool", ident_b[:], ident_d, writes=["ident_b"])
        for i, v in enumerate((1e-6, 1.0, 64e-6, 64e-5)):
            P.emit("pool", lambda e, i=i, v=v: e.memset(cst[:, i:i + 1], v), writes=["cst"])

        def bk_keys(bk):
            return ["pb7", "pb7u", "pb7s"] if bk == 7 else [f"pb{bk}"]

        def fsz(ap):
            n = 1
            for d_ in ap.shape[1:]:
                n *= d_
            return n

        def mm(out, lhsT, rhs, start=True, stop=True, reads=(), writes=(), inc=None):
            if inc is None:
                inc = stop
            return P.emit("pe", lambda e: e.matmul(out, lhsT=lhsT, rhs=rhs, start=start, stop=stop),
                          reads=reads, writes=writes, inc=inc, cost=70.0 + 0.45 * fsz(rhs) * (4.5 if rhs.dtype == F32 else 1))

        def tr(out, in_, idn, reads=(), writes=(), inc=True):
            return P.emit("pe", lambda e: e.transpose(out, in_, idn), reads=reads, writes=writes, inc=inc, cost=120.0)

        def act(out, in_, func, reads=(), writes=(), scale=None, bias=None, accum=None):
            kw = {}
            if scale is not None:
                kw["scale"] = scale
            if bias is not None:
                kw["bias"] = bias
            if accum is not None:
                kw["accum_out"] = accum
            tset = {AF.Exp: "E", AF.Ln: "L", AF.Sqrt: "Q", AF.Silu: "U", AF.Sigmoid: "S", AF.Tanh: "T"}.get(func)
            return P.emit("act", lambda e: e.activation(out=out, in_=in_, func=func, **kw), reads=reads, writes=writes,
                          cost=220.0 + 0.6 * fsz(out), tset=tset)

        def tt(eng, out, in0, in1, op, reads=(), writes=()):
            c = (190.0 + 1.15 * fsz(out)) if eng == "dve" else (400.0 + 2.0 * fsz(out))
            return P.emit(eng, lambda e: e.tensor_tensor(out=out, in0=in0, in1=in1, op=op), reads=reads, writes=writes, cost=c)

        def ts(eng, out, in0, s1, s2, op0, op1=None, reads=(), writes=()):
            c = (190.0 + 1.15 * fsz(out)) if eng == "dve" else (400.0 + 2.0 * fsz(out))
            if op1 is None:
                return P.emit(eng, lambda e: e.tensor_scalar(out=out, in0=in0, scalar1=s1, scalar2=None, op0=op0),
                              reads=reads, writes=writes, cost=c)
            return P.emit(eng, lambda e: e.tensor_scalar(out=out, in0=in0, scalar1=s1, scalar2=s2, op0=op0, op1=op1),
                          reads=reads, writes=writes, cost=c)

        def stt(eng, out, in0, scalar, in1, op0, op1, reads=(), writes=()):
            c = (190.0 + 1.15 * fsz(out)) if eng == "dve" else (400.0 + 2.0 * fsz(out))
            return P.emit(eng, lambda e: e.scalar_tensor_tensor(out=out, in0=in0, scalar=scalar, in1=in1, op0=op0, op1=op1),
                          reads=reads, writes=writes, cost=c)

        def cp(eng, out, in_, reads=(), writes=()):
            if eng == "act":
                return act(out, in_, AF.Copy, reads=reads, writes=writes)
            c = (190.0 + 1.15 * fsz(out)) if eng == "dve" else (400.0 + 2.0 * fsz(out))
            return P.emit(eng, lambda e: e.tensor_copy(out=out, in_=in_), reads=reads, writes=writes, cost=c)

        def recip(out, in_, reads=(), writes=()):
            return P.emit("dve", lambda e: e.reciprocal(out=out, in_=in_), reads=reads, writes=writes, cost=150.0 + 2.5 * fsz(out))

        def red(out, in_, reads=(), writes=()):
            return P.emit("dve", lambda e: e.tensor_reduce(out=out, in_=in_, axis=AX.X, op=ALU.add), reads=reads, writes=writes,
                          cost=150.0 + 1.0 * fsz(in_))

        def tap(name, ap, reads, rows=128):
            if taps is None:
                return
            n = 1
            for d_ in ap.shape[1:]:
                n *= d_
            o = tap_off[0]
            taps[name] = (o, n, rows, tuple(ap.shape))
            dst = dbg_d[0:rows, o:o + n]
            if len(ap.shape) == 3:
                dst = dst.rearrange("p (a b) -> p a b", a=ap.shape[1])
            elif len(ap.shape) == 4:
                dst = dst.rearrange("p (a b c) -> p a b c", a=ap.shape[1], b=ap.shape[2])
            eng = "pool" if ap.dtype != F32 else "sp"
            P.dma(eng, dst, ap, reads=reads)
            tap_off[0] = o + n

        def rms_rstd(xt, xkeys, junk, jkeys, col, n=D):
            act(junk, xt, AF.Square, reads=xkeys, writes=list(jkeys) + [("small", col)], accum=small[:, col:col + 1])
            act(small[:, col + 1:col + 2], small[:, col:col + 1], AF.Sqrt, reads=[("small", col), "cst"],
                writes=[("small", col + 1)], scale=1.0 / n, bias=cst[:, 0:1])
            recip(small[:, col:col + 1], small[:, col + 1:col + 2], reads=[("small", col + 1)], writes=[("small", col)])

        def norm_to_featmajor(xt, xkeys, xn, xnkeys, gT, shT, gkey, dst, dkey, toff, col, banks=(6, 7)):
            rms_rstd(xt, xkeys, xn, xnkeys, col)
            ts("dve", xn, xt, small[:, col:col + 1], None, ALU.mult, reads=list(xkeys) + [("small", col)], writes=xnkeys)
            for half in range(2):
                bk = banks[half]
                for q in range(4):
                    k = half * 4 + q
                    tr(PB[bk][:, q * 128:(q + 1) * 128], xn[:, k * 128:(k + 1) * 128], ident_f[:],
                       reads=list(xnkeys) + ["ident_f"], writes=bk_keys(bk), inc=(q == 3))
                for q in range(4):
                    k = half * 4 + q
                    act(dst[:, k, toff:toff + 128], PB[bk][:, q * 128:(q + 1) * 128], AF.Identity,
                        reads=bk_keys(bk) + [gkey, "modT"], writes=[dkey], scale=gT[:, k:k + 1], bias=shT[:, k:k + 1])

        with ExitStack() as s0:
            stage = [sbt(s0, f"wa{i}", [128, 6 * D], F32) for i in range(2)]
            gate_bc = sbt(s0, "gate_bc", [128, 2 * D], F32)
            cond = sbt(s0, "cond", [128, 8], F32)
            condrep = sbt(s0, "condrep", [128, 8, 128], F32)
            acc = sbt(s0, "acc", [128, 32], F32)
            badaT = sbt(s0, "badaT_sb", [128, 48], F32)
            n12 = sbt(s0, "n12", [128, 16], F32)
            P.dma("sp", cond[:], cT_d, writes=["cond"])
            P.dma("sp", badaT[:], badaT_d, writes=["badaT"])
            P.dma("sp", n12[:, 0:8], n1T_d, writes=["n12"])
            P.dma("sp", n12[:, 8:16], n2T_d, writes=["n12"])
            P.dma("sp", gate_bc[:], bgate_d, writes=["gate_bc"])
            act(cond[:], cond[:], AF.Silu, reads=["cond"], writes=["cond"])
            cp("dve", condrep[:], cond[:].unsqueeze(2).to_broadcast([128, 8, 128]), reads=["cond"], writes=["condrep"])
            pp_cols = [0, 1, 3, 4]
            for k in range(8):
                st = stage[k % 2]
                sk = f"stage{k % 2}"
                P.dma("sp", st[:], wada_d[k * 128:(k + 1) * 128, :], writes=[sk])
                for j in range(32):
                    c0 = pp_cols[j // 8] * D + (j % 8) * 128
                    mm(PB[2][:, j:j + 1], st[:, c0:c0 + 128], cond[:, k:k + 1], reads=[sk, "cond"], writes=["pb2"])
                if k == 0:
                    cp("dve", acc[:], PB[2][:, 0:32], reads=["pb2"], writes=["acc"])
                else:
                    tt("dve", acc[:], acc[:], PB[2][:, 0:32], ALU.add, reads=["pb2", "acc"], writes=["acc"])
                for gi, g in enumerate((2, 5)):
                    for hf in range(2):
                        bank = 3 + gi * 2 + hf
                        mm(PB[bank][:, :], condrep[:, k, :], st[:, g * D + hf * 512: g * D + hf * 512 + 512],
                           start=(k == 0), stop=(k == 7), reads=[sk, "condrep"], writes=[f"pb{bank}"], inc=True)
            for gi in range(2):
                for hf in range(2):
                    bank = 3 + gi * 2 + hf
                    sl = gate_bc[:, gi * D + hf * 512: gi * D + hf * 512 + 512]
                    tt("dve", sl, sl, PB[bank][:, :], ALU.add, reads=[f"pb{bank}", "gate_bc"], writes=["gate_bc"])
            P.dma("sp", gscr_d, gate_bc[:], reads=["gate_bc"], writes=["gscr"])
            for gi, g in enumerate(pp_cols):
                tt("dve", modT[:, gi * 8:(gi + 1) * 8], acc[:, gi * 8:(gi + 1) * 8], badaT[:, g * 8:(g + 1) * 8], ALU.add,
                   reads=["acc", "badaT"], writes=["modT"])
            stt("dve", g1T[:], modT[:, 8:16], 1.0, n12[:, 0:8], ALU.add, ALU.mult, reads=["modT", "n12"], writes=["g1T"])
            stt("dve", g2T[:], modT[:, 24:32], 1.0, n12[:, 8:16], ALU.add, ALU.mult, reads=["modT", "n12"], writes=["g2T"])
            P.barrier()
            P.replay()
        sh1T = modT[:, 0:8]
        sh2T = modT[:, 16:24]

        BLKW = 2304
        blocks = []
        bidx = {}

        def _blk(name, ring_, src, c0, ncols, nk):
            bidx[name] = len(blocks)
            blocks.append((ring_, src, c0, ncols, nk))

        for i_, nm in enumerate(("gq", "gk", "gv")):
            _blk(nm + "0", 0, "win", i_ * 512, 256, 8)
            _blk(nm + "1", 0, "win", i_ * 512 + 256, 256, 8)
        _blk("gz0", 0, "win", 1536, 256, 8)
        _blk("gz1", 0, "win", 1792, 272, 8)
        for i_, nm in enumerate(("rr", "rk", "rv")):
            _blk(nm + "0", 1, "win", 2064 + i_ * 512, 256, 8)
            _blk(nm + "1", 1, "win", 2064 + i_ * 512 + 256, 256, 8)
        _blk("rl", 1, "win", 3600, 288, 8)
        for g_ in range(2):
            for hh_ in range(2):
                _blk(f"gla{g_}{hh_}", 2, "win", 3888 + g_ * 512 + hh_ * 256, 256, 8)
            for hh_ in range(2):
                _blk(f"glb{g_}{hh_}", 2, "win", 4912 + g_ * 512 + hh_ * 256, 256, 8)
            for hh_ in range(2):
                _blk(f"wbg{g_}{hh_}", 2, "wbg", g_ * 512 + hh_ * 256, 256, 4)
            for hh_ in range(2):
                _blk(f"wbr{g_}{hh_}", 2, "wbr", g_ * 512 + hh_ * 256, 256, 4)
        for q_ in range(4):
            _blk(f"wo{q_}", 2, "wout", q_ * 256, 256, 8)
        NBLK = len(blocks)
        wsc_d = nc.dram_tensor("w_scratch", [NBLK, 128, BLKW], BF16).ap()
        if mode != "skipA":
            with ExitStack() as sW:
                win_t = sbt(sW, "win_t", [128, 8, D_IN], BF16)
                wbg_t = sbt(sW, "wbg_t", [128, 4, D], BF16)
                wbr_t = sbt(sW, "wbr_t", [128, 4, D], BF16)
                wout_t = sbt(sW, "wout_t", [128, 8, D], BF16)
                g1row = sbt(sW, "g1row", [128, D], F32)
                wst = [sbt(sW, f"wst{i}", [128, D], F32) for i in range(2)]
                for k in range(8):
                    P.dma("pool", win_t[:, k, :], win_d[k * 128:(k + 1) * 128, :], writes=[("win_t", k)], nbytes=128 * D_IN * 4)
                for k in range(4):
                    P.dma("pool", wbg_t[:, k, :], wbg_d[k * 128:(k + 1) * 128, :], writes=["wbg_t"], nbytes=524288)
                    P.dma("pool", wbr_t[:, k, :], wbr_d[k * 128:(k + 1) * 128, :], writes=["wbr_t"], nbytes=524288)
                P.dma("sp", g1row[:], gscr_d[:, 0:D], writes=["g1row"], nbytes=524288)
                for k in range(8):
                    P.dma("sp", wst[k % 2][:], wout_d[k * 128:(k + 1) * 128, :], writes=[f"wst{k % 2}"], nbytes=524288)
                    tt("dve", wout_t[:, k, :], wst[k % 2][:], g1row[:], ALU.mult, reads=[f"wst{k % 2}", "g1row"], writes=["wout_t"])
                srcs = {"win": (win_t, [("win_t", k) for k in range(8)]), "wbg": (wbg_t, ["wbg_t"]),
                        "wbr": (wbr_t, ["wbr_t"]), "wout": (wout_t, ["wout_t"])}
                for b, (ring_, src, c0, ncols, nk) in enumerate(blocks):
                    t_, keys = srcs[src]
                    P.dma("sp", wsc_d[b, :, 0:nk * ncols].rearrange("p (k c) -> p k c", k=nk), t_[:, :, c0:c0 + ncols],
                          reads=keys, writes=[("wsc", b)], nbytes=128 * nk * ncols * 2)
                P.replay()

            with ExitStack() as sA:
                A = lambda name, shape, dt: sbt(sA, name, shape, dt)
                NRS = (3, 3, 3)
                rings = [[A(f"ring{r_}_{i}", [128, BLKW], BF16) for i in range(NRS[r_])] for r_ in range(3)]
                lora = A("lora_sb", [128, 512], BF16)
                g2s = A("g2_sb", [128, 2, 512], BF16)
                negmask3 = A("negmask3", [128, 384], BF16)
                mask4 = A("mask4", [128, 512], BF16)
                imask = A("imask", [128, 512], BF16)
                ltri = A("ltri", [128, 128], F32)
                lrev = A("lrev", [128, 128], F32)
                masksl = A("masksl", [128, 128], BF16)
                bd = A("bd", [128, 128], BF16)
                ones_f = A("ones_f", [128, 128], F32)
                sel8 = A("sel8", [8, 1024], F32)
                selp8 = A("selp8", [8, 512], F32)
                R8 = A("R8", [8, 384], F32)
                headsel = A("headsel", [128, 32], F32)
                cgT = A("cgT_sb", [128, 48], F32)
                gpar = A("gpar", [128, 16], F32)
                onorm = A("onorm_sb", [128, 64], F32)
                muT = A("muT_sb", [128, 15], F32)
                omuT = A("omuT", [128, 15], F32)
                rwp = A("rwp_sb", [128, 20], F32)
                lnx = A("lnx_sb", [128, 1024], F32)
                P.const("negmask3", "mask4", "imask", "ltri", "lrev", "masksl", "bd", "ones_f", "sel8", "selp8", "headsel", "cgT",
                        "onorm", "rwp", "lnx", "lora", "g2s", "ident_f", "ident_b", "cst", "g1T", "modT")
                xa = [A(f"xa{i}", [128, D], F32) for i in range(2)]
                hTs = [A(f"hT{i}", [128, 8, 128], BF16) for i in range(2)]
                xnA = A("xnA", [128, D], F32)
                stgG = A("stgG", [128, 4, 131], F32)
                cvcar = A("cvcar", [128, 12, 3], F32)
                FG = A("FG", [128, 8, 512], F32)
                sqG = A("sqG", [128, 4, 128], BF16)
                qkn_t = A("qkn", [128, 2048], BF16)
                qtl = A("qtl", [128, 512], BF16)
                kdecG = A("kdecG", [128, 512], BF16)
                gcols = A("gcols", [128, 4, 8], F32)
                tokx = A("tokx", [128, 4, 8], F32)
                gsm = A("gsm", [128, 64], F32)
                HBg = A("HBg", [128, 8, 384], BF16)
                xgG = A("xgG", [128, 512], BF16)
                ugG = A("ugG", [128, 512], BF16)
                ytbG = A("ytbG", [128, 512], BF16)
                yaT = A("yaT", [128, 4, 128], BF16)
                Sb = A("Sb", [128, 4, 128], BF16)
                Sf = A("Sf", [128, 4, 128], F32)
                e3 = A("e3", [128, 2, 384], F32)
                stgR = A("stgR", [128, 4, 129], F32)
                pbcar = A("pbcar", [128, 16], F32)
                FR = A("FR", [128, 10, 512], F32)
                sqR = A("sqR", [128, 4, 128], BF16)
                h2 = A("h2", [128, 512], BF16)
                rat_t = A("rat", [128, 1024], BF16)
                btl_t = A("btl", [128, 512], BF16)
                ktl_t = A("ktl", [128, 512], BF16)
                bdtok = A("bdtok", [128, 512], BF16)
                kdtok = A("kdtok", [128, 512], BF16)
                vtk = A("vtk", [128, 512], BF16)
                gsr = A("gsr", [128, 64], F32)
                lorb = A("lorb", [128, 128], BF16)
                sglb = A("sglb", [128, 2, 128], BF16)
                HBr = A("HBr", [128, 8, 640], BF16)
                xgR = A("xgR", [128, 512], BF16)
                ugR = A("ugR", [128, 512], BF16)
                ytbR = A("ytbR", [128, 512], BF16)
                ybT = A("ybT", [128, 4, 128], BF16)
                Mb = A("Mb", [128, 4, 128], BF16)
                Mf = A("Mf", [128, 4, 128], F32)
                WK = A("WK", [128, 8, 1664], BF16)
                mg_t = A("mg", [128, 1024], BF16)
                mgt = A("mgt", [128, 1024], BF16)
                msa = A("msa", [128, 512], BF16)
                msb = A("msb", [128, 512], BF16)
                mm1 = A("mm1", [128, 512], F32)
                mm2 = A("mm2", [128, 512], F32)

                def FGi(i):
                    return FG[:, i, :]

                def FG3(i):
                    return FG[:, i, :].rearrange("p (j t) -> p j t", j=4)

                def FRi(i):
                    return FR[:, i, :]

                def FR3(i):
                    return FR[:, i, :].rearrange("p (j t) -> p j t", j=4)

                GK = lambda i: ("FG", i)
                RK = lambda i: ("FR", i)
                for fi_ in (0, 1, 2, 9):
                    for j_ in range(4):
                        P.alias(("FR", fi_), ("FRc", fi_, j_))
                qkn = qkn_t[:, :].rearrange("p (f a t) -> p f a t", f=4, a=4)
                rat = rat_t[:, :].rearrange("p (f a t) -> p f a t", f=4, a=2)
                btl = btl_t[:, :].rearrange("p (f t) -> p f t", f=4)
                ktl = ktl_t[:, :].rearrange("p (f t) -> p f t", f=4)
                mg = mg_t[:, :].rearrange("p (k t) -> p k t", k=8)

                P.dma("pool", lora[:], lora_d, writes=["lora"])
                P.dma("pool", g2s[:, 0, :], g2_d[0:128, :], writes=["g2s"])
                P.dma("pool", g2s[0:32, 1, :], g2_d[128:160, :], writes=["g2s"])
                P.dma("pool", negmask3[:], negmask3_d, writes=["negmask3"])
                P.dma("pool", mask4[:], mask4_d, writes=["mask4"])
                P.dma("pool", masksl[:], lrev_d, writes=["masksl"])
                P.dma("pool", bd[:], bd_d, writes=["bd"])
                P.dma("pool", imask[:], imask_d, writes=["imask"])
                for t_, d_, k_ in ((ltri, ltri_d, "ltri"), (lrev, lrev_d, "lrev"), (ones_f, ones_d, "ones_f"), (sel8, sel8_d, "sel8"), (selp8, selp8_d, "selp8"),
                                   (headsel, headsel_d, "headsel"), (cgT, cgT_d, "cgT"), (gpar, gpar_d, "gpar"), (onorm, onorm_d, "onorm"),
                                   (muT, muT_d, "muT"), (rwp, rwp_d, "rwp"), (lnx, lnx_d, "lnx")):
                    P.dma("sp", t_[:], d_, writes=[k_])
                ts("dve", omuT[:], muT[:], -1.0, 1.0, ALU.mult, ALU.add, reads=["muT"], writes=["omuT"])
                act(gpar[:, 0:8], gpar[:, 0:8], AF.Exp, reads=["gpar"], writes=["gpar"])
                ts("dve", gpar[:, 0:8], gpar[:, 0:8], -1.0, None, ALU.mult, reads=["gpar"], writes=["gpar"])
                for t_, k_ in ((Sf, "Sf"), (Mf, "Mf"), (cvcar, "cvcar"), (pbcar, "pbcar"), (Sb, "Sb"), (Mb, "Mb")):
                    P.emit("pool", lambda e, t_=t_: e.memset(t_[:], 0.0), writes=[k_])

                blk_ctr = [0, 0, 0]

                def wblk(name):
                    b = bidx[name]
                    r_, src, c0, ncols, nk = blocks[b]
                    s_ = blk_ctr[r_] % NRS[r_]
                    blk_ctr[r_] += 1
                    key = ("ring", r_, s_)
                    t_ = rings[r_][s_]
                    P.dma("sp", t_[:, 0:nk * ncols], wsc_d[b, :, 0:nk * ncols], writes=[key], nbytes=128 * nk * ncols * 2)
                    return t_[:, 0:nk * ncols].rearrange("p (k c) -> p k c", k=nk), key

                def proj_fm(bank, wv, wkey, j0, nch, hT, hk, widths=None):
                    for j in range(nch):
                        w_ = 128 if widths is None else widths[j]
                        for k in range(8):
                            mm(PB[bank][0:w_, (j0 + j) * 128:(j0 + j + 1) * 128], wv[:, k, j * 128: j * 128 + w_], hT[:, k, :],
                               start=(k == 0), stop=(k == 7), reads=[wkey, hk], writes=[f"pb{bank}"])

                def invert(HBt, hbname, heads, mo):
                    Q = (640, 1024)
                    groups = [heads[i:i + 2] for i in range(0, len(heads), 2)]
                    def G(h0, a, b_):
                        return WK[:, h0:h0 + 2, a:b_]

                    def b0_of(h0):
                        return 3 + 2 * ((h0 // 2) % 2)

                    for hs in groups:
                        h0 = hs[0]
                        hks = [(hbname, h) for h in hs]
                        wks = [("WK", h) for h in hs]
                        tt("dve", G(h0, 0, 128), HBt[:, h0:h0 + 2, mo:mo + 128], imask[:, 0:128].unsqueeze(1).to_broadcast([128, 2, 128]),
                           ALU.mult, reads=hks + ["imask"], writes=wks)
                        tt("dve", G(h0, 128, 640).rearrange("p h (a t) -> p h a t", a=4),
                           HBt[:, h0:h0 + 2, mo + 128:mo + 256].unsqueeze(2).to_broadcast([128, 2, 4, 128]),
                           imask[:].rearrange("p (a t) -> p a t", a=4).unsqueeze(1).to_broadcast([128, 2, 4, 128]),
                           ALU.mult, reads=hks + ["imask"], writes=wks)
                    for lvl in range(4):
                        for hs in groups:
                            h0 = hs[0]
                            b0 = b0_of(h0)
                            bks = [f"pb{b0}", f"pb{b0 + 1}"]
                            wks = [("WK", h) for h in hs]
                            do = Q[(lvl + 1) % 2]
                            so = Q[lvl % 2]
                            for i, h in enumerate(hs):
                                W = WK[:, h, :]
                                bank = b0 + i
                                if lvl == 0:
                                    Mp, MpT, Pm = W[:, 0:128], W[:, 128:256], None
                                else:
                                    Mp, MpT, Pm = W[:, so:so + 128], W[:, so + 128:so + 256], W[:, so + 256:so + 384]
                                last = (i == len(hs) - 1)
                                if lvl < 3:
                                    mm(PB[bank][:, 0:128], MpT, Mp, reads=[("WK", h)], writes=[bks[i]], inc=False)
                                    mm(PB[bank][:, 128:256], Mp, MpT, reads=[("WK", h)], writes=[bks[i]], inc=(last and lvl == 0))
                                if lvl > 0:
                                    mm(PB[bank][:, 256:384], MpT, Pm, reads=[("WK", h)], writes=[bks[i]], inc=last)
                            if lvl < 3:
                                cp("act", G(h0, do, do + 256), PBall[:, b0:b0 + 2, 0:256], reads=bks, writes=wks)
                            if lvl == 0:
                                tt("pool", G(h0, do + 256, do + 384), G(h0, 0, 128), ident_b[:].unsqueeze(1).to_broadcast([128, 2, 128]), ALU.add,
                                   reads=wks + ["ident_b"], writes=wks)
                            else:
                                tt("dve", G(h0, do + 256, do + 384), G(h0, so + 256, so + 384), PBall[:, b0:b0 + 2, 256:384], ALU.add, reads=bks + wks, writes=wks)
                    zo = Q[0] + 256
                    for l in range(1, 4):
                        for hs in groups:
                            h0 = hs[0]
                            b0 = b0_of(h0)
                            bks = [f"pb{b0}", f"pb{b0 + 1}"]
                            hks = [(hbname, h) for h in hs]
                            wks = [("WK", h) for h in hs]
                            for i, h in enumerate(hs):
                                tr(pbb(b0 + i)[:, 768:896], WK[:, h, zo:zo + 128], ident_b[:], reads=[("WK", h), "ident_b"], writes=[bks[i]], inc=(i == 1))
                            cp("act", G(h0, 1536, 1664), PBall_b[:, b0:b0 + 2, 768:896], reads=bks, writes=wks)
                            for i, h in enumerate(hs):
                                mm(PB[b0 + i][:, 0:128], WK[:, h, 128 + l * 128:256 + l * 128], WK[:, h, zo:zo + 128], reads=[("WK", h)], writes=[bks[i]], inc=(i == 1))
                            cp("act", G(h0, 1408, 1536), PBall[:, b0:b0 + 2, 0:128], reads=bks, writes=wks)
                            for i, h in enumerate(hs):
                                mm(PB[b0 + i][:, 128:256], WK[:, h, 1536:1664], WK[:, h, 1408:1536], reads=[("WK", h)], writes=[bks[i]], inc=(i == 1))
                            if l < 3:
                                zn = Q[l % 2] + 256
                                tt("dve", G(h0, zn, zn + 128), G(h0, zo, zo + 128), PBall[:, b0:b0 + 2, 128:256], ALU.add, reads=bks + wks, writes=wks)
                            else:
                                tt("dve", HBt[:, h0:h0 + 2, mo:mo + 128], G(h0, zo, zo + 128), PBall[:, b0:b0 + 2, 128:256], ALU.add,
                                   reads=bks + wks + hks, writes=hks)
                        if l < 3:
                            zo = Q[l % 2] + 256

                def bc8(col_ap):
                    return col_ap.unsqueeze(2).to_broadcast([128, 8, 64])

                def h8(ap):
                    return ap.rearrange("p (h d) -> p h d", h=8)

                cva, rn, zs, vb, otok, tmpf, egc, tmp2 = range(8)
                rT, kT_, vT, sg, asig, gtok, kk, T1, cum, lg = range(10)
                ytk, yc = rT, kT_
                widths_all = [128] * 14 + [32]
                lor = FR3(lg)[:, 0, :]
                glo = FR3(lg)[:, 1:3, :]
                ck = lambda fi: [("FRc", fi, j) for j in range(4)]
                lgk = ck(lg)[0:3]
                stages = {}


                def st_norm(it):
                    r0 = it * 128
                    par = it % 2
                    dbg_tile = (taps is not None and it == (1 if NT > 1 else 0))
                    xat, xk = xa[par], f"xa{par}"
                    hT, hk_ = hTs[par], f"hT{par}"
                    P.tag = "%d:norm" % it
                    P.dma("sp", xat[:], x_d[r0:r0 + 128, :], writes=[xk], nbytes=524288)
                    norm_to_featmajor(xat[:], [xk], xnA[:], ["xnA"], g1T, sh1T, "g1T", hT, hk_, 0, 4 + 4 * par, banks=(2, 2))


                def st_G_prep(it):
                    r0 = it * 128
                    par = it % 2
                    dbg_tile = (taps is not None and it == (1 if NT > 1 else 0))
                    xat, xk = xa[par], f"xa{par}"
                    hT, hk_ = hTs[par], f"hT{par}"
                    P.tag = "%d:G.prep" % it
                    wv, wkey = wblk("gz0")
                    for k in range(8):
                        mm(PB[0][:, 0:256], hT[:, k, :], wv[:, k, 0:256], start=(k == 0), stop=(k == 7), reads=[hk_, wkey], writes=["pb0"])
                    wv, wkey = wblk("gz1")
                    for k in range(8):
                        mm(PB[0][:, 256:512], hT[:, k, :], wv[:, k, 0:256], start=(k == 0), stop=(k == 7), reads=[hk_, wkey], writes=["pb0"])
                    for k in range(8):
                        mm(PB[1][:, 0:16], hT[:, k, :], wv[:, k, 256:272], start=(k == 0), stop=(k == 7), reads=[hk_, wkey], writes=["pb1"])
                    act(FGi(zs), PB[0][:, :], AF.Silu, reads=["pb0"], writes=[GK(zs)])
                    tt("dve", h8(FGi(zs)), h8(FGi(zs)), onorm[:].unsqueeze(1).to_broadcast([128, 8, 64]), ALU.mult, reads=[GK(zs), "onorm"], writes=[GK(zs)])
                    ts("dve", gsm[:, 0:8], PB[1][:, 0:8], -1.0, None, ALU.mult, reads=["pb1"], writes=["gsm_a"])
                    tt("dve", gsm[:, 8:16], PB[1][:, 8:16], gpar[:, 8:16], ALU.add, reads=["pb1", "gpar"], writes=["gsm_a"])
                    act(gsm[:, 0:16], gsm[:, 0:16], AF.Exp, reads=["gsm_a"], writes=["gsm_a"])
                    act(gsm[:, 0:16], gsm[:, 0:16], AF.Ln, reads=["gsm_a", "cst"], writes=["gsm_a"], bias=cst[:, 1:2])
                    ts("dve", gcols[:, 0, :], gsm[:, 0:8], -1.0, None, ALU.mult, reads=["gsm_a"], writes=["gcols"])
                    tt("dve", gsm[:, 16:24], gsm[:, 8:16], gpar[:, 0:8], ALU.mult, reads=["gsm_a", "gpar"], writes=["gsm_g"])
                    mm(PB[1][:, 16:24], ltri[:], gsm[:, 16:24], reads=["ltri", "gsm_g"], writes=["pb1"])
                    mm(PB[1][:, 24:32], lrev[:], gsm[:, 16:24], reads=["lrev", "gsm_g"], writes=["pb1"])
                    cp("dve", gcols[:, 2:4, :], PB[1][:, 16:32].rearrange("p (a h) -> p a h", a=2), reads=["pb1"], writes=["gcols"])
                    tt("dve", gcols[:, 1, :], gcols[:, 0, :], gcols[:, 2, :], ALU.add, reads=["gcols"], writes=["gcols"])
                    act(tokx[:], gcols[:], AF.Exp, reads=["gcols"], writes=["tokx"])
                    ts("dve", gsm[:, 24:32], tokx[:, 1, :], -1.0, None, ALU.mult, reads=["tokx"], writes=["gsm_c"])
                    ts("dve", gsm[:, 32:40], gcols[:, 2, :], -1.0, None, ALU.mult, reads=["gcols"], writes=["gsm_c"])
                    for gi in range(3):
                        bank = gi % 2
                        for hh_ in range(2):
                            wv, wkey = wblk(("gq", "gk", "gv")[gi] + str(hh_))
                            proj_fm(bank, wv, wkey, 2 * hh_, 2, hT, hk_)
                        cp("pool", stgG[:, :, 0:3], cvcar[:, gi * 4:(gi + 1) * 4, :], reads=["cvcar"], writes=["stgG"])
                        cp("act", stgG[:, :, 3:131], PB[bank][:, :].rearrange("p (j t) -> p j t", j=4), reads=[f"pb{bank}"], writes=["stgG"])
                        cp("pool", cvcar[:, gi * 4:(gi + 1) * 4, :], stgG[:, :, 128:131], reads=["stgG"], writes=["cvcar"])
                        for j in range(4):
                            ch = gi * 4 + j
                            o_ = FG3(cva)[:, j, :]
                            ce_ = "dve"
                            ts(ce_, o_, stgG[:, j, 0:128], cgT[:, ch * 4:ch * 4 + 1], None, ALU.mult, reads=["stgG", "cgT"], writes=[("cva", j)])
                            for tp in range(1, 4):
                                stt(ce_, o_, stgG[:, j, tp:tp + 128], cgT[:, ch * 4 + tp:ch * 4 + tp + 1], o_, ALU.mult, ALU.add,
                                    reads=["stgG", "cgT", ("cva", j)], writes=[("cva", j)])
                        cvk = [("cva", j) for j in range(4)]
                        act(FGi(cva), FGi(cva), AF.Silu, reads=cvk, writes=cvk)
                        if gi < 2:
                            act(sqG[:], FG3(cva), AF.Square, reads=cvk, writes=["sqG"])
                            for j in range(4):
                                mm(PB[2][:, j * 128:(j + 1) * 128], bd[:], sqG[:, j, :], reads=["bd", "sqG"], writes=["pb2"])
                            if gi == 0:
                                act(FGi(rn), PB[2][:, :], AF.Sqrt, reads=["pb2", "cst"], writes=[GK(rn)], scale=64.0, bias=cst[:, 2:3])
                            else:
                                act(FGi(rn), PB[2][:, :], AF.Sqrt, reads=["pb2", "cst"], writes=[GK(rn)], scale=1.0, bias=cst[:, 0:1])
                            recip(FGi(rn), FGi(rn), reads=[GK(rn)], writes=[GK(rn)])
                            if gi == 0:
                                tt("dve", qkn[:, :, 0, :], FG3(cva), FG3(rn), ALU.mult, reads=cvk + [GK(rn)], writes=["qkn_q"])
                            else:
                                tt("dve", qkn[:, :, 3, :], FG3(cva), FG3(rn), ALU.mult, reads=cvk + [GK(rn)], writes=["qkn_k"])
                                for a_ in (1, 2):
                                    stt("dve", qkn[:, :, a_, :], FG3(cva), -1.0, FG3(rn), ALU.mult, ALU.mult, reads=cvk + [GK(rn)], writes=[("qkn_nk", a_)])
                                for j in range(4):
                                    tr(pbb(2)[:, j * 128:(j + 1) * 128], qkn[:, j, 3, :], ident_b[:], reads=["qkn_k", "ident_b"],
                                       writes=["pb2"], inc=(j == 3))
                                tt("dve", h8(kdecG[:]), h8(pbb(2)[:, 0:512]), bc8(tokx[:, 3, :]), ALU.mult, reads=["pb2", "tokx"], writes=["kdecG"])
                        else:
                            for j in range(4):
                                tr(PB[2][:, j * 128:(j + 1) * 128], FG3(cva)[:, j, :], ident_f[:], reads=cvk + ["ident_f"],
                                   writes=["pb2"], inc=(j == 3))
                            tt("dve", h8(FGi(vb)), h8(PB[2][:, :]), bc8(tokx[:, 0, :]), ALU.mult, reads=["pb2", "tokx"], writes=[GK(vb)])


                def st_G_S1(it):
                    r0 = it * 128
                    par = it % 2
                    dbg_tile = (taps is not None and it == (1 if NT > 1 else 0))
                    xat, xk = xa[par], f"xa{par}"
                    hT, hk_ = hTs[par], f"hT{par}"
                    P.tag = "%d:G.S1" % it
                    mm(PB[7][0:8, 0:128], gcols[:, 2, :], ident_f[:], reads=["gcols", "ident_f"], writes=["pb7"], inc=False)
                    mm(PB[7][0:8, 128:256], gcols[:, 1, :], ident_f[:], reads=["gcols", "ident_f"], writes=["pb7"], inc=False)
                    mm(PB[7][0:8, 256:384], gsm[:, 32:40], ident_f[:], reads=["gsm_c", "ident_f"], writes=["pb7"])
                    cp("act", R8[:, :], PB[7][0:8, 0:384], reads=["pb7"], writes=["R8"])
                    for f in range(4):
                        for hp in range(2):
                            h = 2 * f + hp
                            hk = ("HBg", h)
                            q4 = h % 2
                            ps_ = slice(hp * 64, (hp + 1) * 64)
                            bnk = 4 + h % 3
                            mm(PB[3][:, 0:384], qkn[ps_, f, 3, :], qkn[ps_, f, 0:3, :], reads=["qkn_q", "qkn_k", ("qkn_nk", 1), ("qkn_nk", 2)], writes=["pb3"])
                            mm(PB[bnk][:, 0:384], sel8[:, h * 128:(h + 1) * 128], R8[:, :], start=True, stop=False, reads=["sel8", "R8"], writes=[f"pb{bnk}"])
                            mm(PB[bnk][:, 0:384], ident_b[:], negmask3[:], start=False, stop=True, reads=["ident_b", "negmask3"], writes=[f"pb{bnk}"])
                            act(e3[:, q4, 0:256], PB[bnk][:, 0:256], AF.Exp, reads=[f"pb{bnk}", "gsm_c"], writes=[("e3", q4)], bias=gsm[:, 32 + h:33 + h])
                            act(e3[:, q4, 256:384], PB[bnk][:, 256:384], AF.Exp, reads=[f"pb{bnk}", "gcols"], writes=[("e3", q4)], bias=gcols[:, 1, h:h + 1])
                            tt("dve", HBg[:, h, 0:384], PB[3][:, 0:384], e3[:, q4, :], ALU.mult, reads=["pb3", ("e3", q4)], writes=[hk])
                        mm(PB[7][:, 384:512], selp8[:, f * 128:(f + 1) * 128], R8[:, 0:128], reads=["selp8", "R8"], writes=["pb7"])
                        act(FG3(egc)[:, f, :], PB[7][:, 384:512], AF.Exp, reads=["pb7"], writes=[("egc", f)])
                        tt("dve", qtl[:, f * 128:(f + 1) * 128], qkn[:, f, 0, :], FG3(egc)[:, f, :], ALU.mult, reads=["qkn_q", ("egc", f)], writes=[("qtl", f)])

                def st_G_S2(it):
                    r0 = it * 128
                    par = it % 2
                    dbg_tile = (taps is not None and it == (1 if NT > 1 else 0))
                    xat, xk = xa[par], f"xa{par}"
                    hT, hk_ = hTs[par], f"hT{par}"
                    P.tag = "%d:G.S2" % it
                    invert(HBg, "HBg", list(range(8)), 128)

                def st_G_S3(it):
                    r0 = it * 128
                    par = it % 2
                    dbg_tile = (taps is not None and it == (1 if NT > 1 else 0))
                    xat, xk = xa[par], f"xa{par}"
                    hT, hk_ = hTs[par], f"hT{par}"
                    P.tag = "%d:G.S3" % it
                    for f in range(4):
                        mm(PB[7][:, f * 128:(f + 1) * 128], qkn[:, f, 3, :], Sb[:, f, :], reads=["qkn_k", "Sb"], writes=["pb7"], inc=(f == 3))
                    tt("dve", h8(FGi(tmp2)), h8(PB[7][:, :]), bc8(gsm[:, 24:32]), ALU.mult, reads=["pb7", "gsm_c"], writes=[GK(tmp2)])
                    tt("dve", xgG[:], FGi(tmp2), FGi(vb), ALU.add, reads=[GK(tmp2), GK(vb)], writes=["xgG"])
                    for h in range(8):
                        mm(PB[3][:, h * 64:(h + 1) * 64], HBg[:, h, 128:256], xgG[:, h * 64:(h + 1) * 64], reads=[("HBg", h), "xgG"], writes=["pb3"], inc=(h == 7))
                    cp("act", ugG[:], PB[3][:, :], reads=["pb3"], writes=["ugG"])
                    for f in range(4):
                        mm(PB[3][:, f * 128:(f + 1) * 128], qtl[:, f * 128:(f + 1) * 128], Sb[:, f, :], start=True, stop=False,
                           reads=[("qtl", f), "Sb"], writes=["pb3"], inc=False)
                        for hp in range(2):
                            h = 2 * f + hp
                            mm(PB[3][:, h * 64:(h + 1) * 64], HBg[:, h, 0:128], ugG[:, h * 64:(h + 1) * 64], start=False, stop=(hp == 1),
                               reads=[("HBg", h), "ugG"], writes=["pb3"], inc=(h == 7))
                    for f in range(4):
                        mm(PB[7][:, f * 128:(f + 1) * 128], kdecG[:, f * 128:(f + 1) * 128], ugG[:, f * 128:(f + 1) * 128],
                           reads=["kdecG", "ugG"], writes=["pb7"], inc=(f == 3))
                    for f in range(4):
                        for hp in range(2):
                            ps_ = slice(hp * 64, (hp + 1) * 64)
                            stt("dve", Sf[ps_, f, ps_], Sf[ps_, f, ps_], FG3(egc)[ps_, f, 127:128], PB[7][ps_, f * 128 + hp * 64:f * 128 + (hp + 1) * 64],
                                ALU.mult, ALU.add, reads=["pb7", ("egc", f), "Sf"], writes=["Sf"])
                    cp("pool", Sb[:], Sf[:], reads=["Sf"], writes=["Sb"])

                def st_G_epi(it):
                    r0 = it * 128
                    par = it % 2
                    dbg_tile = (taps is not None and it == (1 if NT > 1 else 0))
                    xat, xk = xa[par], f"xa{par}"
                    hT, hk_ = hTs[par], f"hT{par}"
                    P.tag = "%d:G.epi" % it
                    cp("act", FGi(otok), PB[3][:, :], reads=["pb3"], writes=[GK(otok)])
                    if dbg_tile:
                        tap("otok", FGi(otok), [GK(otok)])
                    tt("dve", FGi(tmpf), FGi(otok), FGi(otok), ALU.mult, reads=[GK(otok)], writes=[GK(tmpf)])
                    red(gsm[:, 40:48], h8(FGi(tmpf)), reads=[GK(tmpf)], writes=["gsm_e"])
                    act(gsm[:, 40:48], gsm[:, 40:48], AF.Sqrt, reads=["gsm_e", "cst"], writes=["gsm_e"], scale=1.0 / 64, bias=cst[:, 0:1])
                    recip(gsm[:, 40:48], gsm[:, 40:48], reads=["gsm_e"], writes=["gsm_e"])
                    tt("dve", h8(FGi(tmpf)), h8(FGi(otok)), bc8(gsm[:, 40:48]), ALU.mult, reads=[GK(otok), "gsm_e"], writes=[GK(tmpf)])
                    tt("dve", ytbG[:], FGi(tmpf), FGi(zs), ALU.mult, reads=[GK(tmpf), GK(zs)], writes=["ytbG"])
                    for j in range(4):
                        tr(pbb(7)[:, j * 128:(j + 1) * 128], ytbG[:, j * 128:(j + 1) * 128], ident_b[:], reads=["ytbG", "ident_b"],
                           writes=["pb7"], inc=(j == 3))
                    cp("act", yaT[:], pbb(7)[:, 0:512].rearrange("p (j t) -> p j t", j=4), reads=["pb7"], writes=["yaT"])


                def st_R_prep(it):
                    r0 = it * 128
                    par = it % 2
                    dbg_tile = (taps is not None and it == (1 if NT > 1 else 0))
                    xat, xk = xa[par], f"xa{par}"
                    hT, hk_ = hTs[par], f"hT{par}"
                    P.tag = "%d:R.prep" % it
                    for gi in range(4):
                        nch = 4 if gi < 3 else 3
                        bank = gi % 2
                        ws_ = widths_all[gi * 4: gi * 4 + nch]
                        if gi < 3:
                            for hh_ in range(2):
                                wv, wkey = wblk(("rr", "rk", "rv")[gi] + str(hh_))
                                proj_fm(bank, wv, wkey, 2 * hh_, 2, hT, hk_)
                        else:
                            wv, wkey = wblk("rl")
                            proj_fm(bank, wv, wkey, 0, nch, hT, hk_, ws_)
                        cp("pool", stgR[:, 0:nch, 0:1], pbcar[:, gi * 4:gi * 4 + nch].unsqueeze(2), reads=["pbcar"], writes=["stgR"])
                        if gi < 3:
                            cp("act", stgR[:, 0:nch, 1:129], PB[bank][:, 0:nch * 128].rearrange("p (j t) -> p j t", j=nch),
                               reads=[f"pb{bank}"], writes=["stgR"])
                        else:
                            cp("act", stgR[:, 0:2, 1:129], PB[bank][:, 0:256].rearrange("p (j t) -> p j t", j=2), reads=[f"pb{bank}"], writes=["stgR"])
                            cp("act", stgR[0:32, 2, 1:129], PB[bank][0:32, 256:384], reads=[f"pb{bank}"], writes=["stgR"])
                        cp("pool", pbcar[:, gi * 4:gi * 4 + nch].unsqueeze(2), stgR[:, 0:nch, 128:129], reads=["stgR"], writes=["pbcar"])
                        for j in range(nch):
                            ch = gi * 4 + j
                            w_ = ws_[j]
                            fi = gi if gi < 3 else lg
                            dst = FR3(fi)[0:w_, j, :]
                            dk = ("FRc", fi, j)
                            se_ = "dve"
                            ts(se_, dst, stgR[0:w_, j, 1:129], omuT[0:w_, ch:ch + 1], None, ALU.mult, reads=["stgR", "omuT"], writes=[dk])
                            stt(se_, dst, stgR[0:w_, j, 0:128], muT[0:w_, ch:ch + 1], dst, ALU.mult, ALU.add, reads=["stgR", "muT", dk], writes=[dk])
                    for j in range(4):
                        tr(PB[2][:, j * 128:(j + 1) * 128], FR3(vT)[:, j, :], ident_f[:], reads=ck(vT) + ["ident_f"], writes=["pb2"], inc=(j == 3))
                    cp("act", vtk[:], PB[2][:, :], reads=["pb2"], writes=["vtk"])
                    cp("dve", FRi(vT), PB[2][:, :], reads=["pb2"], writes=ck(vT) + [RK(vT)])
                    act(lorb[0:64, :], lor[0:64, :], AF.Tanh, reads=lgk, writes=["lorb"])
                    cp("dve", lorb[64:128, :], lor[64:128, :], reads=lgk, writes=["lorb"])
                    act(sglb[:, 0, :], glo[:, 0, :], AF.Sigmoid, reads=lgk, writes=["sglb"])
                    act(sglb[0:32, 1, :], glo[0:32, 1, :], AF.Sigmoid, reads=lgk, writes=["sglb"])
                    for j in range(4):
                        mm(PB[0][:, j * 128:(j + 1) * 128], lora[0:64, j * 128:(j + 1) * 128], lorb[0:64, :], reads=["lora", "lorb"], writes=["pb0"])
                    for j in range(4):
                        mm(PB[1][:, j * 128:(j + 1) * 128], lora[64:128, j * 128:(j + 1) * 128], lorb[64:128, :], reads=["lora", "lorb"], writes=["pb1"])
                    for j in range(4):
                        act(FR3(sg)[:, j, :], PB[0][:, j * 128:(j + 1) * 128], AF.Sigmoid, reads=["pb0", "rwp"], writes=[RK(sg)], bias=rwp[:, j:j + 1])
                        act(FR3(asig)[:, j, :], PB[1][:, j * 128:(j + 1) * 128], AF.Sigmoid, reads=["pb1", "rwp"], writes=[RK(asig)], bias=rwp[:, 4 + j:5 + j])
                    mm(PB[2][:, :], sglb[:, 0, :], g2s[:, 0, :], start=True, stop=False, reads=["sglb", "g2s"], writes=["pb2"])
                    mm(PB[2][:, :], sglb[0:32, 1, :], g2s[0:32, 1, :], start=False, stop=True, reads=["sglb", "g2s"], writes=["pb2"])
                    cp("act", FRi(gtok), PB[2][:, :], reads=["pb2"], writes=[RK(gtok)])
                    for j in range(4):
                        ts("dve", FR3(kk)[:, j, :], FR3(kT_)[:, j, :], rwp[:, 8 + j:9 + j], None, ALU.mult, reads=ck(kT_) + ["rwp"], writes=[RK(kk)])
                    act(sqR[:], FR3(kk), AF.Square, reads=[RK(kk)], writes=["sqR"])
                    for j in range(4):
                        mm(PB[2][:, j * 128:(j + 1) * 128], bd[:], sqR[:, j, :], reads=["bd", "sqR"], writes=["pb2"])
                    act(FRi(T1), PB[2][:, :], AF.Sqrt, reads=["pb2", "cst"], writes=[RK(T1)], scale=1.0, bias=cst[:, 0:1])
                    recip(FRi(T1), FRi(T1), reads=[RK(T1)], writes=[RK(T1)])
                    tt("dve", FRi(kk), FRi(kk), FRi(T1), ALU.mult, reads=[RK(kk), RK(T1)], writes=[RK(kk)])
                    for j in range(4):
                        ts("dve", FR3(T1)[:, j, :], FR3(asig)[:, j, :], -1.0, rwp[:, 12 + j:13 + j], ALU.add, ALU.mult, reads=[RK(asig), "rwp"], writes=[RK(T1)])
                    stt("dve", FRi(kT_), FRi(T1), 1.0, FRi(kT_), ALU.add, ALU.mult, reads=[RK(T1)] + ck(kT_), writes=ck(kT_) + [RK(kT_)])
                    tt("dve", FRi(asig), FRi(asig), FRi(kk), ALU.mult, reads=[RK(asig), RK(kk)], writes=[RK(asig)])
                    for j in range(4):
                        ts("dve", FR3(T1)[:, j, :], FR3(rT)[:, j, :], rwp[:, 16 + j:17 + j], None, ALU.mult, reads=ck(rT) + ["rwp"], writes=[RK(T1)])
                    tt("dve", FRi(T1), FRi(T1), FRi(kT_), ALU.mult, reads=[RK(T1), RK(kT_)], writes=[RK(T1)])
                    for j in range(4):
                        mm(PB[2][:, 0:8], FR3(T1)[:, j, :], headsel[:, j * 8:(j + 1) * 8], start=(j == 0), stop=(j == 3), reads=[RK(T1), "headsel"], writes=["pb2"])
                    cp("act", gsr[:, 48:56], PB[2][:, 0:8], reads=["pb2"], writes=["gsr_b"])
                    for j in range(4):
                        P.emit("dve", lambda e, j=j: e.tensor_tensor_scan(out=FR3(cum)[:, j, :], data0=ones_f[:], data1=FR3(sg)[:, j, :], initial=0.0,
                                                                           op0=ALU.mult, op1=ALU.add), reads=["ones_f", RK(sg)], writes=[RK(cum)], cost=420.0)
                    act(FRi(T1), FRi(cum), AF.Exp, reads=[RK(cum)], writes=[RK(T1)], scale=-C05)
                    tt("dve", rat[:, :, 0, :], FR3(rT), FR3(T1), ALU.mult, reads=ck(rT) + [RK(T1)], writes=["rat_r"])
                    tt("dve", FRi(rT), FRi(cum), FRi(sg), ALU.subtract, reads=[RK(cum), RK(sg)], writes=ck(rT) + [RK(rT)])
                    act(FRi(rT), FRi(rT), AF.Exp, reads=[RK(rT)], writes=[RK(rT)], scale=-C05)
                    stt("dve", rat[:, :, 1, :], FR3(kk), -1.0, FR3(rT), ALU.mult, ALU.mult, reads=[RK(kk), RK(rT)], writes=["rat_a"])
                    act(FRi(T1), FRi(cum), AF.Exp, reads=[RK(cum)], writes=[RK(T1)], scale=C05)
                    tt("dve", btl, FR3(asig), FR3(T1), ALU.mult, reads=[RK(asig), RK(T1)], writes=["btl"])
                    tt("dve", ktl, FR3(kT_), FR3(T1), ALU.mult, reads=[RK(kT_), RK(T1)], writes=["ktl"])
                    ts("dve", gsr[:, 56:60], FR3(cum)[:, :, 127], -C05, None, ALU.mult, reads=[RK(cum)], writes=["gsr_c"])
                    for j in range(4):
                        act(FR3(T1)[:, j, :], FR3(cum)[:, j, :], AF.Exp, reads=[RK(cum), "gsr_c"], writes=[RK(T1)], scale=C05, bias=gsr[:, 56 + j:57 + j])
                    act(gsr[:, 60:64], gsr[:, 56:60], AF.Exp, reads=["gsr_c"], writes=["gsr_g"])
                    tt("dve", sqR[:], FR3(asig), FR3(T1), ALU.mult, reads=[RK(asig), RK(T1)], writes=["sqR"])
                    tt("dve", h2[:].rearrange("p (j t) -> p j t", j=4), FR3(kT_), FR3(T1), ALU.mult, reads=[RK(kT_), RK(T1)], writes=["h2"])
                    for j in range(4):
                        tr(pbb(2)[:, j * 128:(j + 1) * 128], sqR[:, j, :], ident_b[:], reads=["sqR", "ident_b"], writes=["pb2"], inc=False)
                    for j in range(4):
                        tr(pbb(2)[:, 512 + j * 128:512 + (j + 1) * 128], h2[:, j * 128:(j + 1) * 128], ident_b[:], reads=["h2", "ident_b"],
                           writes=["pb2"], inc=(j == 3))
                    cp("act", bdtok[:], pbb(2)[:, 0:512], reads=["pb2"], writes=["bdtok"])
                    cp("act", kdtok[:], pbb(2)[:, 512:1024], reads=["pb2"], writes=["kdtok"])


                def st_R_S1(it):
                    r0 = it * 128
                    par = it % 2
                    dbg_tile = (taps is not None and it == (1 if NT > 1 else 0))
                    xat, xk = xa[par], f"xa{par}"
                    hT, hk_ = hTs[par], f"hT{par}"
                    P.tag = "%d:R.S1" % it
                    for f in range(4):
                        for hp in range(2):
                            h = 2 * f + hp
                            hk = ("HBr", h)
                            bnk = 4 + h % 3
                            ps_ = slice(hp * 64, (hp + 1) * 64)
                            mm(PB[3][:, 0:256], ktl[ps_, f, :], rat[ps_, f, :, :], reads=["ktl", "rat_r", "rat_a"], writes=["pb3"], inc=False)
                            mm(PB[3][:, 256:512], btl[ps_, f, :], rat[ps_, f, :, :], reads=["btl", "rat_r", "rat_a"], writes=["pb3"])
                            mm(PB[bnk][:, 0:128], rat[ps_, f, 1, :], btl[ps_, f, :], reads=["btl", "rat_a"], writes=[f"pb{bnk}"])
                            tt("dve", HBr[:, h, 0:512], PB[3][:, :], mask4[:], ALU.mult, reads=["pb3", "mask4"], writes=[hk])
                            tt("dve", HBr[:, h, 512:640], PB[bnk][:, 0:128], masksl[:], ALU.mult, reads=[f"pb{bnk}", "masksl"], writes=[hk])

                def st_R_S2(it):
                    r0 = it * 128
                    par = it % 2
                    dbg_tile = (taps is not None and it == (1 if NT > 1 else 0))
                    xat, xk = xa[par], f"xa{par}"
                    hT, hk_ = hTs[par], f"hT{par}"
                    P.tag = "%d:R.S2" % it
                    invert(HBr, "HBr", list(range(8)), 384)

                def st_R_S3(it):
                    r0 = it * 128
                    par = it % 2
                    dbg_tile = (taps is not None and it == (1 if NT > 1 else 0))
                    xat, xk = xa[par], f"xa{par}"
                    hT, hk_ = hTs[par], f"hT{par}"
                    P.tag = "%d:R.S3" % it
                    for f in range(4):
                        mm(PB[7][:, f * 128:(f + 1) * 128], rat[:, f, 1, :], Mb[:, f, :], start=True, stop=False, reads=["rat_a", "Mb"], writes=["pb7"], inc=False)
                        for hp in range(2):
                            h = 2 * f + hp
                            mm(PB[7][:, h * 64:(h + 1) * 64], HBr[:, h, 128:256], vtk[:, h * 64:(h + 1) * 64], start=False, stop=(hp == 1),
                               reads=[("HBr", h), "vtk"], writes=["pb7"], inc=(h == 7))
                    cp("act", xgR[:], PB[7][:, :], reads=["pb7"], writes=["xgR"])
                    for h in range(8):
                        mm(PB[3][:, h * 64:(h + 1) * 64], HBr[:, h, 384:512], xgR[:, h * 64:(h + 1) * 64], reads=[("HBr", h), "xgR"], writes=["pb3"], inc=(h == 7))
                    cp("act", ugR[:], PB[3][:, :], reads=["pb3"], writes=["ugR"])
                    for f in range(4):
                        mm(PB[3][:, f * 128:(f + 1) * 128], rat[:, f, 0, :], Mb[:, f, :], start=True, stop=False, reads=["rat_r", "Mb"], writes=["pb3"], inc=False)
                        for hp in range(2):
                            h = 2 * f + hp
                            mm(PB[3][:, h * 64:(h + 1) * 64], HBr[:, h, 256:384], ugR[:, h * 64:(h + 1) * 64], start=False, stop=False,
                               reads=[("HBr", h), "ugR"], writes=["pb3"], inc=False)
                            mm(PB[3][:, h * 64:(h + 1) * 64], HBr[:, h, 0:128], vtk[:, h * 64:(h + 1) * 64], start=False, stop=(hp == 1),
                               reads=[("HBr", h), "vtk"], writes=["pb3"], inc=(h == 7))
                    for f in range(4):
                        mm(PB[7][:, f * 128:(f + 1) * 128], bdtok[:, f * 128:(f + 1) * 128], ugR[:, f * 128:(f + 1) * 128], start=True, stop=False,
                           reads=["bdtok", "ugR"], writes=["pb7"], inc=False)
                        mm(PB[7][:, f * 128:(f + 1) * 128], kdtok[:, f * 128:(f + 1) * 128], vtk[:, f * 128:(f + 1) * 128], start=False, stop=True,
                           reads=["kdtok", "vtk"], writes=["pb7"], inc=(f == 3))
                    for f in range(4):
                        for hp in range(2):
                            ps_ = slice(hp * 64, (hp + 1) * 64)
                            stt("dve", Mf[ps_, f, ps_], Mf[ps_, f, ps_], gsr[ps_, 60 + f:61 + f], PB[7][ps_, f * 128 + hp * 64:f * 128 + (hp + 1) * 64],
                                ALU.mult, ALU.add, reads=["pb7", "gsr_g", "Mf"], writes=["Mf"])
                    cp("pool", Mb[:], Mf[:], reads=["Mf"], writes=["Mb"])

                def st_R_epi(it):
                    r0 = it * 128
                    par = it % 2
                    dbg_tile = (taps is not None and it == (1 if NT > 1 else 0))
                    xat, xk = xa[par], f"xa{par}"
                    hT, hk_ = hTs[par], f"hT{par}"
                    P.tag = "%d:R.epi" % it
                    cp("act", FRi(ytk), PB[3][:, :], reads=["pb3"], writes=[RK(ytk)])
                    if dbg_tile:
                        tap("ytok", FRi(ytk), [RK(ytk)])
                    y3, c3, s3 = h8(FRi(ytk)), h8(FRi(yc)), h8(FRi(T1))
                    red(gsr[:, 0:8], y3, reads=[RK(ytk)], writes=["gsr_e"])
                    ts("dve", gsr[:, 0:8], gsr[:, 0:8], -1.0 / 64, None, ALU.mult, reads=["gsr_e"], writes=["gsr_e"])
                    tt("dve", c3, y3, bc8(gsr[:, 0:8]), ALU.add, reads=[RK(ytk), "gsr_e"], writes=[RK(yc)])
                    tt("dve", FRi(T1), FRi(yc), FRi(yc), ALU.mult, reads=[RK(yc)], writes=[RK(T1)])
                    red(gsr[:, 8:16], s3, reads=[RK(T1)], writes=["gsr_f"])
                    act(gsr[:, 8:16], gsr[:, 8:16], AF.Sqrt, reads=["gsr_f", "cst"], writes=["gsr_f"], scale=1.0 / 64, bias=cst[:, 3:4])
                    recip(gsr[:, 8:16], gsr[:, 8:16], reads=["gsr_f"], writes=["gsr_f"])
                    tt("dve", c3, c3, bc8(gsr[:, 8:16]), ALU.mult, reads=[RK(yc), "gsr_f"], writes=[RK(yc)])
                    tt("dve", FRi(yc), FRi(yc), lnx[:, 0:512], ALU.mult, reads=[RK(yc), "lnx"], writes=[RK(yc)])
                    tt("dve", FRi(yc), FRi(yc), lnx[:, 512:1024], ALU.add, reads=[RK(yc), "lnx"], writes=[RK(yc)])
                    tt("dve", s3, h8(FRi(vT)), bc8(gsr[:, 48:56]), ALU.mult, reads=[RK(vT), "gsr_b"], writes=[RK(T1)])
                    tt("dve", FRi(yc), FRi(yc), FRi(T1), ALU.add, reads=[RK(yc), RK(T1)], writes=[RK(yc)])
                    tt("dve", ytbR[:], FRi(yc), FRi(gtok), ALU.mult, reads=[RK(yc), RK(gtok)], writes=["ytbR"])
                    for j in range(4):
                        tr(pbb(7)[:, j * 128:(j + 1) * 128], ytbR[:, j * 128:(j + 1) * 128], ident_b[:], reads=["ytbR", "ident_b"],
                           writes=["pb7"], inc=(j == 3))
                    cp("act", ybT[:], pbb(7)[:, 0:512].rearrange("p (j t) -> p j t", j=4), reads=["pb7"], writes=["ybT"])


                def st_merge(it):
                    r0 = it * 128
                    par = it % 2
                    dbg_tile = (taps is not None and it == (1 if NT > 1 else 0))
                    xat, xk = xa[par], f"xa{par}"
                    hT, hk_ = hTs[par], f"hT{par}"
                    P.tag = "%d:merge" % it
                    sga, sgb, m1, m2 = sg, asig, kk, cum
                    for g in range(2):
                        for bank_, nm_ in ((0, "gla"), (1, "glb")):
                            for hh_ in range(2):
                                wv, wkey = wblk(f"{nm_}{g}{hh_}")
                                for k in range(8):
                                    mm(PB[bank_][:, hh_ * 256:(hh_ + 1) * 256], hT[:, k, :], wv[:, k, :], start=(k == 0), stop=(k == 7),
                                       reads=[wkey, hk_], writes=[f"pb{bank_}"])
                        act(msa[:], PB[0][:, :], AF.Sigmoid, reads=["pb0"], writes=["msa"])
                        act(msb[:], PB[1][:, :], AF.Sigmoid, reads=["pb1"], writes=["msb"])
                        for nm_, yT_, yk_, ms_, msk_, mo_, mok_ in (("wbg", yaT, "yaT", msa, "msa", mm1, "mm1"), ("wbr", ybT, "ybT", msb, "msb", mm2, "mm2")):
                            for hh_ in range(2):
                                wv, wkey = wblk(f"{nm_}{g}{hh_}")
                                for k in range(4):
                                    mm(PB[2][:, hh_ * 256:(hh_ + 1) * 256], yT_[:, k, :], wv[:, k, :], start=(k == 0), stop=(k == 3),
                                       reads=[wkey, yk_], writes=["pb2"])
                            tt("dve", mo_[:], ms_[:], PB[2][:, :], ALU.mult, reads=[msk_, "pb2"], writes=[mok_])
                        tt("dve", mgt[:, g * 512:(g + 1) * 512], mm1[:], mm2[:], ALU.add, reads=["mm1", "mm2"], writes=[("mgt", g)])
                        for j in range(4):
                            tr(pbb(2)[:, j * 128:(j + 1) * 128], mgt[:, g * 512 + j * 128:g * 512 + (j + 1) * 128], ident_b[:], reads=[("mgt", g), "ident_b"],
                               writes=["pb2"], inc=(j == 3))
                        cp("act", mg[:, g * 4:(g + 1) * 4, :], pbb(2)[:, 0:512].rearrange("p (j t) -> p j t", j=4), reads=["pb2"], writes=[("mg", g)])
                    for hf in range(2):
                        bk = hf
                        for hh_ in range(2):
                            wv, wkey = wblk(f"wo{2 * hf + hh_}")
                            for k in range(8):
                                mm(PB[bk][:, hh_ * 256:(hh_ + 1) * 256], mg[:, k, :], wv[:, k, :], start=(k == 0), stop=(k == 7),
                                   reads=[("mg", 0), ("mg", 1), wkey], writes=[f"pb{bk}"])
                        tt("dve", xat[:, hf * 512:(hf + 1) * 512], xat[:, hf * 512:(hf + 1) * 512], PB[bk][:, :], ALU.add,
                           reads=[xk, f"pb{bk}"], writes=[xk])
                    P.dma("sp", x1_d[r0:r0 + 128, :], xat[:], reads=[xk], writes=[("x1_dram", it)], nbytes=524288)

                st_norm(0)
                st_G_prep(0)
                for it in range(NT):
                    st_G_S1(it)
                    st_R_prep(it)
                    st_G_S2(it)
                    st_G_S3(it)
                    st_G_epi(it)
                    st_R_S1(it)
                    if it + 1 < NT:
                        st_norm(it + 1)
                        st_G_prep(it + 1)
                    st_R_S2(it)
                    st_R_S3(it)
                    st_R_epi(it)
                    st_merge(it)
                P.replay()

        GS = 4 if NT % 4 == 0 else (2 if NT % 2 == 0 else 1)
        TG = GS * 128
        with ExitStack() as sB:
            wfi = sbt(sB, "wfi", [128, 8, 2 * D_FF], BF16)
            wfo = sbt(sB, "wfo", [128, 22, D], BF16)
            cfT = sbt(sB, "cfT_sb", [128, 66], F32)
            nf_bc = sbt(sB, "nf_bc_sb", [128, D], F32)
            xs = sbt(sB, "xs", [128, GS, D], F32)
            xnB = sbt(sB, "xnB", [128, D], F32)
            hT2 = sbt(sB, "hT2", [128, 8, TG], BF16)
            hid = sbt(sB, "hid", [128, 22, TG], BF16)
            gbuf = [sbt(sB, f"gbuf{i}", [128, TG + 2], F32) for i in range(2)]
            cacc = [sbt(sB, f"cacc{i}", [128, TG], F32) for i in range(2)]
            carry = sbt(sB, "carryB", [128, 22, 2], F32)
            QW = 1408
            for q_ in (0, 2, 1, 3):
                for k in range(8):
                    P.dma("pool", wfi[:, k, q_ * QW:(q_ + 1) * QW], wfi_d[k * 128:(k + 1) * 128, q_ * QW:(q_ + 1) * QW], writes=[("wfi", q_)],
                          nbytes=128 * QW * 4)
            P.dma("sp", xnB[:], gscr_d[:, D:2 * D], reads=["gscr"], writes=["xnB"])
            for c in range(22):
                s_ = c % GS if GS > 1 else 0
                P.dma("sp", xs[:, s_, :], wfo_d[c * 128:(c + 1) * 128, :], writes=[("xs", s_)])
                tt("dve", wfo[:, c, :], xs[:, s_, :], xnB[:], ALU.mult, reads=[("xs", s_), "xnB"], writes=["wfo"])
            P.dma("sp", cfT[:], cfT_d, writes=["cfT"])
            P.dma("sp", nf_bc[:], nf_d, writes=["nf_bc"])
            P.emit("pool", lambda e: e.memset(carry[:], 0.0), writes=["carryB"])
            src_d = x_d if mode == "skipA" else x1_d
            for g in range(NT // GS):
                for s in range(GS):
                    r0 = (g * GS + s) * 128
                    P.dma("sp", xs[:, s, :], src_d[r0:r0 + 128, :], writes=[("xs", s)], nbytes=524288)
                    norm_to_featmajor(xs[:, s, :], [("xs", s)], xnB[:], ["xnB"], g2T, sh2T, "g2T", hT2, "hT2", s * 128, 0)
                for c in range(22):
                    gb = gbuf[c % 2]
                    gk = f"gbuf{c % 2}"
                    ca = cacc[c % 2]
                    ck = f"cacc{c % 2}"
                    bg = c % 2
                    for k in range(8):
                        mm(PB[bg][:, 0:TG], wfi[:, k, c * 128:(c + 1) * 128], hT2[:, k, :], start=(k == 0), stop=(k == 7),
                           reads=[("wfi", (c * 128) // 1408), "hT2"], writes=[f"pb{bg}"])
                    for k in range(8):
                        mm(PB[2 + bg][:, 0:TG], wfi[:, k, D_FF + c * 128:D_FF + (c + 1) * 128], hT2[:, k, :], start=(k == 0),
                           stop=(k == 7), reads=[("wfi", (D_FF + c * 128) // 1408), "hT2"], writes=[f"pb{2 + bg}"])
                    cp("pool", gb[:, 0:2], carry[:, c, :], reads=["carryB"], writes=[gk])
                    cp("act", gb[:, 2:TG + 2], PB[bg][:, 0:TG], reads=[f"pb{bg}"], writes=[gk])
                    cp("pool", carry[:, c, :], gb[:, TG:TG + 2], reads=[gk], writes=["carryB"])
                    ts("dve", ca[:], gb[:, 0:TG], cfT[:, c * 3:c * 3 + 1], None, ALU.mult, reads=[gk, "cfT"], writes=[ck])
                    stt("dve", ca[:], gb[:, 1:TG + 1], cfT[:, c * 3 + 1:c * 3 + 2], ca[:], ALU.mult, ALU.add, reads=[gk, ck, "cfT"], writes=[ck])
                    stt("dve", ca[:], gb[:, 2:TG + 2], cfT[:, c * 3 + 2:c * 3 + 3], ca[:], ALU.mult, ALU.add, reads=[gk, ck, "cfT"], writes=[ck])
                    act(ca[:], ca[:], AF.Silu, reads=[ck], writes=[ck])
                    tt("dve", hid[:, c, :], ca[:], PB[2 + bg][:, 0:TG], ALU.mult, reads=[ck, f"pb{2 + bg}"], writes=[("hid", c)])
                for s in range(GS):
                    r0 = (g * GS + s) * 128
                    for hf in range(2):
                        bk = 4 + hf
                        for c in range(22):
                            mm(PB[bk][:, :], hid[:, c, s * 128:(s + 1) * 128], wfo[:, c, hf * 512:(hf + 1) * 512], start=(c == 0),
                               stop=(c == 21), reads=[("hid", c), "wfo"], writes=[f"pb{bk}"])
                        xsl = xs[:, s, hf * 512:(hf + 1) * 512]
                        tt("dve", xsl, xsl, PB[bk][:, :], ALU.add, reads=[f"pb{bk}", ("xs", s)], writes=[("xs", s)])
                    rms_rstd(xs[:, s, :], [("xs", s)], xnB[:], ["xnB"], 2)
                    stt("dve", xs[:, s, :], xs[:, s, :], small[:, 2:3], nf_bc[:], ALU.mult, ALU.mult,
                        reads=[("xs", s), ("small", 2), "nf_bc"], writes=[("xs", s)])
                    P.dma("sp", y_d[r0:r0 + 128, :], xs[:, s, :], reads=[("xs", s)], writes=[("y_dram", r0)], nbytes=524288)
            P.barrier()
            P.replay()
    return nc


def _prep_inputs(inp, T):
    f = lambda a: np.ascontiguousarray(np.asarray(a, np.float32))
    shared = {
        "w_ada": f(inp["w_ada"][0]),
        "badaT": _fm(inp["b_ada"][0], 48),
        "bgate": _bc(np.concatenate([np.asarray(inp["b_ada"][0][2048:3072]), np.asarray(inp["b_ada"][0][5120:6144])])),
        "n1T": _fm(inp["norm1_w"][0], 8),
        "n2T": _fm(inp["norm2_w"][0], 8),
        "nf_bc": _bc(inp["norm_f_w"]),
        "w_in": f(inp["w_in"][0]),
        "cgT": np.ascontiguousarray(f(inp["conv_gdn"][0]).T.reshape(12, 128, 4).transpose(1, 0, 2).reshape(128, 48)),
        "gpar_bc": _bc(np.concatenate([np.asarray(inp["a_log"][0]), np.asarray(inp["dt_bias"][0])])),
        "onorm_bc": _bc(inp["onorm_gdn"][0]),
        "wbg": f(inp["w_branch_gdn"][0]),
        "wbr": f(inp["w_branch_rwkv"][0]),
        "wout": f(inp["w_out"][0]),
        "muT": _fm(inp["mu_rwkv"][0], 15),
        "rwp": np.ascontiguousarray(np.concatenate(
            [_fm(inp["w0"][0], 4), _fm(inp["a0"][0], 4), _fm(inp["k_k"][0], 4), _fm(inp["k_a"][0], 4),
             _fm(np.asarray(inp["r_k"][0]).reshape(-1), 4)], axis=1)),
        "lnx_bc": _bc(np.concatenate([np.asarray(inp["lnx_w"][0]), np.asarray(inp["lnx_b"][0])])),
        "lora": np.ascontiguousarray(np.concatenate([f(inp["w2"][0]), f(inp["a2"][0])], axis=0)),
        "g2": f(inp["g2"][0]),
        "w_ffn_in": f(inp["w_ffn_in"][0]),
        "cfT": np.ascontiguousarray(f(inp["conv_ffn"][0]).T.reshape(22, 128, 3).transpose(1, 0, 2).reshape(128, 66)),
        "w_ffn_out": f(inp["w_ffn_out"][0]),
    }
    shared.update(_consts())
    maps = []
    x = np.asarray(inp["x"], np.float32)
    c = np.asarray(inp["c"], np.float32)
    for b in range(x.shape[0]):
        m = dict(shared)
        m["x"] = np.ascontiguousarray(x[b, :T])
        m["cT"] = _fm(c[b], 8)
        maps.append(m)
    return maps


_NC_CACHE = {}


def kernel(**inputs):
    x = np.asarray(inputs["x"])
    B, T, _ = x.shape
    maps = _prep_inputs(inputs, T)
    if T not in _NC_CACHE:
        _NC_CACHE[T] = build_nc(T)
    nc = _NC_CACHE[T]
    res = run_bass_kernel_spmd(nc, maps, core_ids=list(range(B)))
    out = np.stack([np.asarray(r["y"], np.float32) for r in res.results], axis=0)
    return out
```

```python
import heapq
import numpy as np
from contextlib import ExitStack
import concourse.bass as bass
import concourse.mybir as mybir
from concourse.bass_utils import run_bass_kernel_spmd

F32 = mybir.dt.float32
BF16 = mybir.dt.bfloat16
AF = mybir.ActivationFunctionType
ALU = mybir.AluOpType
AX = mybir.AxisListType

D = 1024
D_IN = 5936
D_FF = 2816
NEG = -30000.0
ENGS = ("pe", "act", "dve", "pool", "sp")
SEM_LAT = 180.0
DMA_ISSUE = 80.0
DMA_LAT = 2000.0
ACT_TABLE_NS = 1283.0
ACT_LOOKAHEAD = 1
DMA_BPNS = 250.0


class Op:
    __slots__ = ("eng", "fns", "deps", "bdeps", "cost", "idx", "kind", "nbytes", "pos", "slot", "val", "done", "out_ap", "in_ap", "kw", "tag", "crit", "start", "why", "tset")

    def __init__(self, eng, kind, idx):
        self.eng = eng
        self.kind = kind
        self.idx = idx
        self.fns = []
        self.deps = set()
        self.bdeps = set()
        self.cost = 100.0
        self.pos = 0
        self.tag = ""
        self.crit = None
        self.why = {}
        self.tset = None


class Prog:
    def __init__(self, nc, sems, dma_sems, reorder=True):
        self.nc = nc
        self.sem = sems
        self.dma_sems = dma_sems
        self.reorder = reorder
        self.dma_val = [0] * len(dma_sems)
        self.dma_free = [0.0] * len(dma_sems)
        self.dma_last = [None] * len(dma_sems)
        self.cnt = {e: 0 for e in ENGS}
        self.waited = {e: {} for e in ENGS}
        self.seq = []
        self.last_w = {}
        self.readers = {}
        self.aliases = {}
        self.consts = set()
        self.bank_last = {}
        self.tag = ""
        self._pend = None

    def _slots(self, eng):
        n = len(self.dma_sems)
        return range(n - 6, n) if eng == "pool" else range(0, n - 6)

    def alias(self, *keys):
        for k in keys:
            self.aliases.setdefault(k, set()).update(x for x in keys if x != k)

    def const(self, *keys):
        self.consts.update(keys)

    def _expand(self, writes):
        out = list(writes)
        for w in writes:
            for a in self.aliases.get(w, ()):
                if a not in out:
                    out.append(a)
        return out

    def _track(self, op, reads, writes):
        for k in list(reads) + list(writes):
            if isinstance(k, str) and k.startswith("pb"):
                bank = k[:3]
                prev = self.bank_last.get(bank)
                if prev is not None and prev is not op:
                    op.bdeps.add(prev)
                    op.why[prev] = ("bank", bank)
                self.bank_last[bank] = op
        writes = self._expand(writes)
        for b in reads:
            w = self.last_w.get(b)
            if w is not None:
                op.deps.add(w)
                op.why[w] = ("RAW", b)
        for b in writes:
            w = self.last_w.get(b)
            if w is not None:
                op.deps.add(w)
                op.why[w] = ("WAW", b)
            for r in self.readers.get(b, ()):
                op.deps.add(r)
                op.why[r] = ("WAR", b)
        op.deps.discard(op)
        for b in writes:
            self.last_w[b] = op
            self.readers[b] = []
        for b in reads:
            if b in self.consts:
                continue
            self.readers.setdefault(b, []).append(op)

    def emit(self, eng, fn, reads=(), writes=(), inc=True, cost=100.0, tset=None):
        if eng == "pe":
            if self._pend is None:
                self._pend = Op("pe", "c", len(self.seq))
                self._pend.cost = 0.0
                self._pend.tag = self.tag
            op = self._pend
            op.fns.append(fn)
            op.cost += cost
            self._track(op, reads, writes)
            if inc:
                self.seq.append(op)
                self._pend = None
            return op
        assert self._pend is None, "non-PE emit inside an open PE accumulation bundle"
        op = Op(eng, "c", len(self.seq))
        op.tag = self.tag
        op.tset = tset
        op.fns.append(fn)
        op.cost = cost
        self._track(op, reads, writes)
        self.seq.append(op)
        return op

    def dma(self, eng, out_ap, in_ap, reads=(), writes=(), nbytes=65536, **kw):
        assert self._pend is None
        op = Op(eng, "dma", len(self.seq))
        op.tag = self.tag
        op.out_ap, op.in_ap, op.kw = out_ap, in_ap, kw
        op.nbytes = nbytes
        op.cost = DMA_ISSUE
        self._track(op, reads, writes)
        self.seq.append(op)
        return op

    def _schedule(self):
        ops = self.seq
        n = len(ops)
        for i, op in enumerate(ops):
            op.idx = i
        inphase = set(ops)
        succs = [[] for _ in range(n)]
        ndeps = [0] * n
        for op in ops:
            op.deps = {d for d in op.deps if d in inphase}
            op.bdeps = {d for d in op.bdeps if d in inphase and d not in op.deps}
            alld = op.deps | op.bdeps
            ndeps[op.idx] = len(alld)
            for d in alld:
                succs[d.idx].append(op)
        order = {e: [] for e in ENGS}
        if not self.reorder:
            t = 0.0
            for op in ops:
                if op.kind == "dma":
                    s = min(self._slots(op.eng), key=lambda i: self.dma_free[i])
                    op.slot = s
                    prev = self.dma_last[s]
                    if prev is not None:
                        op.deps.add(prev)
                    self.dma_last[s] = op
                    self.dma_free[s] = t = t + 1.0
                order[op.eng].append(op)
            return order
        bl = [0.0] * n
        for op in reversed(ops):
            m = 0.0
            for s_ in succs[op.idx]:
                v = bl[s_.idx] + SEM_LAT
                if v > m:
                    m = v
            bl[op.idx] = m + (op.cost if op.kind == "c" else DMA_ISSUE + DMA_LAT + op.nbytes / DMA_BPNS)
        import os
        use_bl = os.environ.get("SCHED_PRIO", "bl") == "bl"
        prio = [(-bl[i] if use_bl else i) for i in range(n)]
        act_set = None
        n_tload = 0
        ready_t = [0.0] * n
        ready_by = [None] * n
        eng_last = {e: None for e in ENGS}
        eng_free = {e: 0.0 for e in ENGS}
        heaps = {e: [] for e in ENGS}
        avail = {e: [] for e in ENGS}
        for op in ops:
            if ndeps[op.idx] == 0:
                heapq.heappush(heaps[op.eng], (0.0, op.idx))
        done_n = 0
        while done_n < n:
            best = None
            for e in ENGS:
                h, a = heaps[e], avail[e]
                ef = eng_free[e]
                while h and h[0][0] <= ef:
                    j_ = heapq.heappop(h)[1]
                    heapq.heappush(a, (prio[j_], j_))
                if a:
                    c = (ef, a[0][0], e, True, a[0][1])
                elif h:
                    c = (h[0][0], prio[h[0][1]], e, False, h[0][1])
                else:
                    continue
                if best is None or c[:2] < best[:2]:
                    best = c
            start, _p, e, from_avail, idx = best
            if from_avail:
                if e == "act":
                    cand = [heapq.heappop(avail[e]) for _ in range(min(ACT_LOOKAHEAD, len(avail[e])))]
                    pick = 0
                    for ci, (_pp, jj) in enumerate(cand):
                        ts_ = ops[jj].tset
                        if ts_ is None or ts_ == act_set or (ts_ == "T" and act_set in ("E", "S", "U")):
                            pick = ci
                            break
                    idx = cand[pick][1]
                    for ci, it_ in enumerate(cand):
                        if ci != pick:
                            heapq.heappush(avail[e], it_)
                else:
                    heapq.heappop(avail[e])
            else:
                heapq.heappop(heaps[e])
            if e == "act":
                ts_ = ops[idx].tset
                if ts_ is not None and not (ts_ == act_set or (ts_ == "T" and act_set in ("E", "S", "U"))):
                    act_set = ts_
                    start += ACT_TABLE_NS
                    n_tload += 1
            op = ops[idx]
            if op.kind == "dma":
                s = min(self._slots(op.eng), key=lambda i: self.dma_free[i])
                start = max(start, self.dma_free[s])
                op.slot = s
                prev = self.dma_last[s]
                if prev is not None:
                    op.deps.add(prev)
                self.dma_last[s] = op
                done = start + DMA_ISSUE + DMA_LAT + op.nbytes / DMA_BPNS
                self.dma_free[s] = done
                eng_free[e] = start + DMA_ISSUE
            else:
                done = start + op.cost
                eng_free[e] = done
            op.done = done
            op.start = start
            op.crit = ready_by[idx] if (ready_t[idx] >= start - 1e-6 and ready_by[idx] is not None) else eng_last[e]
            eng_last[e] = op
            order[e].append(op)
            for s_ in succs[idx]:
                j = s_.idx
                ndeps[j] -= 1
                rt = done + SEM_LAT
                if rt > ready_t[j]:
                    ready_t[j] = rt
                    ready_by[j] = op
                if ndeps[j] == 0:
                    heapq.heappush(heaps[s_.eng], (ready_t[j], j))
            done_n += 1
        self.sim_end = max(eng_free.values())
        self.n_tload = n_tload
        import os
        if os.environ.get("SCHED_DEBUG") and n > 10000:
            last = max(ops, key=lambda o: o.done)
            agg = {}
            cur = last
            while cur is not None:
                prev = cur.crit
                t0 = prev.done if prev is not None else 0.0
                key = (cur.tag.split(":")[-1], cur.eng, "wait" if (prev is not None and prev.eng != cur.eng) else "fifo")
                agg[key] = agg.get(key, 0.0) + (cur.done - t0)
                cur = prev
            tot = sum(agg.values())
            print("CRITICAL PATH total %.1f us" % (tot / 1e3))
            cur = last
            hops = []
            while cur is not None:
                prev = cur.crit
                if prev is not None and prev.tag != cur.tag:
                    hops.append((prev.done / 1e3, prev.tag, prev.eng, cur.tag, cur.eng, cur.why.get(prev, ("fifo", ""))))
                cur = prev
            hops.reverse()
            mid = [x for x in hops if 3000 < x[0] < 3500]
            for x in mid:
                print("   HOP t=%.1f  %s/%s -> %s/%s  %s" % x)
            for k, v in sorted(agg.items(), key=lambda kv: -kv[1])[:40]:
                print("   %-28s %8.1f us" % (str(k), v / 1e3))
        return order

    def _verify(self, order):
        ptr = {e: 0 for e in ENGS}
        done = set()
        total = sum(len(v) for v in order.values())
        n = 0
        while n < total:
            prog = False
            for e in ENGS:
                q = order[e]
                while ptr[e] < len(q):
                    op = q[ptr[e]]
                    if all(d in done for d in op.deps) and all(d in done for d in op.bdeps):
                        done.add(op)
                        ptr[e] += 1
                        n += 1
                        prog = True
                    else:
                        break
            if not prog:
                raise RuntimeError("schedule deadlock: " + str({e: ptr[e] for e in ENGS}))

    def replay(self, barrier=True):
        assert self._pend is None
        order = self._schedule()
        self._verify(order)
        import os
        if os.environ.get("SCHED_DEBUG"):
            busy = {e: sum((o.cost if o.kind == "c" else DMA_ISSUE) for o in order[e]) for e in ENGS}
            sb = {}
            for e in ENGS:
                for o in order[e]:
                    k_ = (o.tag.split(":")[-1], e)
                    sb[k_] = sb.get(k_, 0.0) + (o.cost if o.kind == "c" else DMA_ISSUE)
            if len(self.seq) > 10000:
                print("STAGE BUSY (us/tile):", {k_: round(v / 32e3, 1) for k_, v in sorted(sb.items()) if v / 32e3 > 0.5})
            print("SCHED n=%d sim_end=%.1f us tloads=%d busy(us)=%s" % (len(self.seq), getattr(self, "sim_end", 0) / 1e3, getattr(self, "n_tload", -1),
                  {e: round(v / 1e3, 1) for e, v in busy.items()}), flush=True)
        for e in ENGS:
            c = self.cnt[e]
            for op in order[e]:
                if op.kind == "c":
                    c += 1
                    op.pos = c
            self.cnt[e] = c
        for op in sorted((o for o in self.seq if o.kind == "dma"), key=lambda o: (o.slot, o.done if self.reorder else o.idx)):
            self.dma_val[op.slot] += 16
            op.val = self.dma_val[op.slot]
        sem, dsem, waited = self.sem, self.dma_sems, self.waited
        final_cnt = dict(self.cnt)
        final_dma = list(self.dma_val)

        def body(e_name):
            def run(e):
                wd = waited[e_name]
                for op in order[e_name]:
                    need = {}
                    for d in list(op.deps) + [b for b in op.bdeps if b.eng != e_name]:
                        if d.kind == "dma":
                            k, v = ("dma", d.slot), d.val
                        else:
                            if d.eng == "pe" and e_name == "pe":
                                continue
                            k, v = d.eng, d.pos
                        if need.get(k, 0) < v:
                            need[k] = v
                    for k, v in need.items():
                        if wd.get(k, 0) >= v:
                            continue
                        wd[k] = v
                        e.wait_ge(dsem[k[1]] if isinstance(k, tuple) else sem[k], v)
                    if op.kind == "dma":
                        e.dma_start(out=op.out_ap, in_=op.in_ap, **op.kw).then_inc(dsem[op.slot], 16)
                    else:
                        ins = None
                        for f in op.fns:
                            ins = f(e)
                        ins.then_inc(sem[e_name], 1)
                if barrier:
                    for k, v in final_cnt.items():
                        if k != e_name and v > wd.get(k, 0):
                            wd[k] = v
                            e.wait_ge(sem[k], v)
                    for i, v in enumerate(final_dma):
                        if v > wd.get(("dma", i), 0):
                            wd[("dma", i)] = v
                            e.wait_ge(dsem[i], v)
            return run

        with self.nc.Block() as block:
            block.tensor(body("pe"))
            block.scalar(body("act"))
            block.vector(body("dve"))
            block.gpsimd(body("pool"))
            block.sync(body("sp"))
        self.seq = []
        self.last_w = {}
        self.readers = {}
        self.bank_last = {}
        self.dma_last = [None] * len(self.dma_sems)
        self.dma_free = [0.0] * len(self.dma_sems)

    def barrier(self):
        pass


def _consts():
    c = {}
    s = np.arange(128)[:, None]
    t = np.arange(128)[None, :]
    c["ident"] = np.eye(128, dtype=np.float32)
    up_s = (s < t)
    up_i = (s <= t)
    lo_s = (t < s)
    nm = lambda m: np.where(m, 0.0, NEG).astype(np.float32)
    c["negmask3"] = np.concatenate([nm(up_i), nm(up_s), nm(lo_s)], axis=1)
    f = lambda m: m.astype(np.float32)
    c["mask4"] = np.concatenate([f(up_i), f(up_s), f(up_i), f(up_s)], axis=1)
    c["ltri"] = f(up_i)
    c["lrev"] = f(lo_s)
    c["bd"] = f((s // 64) == (t // 64))
    hsel = np.zeros((1, 2, 128), np.float32)
    hsel[0, 0, 0:64] = 1.0
    hsel[0, 1, 64:128] = 1.0
    c["halfsel"] = hsel.reshape(1, 256)
    hs = np.zeros((128, 4, 8), np.float32)
    for f_ in range(4):
        hs[0:64, f_, 2 * f_] = 1.0
        hs[64:128, f_, 2 * f_ + 1] = 1.0
    c["headsel"] = hs.reshape(128, 32)
    c["ones"] = np.ones((128, 128), np.float32)
    sel8 = np.zeros((8, 8, 128), np.float32)
    for h_ in range(8):
        sel8[h_, h_, :] = 1.0
    c["sel8"] = sel8.reshape(8, 1024)
    selp = np.zeros((8, 4, 128), np.float32)
    for f_ in range(4):
        selp[2 * f_, f_, 0:64] = 1.0
        selp[2 * f_ + 1, f_, 64:128] = 1.0
    c["selp8"] = selp.reshape(8, 512)
    blk = lambda n: f((s // n) == (t // n))
    c["imask"] = np.concatenate([blk(16), blk(32) - blk(16), blk(64) - blk(32), blk(128) - blk(64)], axis=1)
    return c


def _fm(v, nch):
    v = np.asarray(v, np.float32).reshape(-1)
    pad = nch * 128 - v.shape[0]
    if pad:
        v = np.concatenate([v, np.zeros(pad, np.float32)])
    return np.ascontiguousarray(v.reshape(nch, 128).T)


def _bc(v, n=128):
    v = np.asarray(v, np.float32).reshape(1, -1)
    return np.ascontiguousarray(np.broadcast_to(v, (n, v.shape[1])))


C05 = 0.6065306597126334


def build_nc(T, mode="full", taps=None, reorder=True):
    NT = T // 128
    nc = bass.Bass("TRN2", target_bir_lowering=False)
    di = lambda name, shape: nc.dram_tensor(name, list(shape), F32, kind="ExternalInput").ap()
    x_d = di("x", [T, D])
    cT_d = di("cT", [128, 8])
    wada_d = di("w_ada", [D, 6 * D])
    badaT_d = di("badaT", [128, 48])
    bgate_d = di("bgate", [128, 2 * D])
    n1T_d = di("n1T", [128, 8])
    n2T_d = di("n2T", [128, 8])
    nf_d = di("nf_bc", [128, D])
    win_d = di("w_in", [D, D_IN])
    cgT_d = di("cgT", [128, 48])
    gpar_d = di("gpar_bc", [128, 16])
    onorm_d = di("onorm_bc", [128, 64])
    wbg_d = di("wbg", [512, D])
    wbr_d = di("wbr", [512, D])
    wout_d = di("wout", [D, D])
    muT_d = di("muT", [128, 15])
    rwp_d = di("rwp", [128, 20])
    lnx_d = di("lnx_bc", [128, 1024])
    lora_d = di("lora", [128, 512])
    g2_d = di("g2", [160, 512])
    wfi_d = di("w_ffn_in", [D, 2 * D_FF])
    cfT_d = di("cfT", [128, 66])
    wfo_d = di("w_ffn_out", [D_FF, D])
    ident_d = di("ident", [128, 128])
    negmask3_d = di("negmask3", [128, 384])
    mask4_d = di("mask4", [128, 512])
    ltri_d = di("ltri", [128, 128])
    lrev_d = di("lrev", [128, 128])
    bd_d = di("bd", [128, 128])
    halfsel_d = di("halfsel", [1, 256])
    headsel_d = di("headsel", [128, 32])
    ones_d = di("ones", [128, 128])
    imask_d = di("imask", [128, 512])
    sel8_d = di("sel8", [8, 1024])
    selp8_d = di("selp8", [8, 512])
    y_d = nc.dram_tensor("y", [T, D], F32, kind="ExternalOutput").ap()
    x1_d = nc.dram_tensor("x1_scratch", [T, D], F32).ap()
    gscr_d = nc.dram_tensor("gate_scratch", [128, 2 * D], F32).ap()
    dbg_d = nc.dram_tensor("dbg", [128, 16384], F32, kind="ExternalOutput").ap() if taps is not None else None
    tap_off = [0]

    with ExitStack() as es:
        sems = {e: es.enter_context(nc.semaphore("s_" + e)) for e in ENGS}
        dsems = [es.enter_context(nc.semaphore(f"dq{i}")) for i in range(24)]
        P = Prog(nc, sems, dsems, reorder=reorder)

        def sbt(scope, name, shape, dt):
            return scope.enter_context(nc.sbuf_tensor("s_" + name, list(shape), dt))

        PBall = es.enter_context(nc.psum_tensor("pball", [128, 8, 512], F32))
        PB = [PBall[:, i, :] for i in range(8)]
        PBall_b = PBall[:, :, :].rearrange("p b c -> p (b c)").bitcast(BF16).rearrange("p (b c) -> p b c", b=8)

        def pbb(i):
            return PB[i][:, :].bitcast(BF16)

        modT = sbt(es, "modT", [128, 32], F32)
        g1T = sbt(es, "g1T", [128, 8], F32)
        g2T = sbt(es, "g2T", [128, 8], F32)
        ident_f = sbt(es, "ident_f", [128, 128], F32)
        ident_b = sbt(es, "ident_b", [128, 128], BF16)
        small = sbt(es, "small", [128, 16], F32)
        cst = sbt(es, "cst", [128, 4], F32)

        P.dma("sp", ident_f[:], ident_d, writes=["ident_f"])
        P.dma("pool", ident_b[:], ident_d, writes=["ident_b"])
        for i, v in enumerate((1e-6, 1.0, 64e-6, 64e-5)):
            P.emit("pool", lambda e, i=i, v=v: e.memset(cst[:, i:i + 1], v), writes=["cst"])

        def bk_keys(bk):
            return ["pb7", "pb7u", "pb7s"] if bk == 7 else [f"pb{bk}"]

        def fsz(ap):
            n = 1
            for d_ in ap.shape[1:]:
                n *= d_
            return n

        def mm(out, lhsT, rhs, start=True, stop=True, reads=(), writes=(), inc=None):
            if inc is None:
                inc = stop
            return P.emit("pe", lambda e: e.matmul(out, lhsT=lhsT, rhs=rhs, start=start, stop=stop),
                          reads=reads, writes=writes, inc=inc, cost=70.0 + 0.45 * fsz(rhs) * (4.5 if rhs.dtype == F32 else 1))

        def tr(out, in_, idn, reads=(), writes=(), inc=True):
            return P.emit("pe", lambda e: e.transpose(out, in_, idn), reads=reads, writes=writes, inc=inc, cost=120.0)

        def act(out, in_, func, reads=(), writes=(), scale=None, bias=None, accum=None):
            kw = {}
            if scale is not None:
                kw["scale"] = scale
            if bias is not None:
                kw["bias"] = bias
            if accum is not None:
                kw["accum_out"] = accum
            tset = {AF.Exp: "E", AF.Ln: "L", AF.Sqrt: "Q", AF.Silu: "U", AF.Sigmoid: "S", AF.Tanh: "T"}.get(func)
            return P.emit("act", lambda e: e.activation(out=out, in_=in_, func=func, **kw), reads=reads, writes=writes,
                          cost=220.0 + 0.6 * fsz(out), tset=tset)

        def tt(eng, out, in0, in1, op, reads=(), writes=()):
            c = (190.0 + 1.15 * fsz(out)) if eng == "dve" else (400.0 + 2.0 * fsz(out))
            return P.emit(eng, lambda e: e.tensor_tensor(out=out, in0=in0, in1=in1, op=op), reads=reads, writes=writes, cost=c)

        def ts(eng, out, in0, s1, s2, op0, op1=None, reads=(), writes=()):
            c = (190.0 + 1.15 * fsz(out)) if eng == "dve" else (400.0 + 2.0 * fsz(out))
            if op1 is None:
                return P.emit(eng, lambda e: e.tensor_scalar(out=out, in0=in0, scalar1=s1, scalar2=None, op0=op0),
                              reads=reads, writes=writes, cost=c)
            return P.emit(eng, lambda e: e.tensor_scalar(out=out, in0=in0, scalar1=s1, scalar2=s2, op0=op0, op1=op1),
                          reads=reads, writes=writes, cost=c)

        def stt(eng, out, in0, scalar, in1, op0, op1, reads=(), writes=()):
            c = (190.0 + 1.15 * fsz(out)) if eng == "dve" else (400.0 + 2.0 * fsz(out))
            return P.emit(eng, lambda e: e.scalar_tensor_tensor(out=out, in0=in0, scalar=scalar, in1=in1, op0=op0, op1=op1),
                          reads=reads, writes=writes, cost=c)

        def cp(eng, out, in_, reads=(), writes=()):
            if eng == "act":
                return act(out, in_, AF.Copy, reads=reads, writes=writes)
            c = (190.0 + 1.15 * fsz(out)) if eng == "dve" else (400.0 + 2.0 * fsz(out))
            return P.emit(eng, lambda e: e.tensor_copy(out=out, in_=in_), reads=reads, writes=writes, cost=c)

        def recip(out, in_, reads=(), writes=()):
            return P.emit("dve", lambda e: e.reciprocal(out=out, in_=in_), reads=reads, writes=writes, cost=150.0 + 2.5 * fsz(out))

        def red(out, in_, reads=(), writes=()):
            return P.emit("dve", lambda e: e.tensor_reduce(out=out, in_=in_, axis=AX.X, op=ALU.add), reads=reads, writes=writes,
                          cost=150.0 + 1.0 * fsz(in_))

        def tap(name, ap, reads, rows=128):
            if taps is None:
                return
            n = 1
            for d_ in ap.shape[1:]:
                n *= d_
            o = tap_off[0]
            taps[name] = (o, n, rows, tuple(ap.shape))
            dst = dbg_d[0:rows, o:o + n]
            if len(ap.shape) == 3:
                dst = dst.rearrange("p (a b) -> p a b", a=ap.shape[1])
            elif len(ap.shape) == 4:
                dst = dst.rearrange("p (a b c) -> p a b c", a=ap.shape[1], b=ap.shape[2])
            eng = "pool" if ap.dtype != F32 else "sp"
            P.dma(eng, dst, ap, reads=reads)
            tap_off[0] = o + n

        def rms_rstd(xt, xkeys, junk, jkeys, col, n=D):
            act(junk, xt, AF.Square, reads=xkeys, writes=list(jkeys) + [("small", col)], accum=small[:, col:col + 1])
            act(small[:, col + 1:col + 2], small[:, col:col + 1], AF.Sqrt, reads=[("small", col), "cst"],
                writes=[("small", col + 1)], scale=1.0 / n, bias=cst[:, 0:1])
            recip(small[:, col:col + 1], small[:, col + 1:col + 2], reads=[("small", col + 1)], writes=[("small", col)])

        def norm_to_featmajor(xt, xkeys, xn, xnkeys, gT, shT, gkey, dst, dkey, toff, col, banks=(6, 7)):
            rms_rstd(xt, xkeys, xn, xnkeys, col)
            ts("dve", xn, xt, small[:, col:col + 1], None, ALU.mult, reads=list(xkeys) + [("small", col)], writes=xnkeys)
            for half in range(2):
                bk = banks[half]
                for q in range(4):
                    k = half * 4 + q
                    tr(PB[bk][:, q * 128:(q + 1) * 128], xn[:, k * 128:(k + 1) * 128], ident_f[:],
                       reads=list(xnkeys) + ["ident_f"], writes=bk_keys(bk), inc=(q == 3))
                for q in range(4):
                    k = half * 4 + q
                    act(dst[:, k, toff:toff + 128], PB[bk][:, q * 128:(q + 1) * 128], AF.Identity,
                        reads=bk_keys(bk) + [gkey, "modT"], writes=[dkey], scale=gT[:, k:k + 1], bias=shT[:, k:k + 1])

        with ExitStack() as s0:
            stage = [sbt(s0, f"wa{i}", [128, 6 * D], F32) for i in range(4)]
            gate_bc = sbt(s0, "gate_bc", [128, 2 * D], F32)
            cond = sbt(s0, "cond", [128, 8], F32)
            condrep = sbt(s0, "condrep", [128, 8, 128], F32)
            acc = sbt(s0, "acc", [128, 32], F32)
            badaT = sbt(s0, "badaT_sb", [128, 48], F32)
            n12 = sbt(s0, "n12", [128, 16], F32)
            P.dma("sp", cond[:], cT_d, writes=["cond"])
            P.dma("sp", badaT[:], badaT_d, writes=["badaT"])
            P.dma("sp", n12[:, 0:8], n1T_d, writes=["n12"])
            P.dma("sp", n12[:, 8:16], n2T_d, writes=["n12"])
            P.dma("sp", gate_bc[:], bgate_d, writes=["gate_bc"])
            act(cond[:], cond[:], AF.Silu, reads=["cond"], writes=["cond"])
            cp("dve", condrep[:], cond[:].unsqueeze(2).to_broadcast([128, 8, 128]), reads=["cond"], writes=["condrep"])
            pp_cols = [0, 1, 3, 4]
            for k in range(8):
                st = stage[k % 4]
                sk = f"stage{k % 4}"
                P.dma("sp", st[:], wada_d[k * 128:(k + 1) * 128, :], writes=[sk], nbytes=128 * 6 * D * 4)
                for j in range(32):
                    c0 = pp_cols[j // 8] * D + (j % 8) * 128
                    mm(PB[2][:, j:j + 1], st[:, c0:c0 + 128], cond[:, k:k + 1], reads=[sk, "cond"], writes=["pb2"])
                if k == 0:
                    cp("dve", acc[:], PB[2][:, 0:32], reads=["pb2"], writes=["acc"])
                else:
                    tt("dve", acc[:], acc[:], PB[2][:, 0:32], ALU.add, reads=["pb2", "acc"], writes=["acc"])
                for gi, g in enumerate((2, 5)):
                    for hf in range(2):
                        bank = 3 + gi * 2 + hf
                        mm(PB[bank][:, :], condrep[:, k, :], st[:, g * D + hf * 512: g * D + hf * 512 + 512],
                           start=(k == 0), stop=(k == 7), reads=[sk, "condrep"], writes=[f"pb{bank}"], inc=True)
            for gi in range(2):
                for hf in range(2):
                    bank = 3 + gi * 2 + hf
                    sl = gate_bc[:, gi * D + hf * 512: gi * D + hf * 512 + 512]
                    tt("dve", sl, sl, PB[bank][:, :], ALU.add, reads=[f"pb{bank}", "gate_bc"], writes=["gate_bc"])
            P.dma("sp", gscr_d, gate_bc[:], reads=["gate_bc"], writes=["gscr"])
            for gi, g in enumerate(pp_cols):
                tt("dve", modT[:, gi * 8:(gi + 1) * 8], acc[:, gi * 8:(gi + 1) * 8], badaT[:, g * 8:(g + 1) * 8], ALU.add,
                   reads=["acc", "badaT"], writes=["modT"])
            stt("dve", g1T[:], modT[:, 8:16], 1.0, n12[:, 0:8], ALU.add, ALU.mult, reads=["modT", "n12"], writes=["g1T"])
            stt("dve", g2T[:], modT[:, 24:32], 1.0, n12[:, 8:16], ALU.add, ALU.mult, reads=["modT", "n12"], writes=["g2T"])
            P.barrier()
            P.replay()
        sh1T = modT[:, 0:8]
        sh2T = modT[:, 16:24]

        BLKW = 2304
        blocks = []
        bidx = {}

        def _blk(name, ring_, src, c0, ncols, nk):
            bidx[name] = len(blocks)
            blocks.append((ring_, src, c0, ncols, nk))

        for i_, nm in enumerate(("gq", "gk", "gv")):
            _blk(nm + "0", 0, "win", i_ * 512, 256, 8)
            _blk(nm + "1", 0, "win", i_ * 512 + 256, 256, 8)
        _blk("gz0", 0, "win", 1536, 256, 8)
        _blk("gz1", 0, "win", 1792, 272, 8)
        for i_, nm in enumerate(("rr", "rk", "rv")):
            _blk(nm + "0", 1, "win", 2064 + i_ * 512, 256, 8)
            _blk(nm + "1", 1, "win", 2064 + i_ * 512 + 256, 256, 8)
        _blk("rl", 1, "win", 3600, 288, 8)
        for g_ in range(2):
            for hh_ in range(2):
                _blk(f"gla{g_}{hh_}", 2, "win", 3888 + g_ * 512 + hh_ * 256, 256, 8)
            for hh_ in range(2):
                _blk(f"glb{g_}{hh_}", 2, "win", 4912 + g_ * 512 + hh_ * 256, 256, 8)
            for hh_ in range(2):
                _blk(f"wbg{g_}{hh_}", 2, "wbg", g_ * 512 + hh_ * 256, 256, 4)
            for hh_ in range(2):
                _blk(f"wbr{g_}{hh_}", 2, "wbr", g_ * 512 + hh_ * 256, 256, 4)
        for q_ in range(4):
            _blk(f"wo{q_}", 2, "wout", q_ * 256, 256, 8)
        NBLK = len(blocks)
        wsc_d = nc.dram_tensor("w_scratch", [NBLK, 128, BLKW], BF16).ap()
        if mode != "skipA":
            with ExitStack() as sW:
                win_t = sbt(sW, "win_t", [128, 8, D_IN], BF16)
                wbg_t = sbt(sW, "wbg_t", [128, 4, D], BF16)
                wbr_t = sbt(sW, "wbr_t", [128, 4, D], BF16)
                wout_t = sbt(sW, "wout_t", [128, 8, D], BF16)
                g1row = sbt(sW, "g1row", [128, D], F32)
                wst = [sbt(sW, f"wst{i}", [128, D], F32) for i in range(2)]
                for k in range(8):
                    P.dma("pool", win_t[:, k, :], win_d[k * 128:(k + 1) * 128, :], writes=[("win_t", k)], nbytes=128 * D_IN * 4)
                for k in range(4):
                    P.dma("pool", wbg_t[:, k, :], wbg_d[k * 128:(k + 1) * 128, :], writes=["wbg_t"], nbytes=524288)
                    P.dma("pool", wbr_t[:, k, :], wbr_d[k * 128:(k + 1) * 128, :], writes=["wbr_t"], nbytes=524288)
                P.dma("sp", g1row[:], gscr_d[:, 0:D], writes=["g1row"], nbytes=524288)
                for k in range(8):
                    P.dma("sp", wst[k % 2][:], wout_d[k * 128:(k + 1) * 128, :], writes=[f"wst{k % 2}"], nbytes=524288)
                    tt("dve", wout_t[:, k, :], wst[k % 2][:], g1row[:], ALU.mult, reads=[f"wst{k % 2}", "g1row"], writes=["wout_t"])
                srcs = {"win": (win_t, [("win_t", k) for k in range(8)]), "wbg": (wbg_t, ["wbg_t"]),
                        "wbr": (wbr_t, ["wbr_t"]), "wout": (wout_t, ["wout_t"])}
                for b, (ring_, src, c0, ncols, nk) in enumerate(blocks):
                    t_, keys = srcs[src]
                    P.dma("sp", wsc_d[b, :, 0:nk * ncols].rearrange("p (k c) -> p k c", k=nk), t_[:, :, c0:c0 + ncols],
                          reads=keys, writes=[("wsc", b)], nbytes=128 * nk * ncols * 2)
                P.replay()

            with ExitStack() as sA:
                A = lambda name, shape, dt: sbt(sA, name, shape, dt)
                NRS = (3, 3, 3)
                rings = [[A(f"ring{r_}_{i}", [128, BLKW], BF16) for i in range(NRS[r_])] for r_ in range(3)]
                lora = A("lora_sb", [128, 512], BF16)
                g2s = A("g2_sb", [128, 2, 512], BF16)
                negmask3 = A("negmask3", [128, 384], BF16)
                mask4 = A("mask4", [128, 512], BF16)
                imask = A("imask", [128, 512], BF16)
                ltri = A("ltri", [128, 128], F32)
                lrev = A("lrev", [128, 128], F32)
                masksl = A("masksl", [128, 128], BF16)
                bd = A("bd", [128, 128], BF16)
                ones_f = A("ones_f", [128, 128], F32)
                sel8 = A("sel8", [8, 1024], F32)
                selp8 = A("selp8", [8, 512], F32)
                R8 = A("R8", [8, 384], F32)
                headsel = A("headsel", [128, 32], F32)
                cgT = A("cgT_sb", [128, 48], F32)
                gpar = A("gpar", [128, 16], F32)
                onorm = A("onorm_sb", [128, 64], F32)
                muT = A("muT_sb", [128, 15], F32)
                omuT = A("omuT", [128, 15], F32)
                rwp = A("rwp_sb", [128, 20], F32)
                lnx = A("lnx_sb", [128, 1024], F32)
                P.const("negmask3", "mask4", "imask", "ltri", "lrev", "masksl", "bd", "ones_f", "sel8", "selp8", "headsel", "cgT",
                        "onorm", "rwp", "lnx", "lora", "g2s", "ident_f", "ident_b", "cst", "g1T", "modT")
                xa = [A(f"xa{i}", [128, D], F32) for i in range(2)]
                hTs = [A(f"hT{i}", [128, 8, 128], BF16) for i in range(2)]
                xnA = A("xnA", [128, D], F32)
                stgG = A("stgG", [128, 4, 131], F32)
                cvcar = A("cvcar", [128, 12, 3], F32)
                FG = A("FG", [128, 8, 512], F32)
                sqG = A("sqG", [128, 4, 128], BF16)
                qkn_t = A("qkn", [128, 2048], BF16)
                qtl = A("qtl", [128, 512], BF16)
                kdecG = A("kdecG", [128, 512], BF16)
                gcols = A("gcols", [128, 4, 8], F32)
                tokx = A("tokx", [128, 4, 8], F32)
                gsm = A("gsm", [128, 64], F32)
                HBg = A("HBg", [128, 8, 384], BF16)
                xgG = A("xgG", [128, 512], BF16)
                ugG = A("ugG", [128, 512], BF16)
                ytbG = A("ytbG", [128, 512], BF16)
                yaT = A("yaT", [128, 4, 128], BF16)
                Sb = A("Sb", [128, 4, 128], BF16)
                Sf = A("Sf", [128, 4, 128], F32)
                e3 = A("e3", [128, 2, 384], F32)
                stgR = A("stgR", [128, 4, 129], F32)
                pbcar = A("pbcar", [128, 16], F32)
                FR = A("FR", [128, 10, 512], F32)
                sqR = A("sqR", [128, 4, 128], BF16)
                h2 = A("h2", [128, 512], BF16)
                rat_t = A("rat", [128, 1024], BF16)
                btl_t = A("btl", [128, 512], BF16)
                ktl_t = A("ktl", [128, 512], BF16)
                bdtok = A("bdtok", [128, 512], BF16)
                kdtok = A("kdtok", [128, 512], BF16)
                vtk = A("vtk", [128, 512], BF16)
                gsr = A("gsr", [128, 64], F32)
                lorb = A("lorb", [128, 128], BF16)
                sglb = A("sglb", [128, 2, 128], BF16)
                HBr = A("HBr", [128, 8, 640], BF16)
                xgR = A("xgR", [128, 512], BF16)
                ugR = A("ugR", [128, 512], BF16)
                ytbR = A("ytbR", [128, 512], BF16)
                ybT = A("ybT", [128, 4, 128], BF16)
                Mb = A("Mb", [128, 4, 128], BF16)
                Mf = A("Mf", [128, 4, 128], F32)
                WK = A("WK", [128, 8, 1664], BF16)
                mg_t = A("mg", [128, 1024], BF16)
                mgt = A("mgt", [128, 1024], BF16)
                msa = A("msa", [128, 512], BF16)
                msb = A("msb", [128, 512], BF16)
                mm1 = A("mm1", [128, 512], F32)
                mm2 = A("mm2", [128, 512], F32)

                def FGi(i):
                    return FG[:, i, :]

                def FG3(i):
                    return FG[:, i, :].rearrange("p (j t) -> p j t", j=4)

                def FRi(i):
                    return FR[:, i, :]

                def FR3(i):
                    return FR[:, i, :].rearrange("p (j t) -> p j t", j=4)

                GK = lambda i: ("FG", i)
                RK = lambda i: ("FR", i)
                for fi_ in (0, 1, 2, 9):
                    for j_ in range(4):
                        P.alias(("FR", fi_), ("FRc", fi_, j_))
                qkn = qkn_t[:, :].rearrange("p (f a t) -> p f a t", f=4, a=4)
                rat = rat_t[:, :].rearrange("p (f a t) -> p f a t", f=4, a=2)
                btl = btl_t[:, :].rearrange("p (f t) -> p f t", f=4)
                ktl = ktl_t[:, :].rearrange("p (f t) -> p f t", f=4)
                mg = mg_t[:, :].rearrange("p (k t) -> p k t", k=8)

                P.dma("pool", lora[:], lora_d, writes=["lora"])
                P.dma("pool", g2s[:, 0, :], g2_d[0:128, :], writes=["g2s"])
                P.dma("pool", g2s[0:32, 1, :], g2_d[128:160, :], writes=["g2s"])
                P.dma("pool", negmask3[:], negmask3_d, writes=["negmask3"])
                P.dma("pool", mask4[:], mask4_d, writes=["mask4"])
                P.dma("pool", masksl[:], lrev_d, writes=["masksl"])
                P.dma("pool", bd[:], bd_d, writes=["bd"])
                P.dma("pool", imask[:], imask_d, writes=["imask"])
                for t_, d_, k_ in ((ltri, ltri_d, "ltri"), (lrev, lrev_d, "lrev"), (ones_f, ones_d, "ones_f"), (sel8, sel8_d, "sel8"), (selp8, selp8_d, "selp8"),
                                   (headsel, headsel_d, "headsel"), (cgT, cgT_d, "cgT"), (gpar, gpar_d, "gpar"), (onorm, onorm_d, "onorm"),
                                   (muT, muT_d, "muT"), (rwp, rwp_d, "rwp"), (lnx, lnx_d, "lnx")):
                    P.dma("sp", t_[:], d_, writes=[k_])
                ts("dve", omuT[:], muT[:], -1.0, 1.0, ALU.mult, ALU.add, reads=["muT"], writes=["omuT"])
                act(gpar[:, 0:8], gpar[:, 0:8], AF.Exp, reads=["gpar"], writes=["gpar"])
                ts("dve", gpar[:, 0:8], gpar[:, 0:8], -1.0, None, ALU.mult, reads=["gpar"], writes=["gpar"])
                for t_, k_ in ((Sf, "Sf"), (Mf, "Mf"), (cvcar, "cvcar"), (pbcar, "pbcar"), (Sb, "Sb"), (Mb, "Mb")):
                    P.emit("pool", lambda e, t_=t_: e.memset(t_[:], 0.0), writes=[k_])

                blk_ctr = [0, 0, 0]

                def wblk(name):
                    b = bidx[name]
                    r_, src, c0, ncols, nk = blocks[b]
                    s_ = blk_ctr[r_] % NRS[r_]
                    blk_ctr[r_] += 1
                    key = ("ring", r_, s_)
                    t_ = rings[r_][s_]
                    P.dma("sp", t_[:, 0:nk * ncols], wsc_d[b, :, 0:nk * ncols], writes=[key], nbytes=128 * nk * ncols * 2)
                    return t_[:, 0:nk * ncols].rearrange("p (k c) -> p k c", k=nk), key

                def proj_fm(bank, wv, wkey, j0, nch, hT, hk, widths=None):
                    for j in range(nch):
                        w_ = 128 if widths is None else widths[j]
                        for k in range(8):
                            mm(PB[bank][0:w_, (j0 + j) * 128:(j0 + j + 1) * 128], wv[:, k, j * 128: j * 128 + w_], hT[:, k, :],
                               start=(k == 0), stop=(k == 7), reads=[wkey, hk], writes=[f"pb{bank}"])

                def invert(HBt, hbname, heads, mo):
                    Q = (640, 1024)
                    groups = [heads[i:i + 2] for i in range(0, len(heads), 2)]
                    def G(h0, a, b_):
                        return WK[:, h0:h0 + 2, a:b_]

                    def b0_of(h0):
                        return 3 + 2 * ((h0 // 2) % 2)

                    for hs in groups:
                        h0 = hs[0]
                        hks = [(hbname, h) for h in hs]
                        wks = [("WK", h) for h in hs]
                        tt("dve", G(h0, 0, 128), HBt[:, h0:h0 + 2, mo:mo + 128], imask[:, 0:128].unsqueeze(1).to_broadcast([128, 2, 128]),
                           ALU.mult, reads=hks + ["imask"], writes=wks)
                        tt("dve", G(h0, 128, 640).rearrange("p h (a t) -> p h a t", a=4),
                           HBt[:, h0:h0 + 2, mo + 128:mo + 256].unsqueeze(2).to_broadcast([128, 2, 4, 128]),
                           imask[:].rearrange("p (a t) -> p a t", a=4).unsqueeze(1).to_broadcast([128, 2, 4, 128]),
                           ALU.mult, reads=hks + ["imask"], writes=wks)
                    for lvl in range(4):
                        for hs in groups:
                            h0 = hs[0]
                            b0 = b0_of(h0)
                            bks = [f"pb{b0}", f"pb{b0 + 1}"]
                            wks = [("WK", h) for h in hs]
                            do = Q[(lvl + 1) % 2]
                            so = Q[lvl % 2]
                            for i, h in enumerate(hs):
                                W = WK[:, h, :]
                                bank = b0 + i
                                if lvl == 0:
                                    Mp, MpT, Pm = W[:, 0:128], W[:, 128:256], None
                                else:
                                    Mp, MpT, Pm = W[:, so:so + 128], W[:, so + 128:so + 256], W[:, so + 256:so + 384]
                                last = (i == len(hs) - 1)
                                if lvl < 3:
                                    mm(PB[bank][:, 0:128], MpT, Mp, reads=[("WK", h)], writes=[bks[i]], inc=False)
                                    mm(PB[bank][:, 128:256], Mp, MpT, reads=[("WK", h)], writes=[bks[i]], inc=(last and lvl == 0))
                                if lvl > 0:
                                    mm(PB[bank][:, 256:384], MpT, Pm, reads=[("WK", h)], writes=[bks[i]], inc=last)
                            if lvl < 3:
                                cp("act", G(h0, do, do + 256), PBall[:, b0:b0 + 2, 0:256], reads=bks, writes=wks)
                            if lvl == 0:
                                tt("pool", G(h0, do + 256, do + 384), G(h0, 0, 128), ident_b[:].unsqueeze(1).to_broadcast([128, 2, 128]), ALU.add,
                                   reads=wks + ["ident_b"], writes=wks)
                            else:
                                tt("dve", G(h0, do + 256, do + 384), G(h0, so + 256, so + 384), PBall[:, b0:b0 + 2, 256:384], ALU.add, reads=bks + wks, writes=wks)
                    zo = Q[0] + 256
                    for l in range(1, 4):
                        for hs in groups:
                            h0 = hs[0]
                            b0 = b0_of(h0)
                            bks = [f"pb{b0}", f"pb{b0 + 1}"]
                            hks = [(hbname, h) for h in hs]
                            wks = [("WK", h) for h in hs]
                            for i, h in enumerate(hs):
                                tr(pbb(b0 + i)[:, 768:896], WK[:, h, zo:zo + 128], ident_b[:], reads=[("WK", h), "ident_b"], writes=[bks[i]], inc=(i == 1))
                            cp("act", G(h0, 1536, 1664), PBall_b[:, b0:b0 + 2, 768:896], reads=bks, writes=wks)
                            for i, h in enumerate(hs):
                                mm(PB[b0 + i][:, 0:128], WK[:, h, 128 + l * 128:256 + l * 128], WK[:, h, zo:zo + 128], reads=[("WK", h)], writes=[bks[i]], inc=(i == 1))
                            cp("act", G(h0, 1408, 1536), PBall[:, b0:b0 + 2, 0:128], reads=bks, writes=wks)
                            for i, h in enumerate(hs):
                                mm(PB[b0 + i][:, 128:256], WK[:, h, 1536:1664], WK[:, h, 1408:1536], reads=[("WK", h)], writes=[bks[i]], inc=(i == 1))
                            if l < 3:
                                zn = Q[l % 2] + 256
                                tt("dve", G(h0, zn, zn + 128), G(h0, zo, zo + 128), PBall[:, b0:b0 + 2, 128:256], ALU.add, reads=bks + wks, writes=wks)
                            else:
                                tt("dve", HBt[:, h0:h0 + 2, mo:mo + 128], G(h0, zo, zo + 128), PBall[:, b0:b0 + 2, 128:256], ALU.add,
                                   reads=bks + wks + hks, writes=hks)
                        if l < 3:
                            zo = Q[l % 2] + 256

                def bc8(col_ap):
                    return col_ap.unsqueeze(2).to_broadcast([128, 8, 64])

                def h8(ap):
                    return ap.rearrange("p (h d) -> p h d", h=8)

                cva, rn, zs, vb, otok, tmpf, egc, tmp2 = range(8)
                rT, kT_, vT, sg, asig, gtok, kk, T1, cum, lg = range(10)
                ytk, yc = rT, kT_
                widths_all = [128] * 14 + [32]
                lor = FR3(lg)[:, 0, :]
                glo = FR3(lg)[:, 1:3, :]
                ck = lambda fi: [("FRc", fi, j) for j in range(4)]
                lgk = ck(lg)[0:3]
                stages = {}


                def st_norm(it):
                    r0 = it * 128
                    par = it % 2
                    dbg_tile = (taps is not None and it == (1 if NT > 1 else 0))
                    xat, xk = xa[par], f"xa{par}"
                    hT, hk_ = hTs[par], f"hT{par}"
                    P.tag = "%d:norm" % it
                    P.dma("sp", xat[:], x_d[r0:r0 + 128, :], writes=[xk], nbytes=524288)
                    norm_to_featmajor(xat[:], [xk], xnA[:], ["xnA"], g1T, sh1T, "g1T", hT, hk_, 0, 4 + 4 * par, banks=(2, 2))


                def st_G_prep(it):
                    r0 = it * 128
                    par = it % 2
                    dbg_tile = (taps is not None and it == (1 if NT > 1 else 0))
                    xat, xk = xa[par], f"xa{par}"
                    hT, hk_ = hTs[par], f"hT{par}"
                    P.tag = "%d:G.prep" % it
                    wv, wkey = wblk("gz0")
                    for k in range(8):
                        mm(PB[0][:, 0:256], hT[:, k, :], wv[:, k, 0:256], start=(k == 0), stop=(k == 7), reads=[hk_, wkey], writes=["pb0"])
                    wv, wkey = wblk("gz1")
                    for k in range(8):
                        mm(PB[0][:, 256:512], hT[:, k, :], wv[:, k, 0:256], start=(k == 0), stop=(k == 7), reads=[hk_, wkey], writes=["pb0"])
                    for k in range(8):
                        mm(PB[1][:, 0:16], hT[:, k, :], wv[:, k, 256:272], start=(k == 0), stop=(k == 7), reads=[hk_, wkey], writes=["pb1"])
                    act(FGi(zs), PB[0][:, :], AF.Silu, reads=["pb0"], writes=[GK(zs)])
                    tt("dve", h8(FGi(zs)), h8(FGi(zs)), onorm[:].unsqueeze(1).to_broadcast([128, 8, 64]), ALU.mult, reads=[GK(zs), "onorm"], writes=[GK(zs)])
                    ts("dve", gsm[:, 0:8], PB[1][:, 0:8], -1.0, None, ALU.mult, reads=["pb1"], writes=["gsm_a"])
                    tt("dve", gsm[:, 8:16], PB[1][:, 8:16], gpar[:, 8:16], ALU.add, reads=["pb1", "gpar"], writes=["gsm_a"])
                    act(gsm[:, 0:16], gsm[:, 0:16], AF.Exp, reads=["gsm_a"], writes=["gsm_a"])
                    act(gsm[:, 0:16], gsm[:, 0:16], AF.Ln, reads=["gsm_a", "cst"], writes=["gsm_a"], bias=cst[:, 1:2])
                    ts("dve", gcols[:, 0, :], gsm[:, 0:8], -1.0, None, ALU.mult, reads=["gsm_a"], writes=["gcols"])
                    tt("dve", gsm[:, 16:24], gsm[:, 8:16], gpar[:, 0:8], ALU.mult, reads=["gsm_a", "gpar"], writes=["gsm_g"])
                    mm(PB[1][:, 16:24], ltri[:], gsm[:, 16:24], reads=["ltri", "gsm_g"], writes=["pb1"])
                    mm(PB[1][:, 24:32], lrev[:], gsm[:, 16:24], reads=["lrev", "gsm_g"], writes=["pb1"])
                    cp("dve", gcols[:, 2:4, :], PB[1][:, 16:32].rearrange("p (a h) -> p a h", a=2), reads=["pb1"], writes=["gcols"])
                    tt("dve", gcols[:, 1, :], gcols[:, 0, :], gcols[:, 2, :], ALU.add, reads=["gcols"], writes=["gcols"])
                    act(tokx[:], gcols[:], AF.Exp, reads=["gcols"], writes=["tokx"])
                    ts("dve", gsm[:, 24:32], tokx[:, 1, :], -1.0, None, ALU.mult, reads=["tokx"], writes=["gsm_c"])
                    ts("dve", gsm[:, 32:40], gcols[:, 2, :], -1.0, None, ALU.mult, reads=["gcols"], writes=["gsm_c"])
                    for gi in range(3):
                        bank = gi % 2
                        for hh_ in range(2):
                            wv, wkey = wblk(("gq", "gk", "gv")[gi] + str(hh_))
                            proj_fm(bank, wv, wkey, 2 * hh_, 2, hT, hk_)
                        cp("pool", stgG[:, :, 0:3], cvcar[:, gi * 4:(gi + 1) * 4, :], reads=["cvcar"], writes=["stgG"])
                        cp("act", stgG[:, :, 3:131], PB[bank][:, :].rearrange("p (j t) -> p j t", j=4), reads=[f"pb{bank}"], writes=["stgG"])
                        cp("pool", cvcar[:, gi * 4:(gi + 1) * 4, :], stgG[:, :, 128:131], reads=["stgG"], writes=["cvcar"])
                        for j in range(4):
                            ch = gi * 4 + j
                            o_ = FG3(cva)[:, j, :]
                            ce_ = "dve"
                            ts(ce_, o_, stgG[:, j, 0:128], cgT[:, ch * 4:ch * 4 + 1], None, ALU.mult, reads=["stgG", "cgT"], writes=[("cva", j)])
                            for tp in range(1, 4):
                                stt(ce_, o_, stgG[:, j, tp:tp + 128], cgT[:, ch * 4 + tp:ch * 4 + tp + 1], o_, ALU.mult, ALU.add,
                                    reads=["stgG", "cgT", ("cva", j)], writes=[("cva", j)])
                        cvk = [("cva", j) for j in range(4)]
                        act(FGi(cva), FGi(cva), AF.Silu, reads=cvk, writes=cvk)
                        if gi < 2:
                            act(sqG[:], FG3(cva), AF.Square, reads=cvk, writes=["sqG"])
                            for j in range(4):
                                mm(PB[2][:, j * 128:(j + 1) * 128], bd[:], sqG[:, j, :], reads=["bd", "sqG"], writes=["pb2"])
                            if gi == 0:
                                act(FGi(rn), PB[2][:, :], AF.Sqrt, reads=["pb2", "cst"], writes=[GK(rn)], scale=64.0, bias=cst[:, 2:3])
                            else:
                                act(FGi(rn), PB[2][:, :], AF.Sqrt, reads=["pb2", "cst"], writes=[GK(rn)], scale=1.0, bias=cst[:, 0:1])
                            recip(FGi(rn), FGi(rn), reads=[GK(rn)], writes=[GK(rn)])
                            if gi == 0:
                                tt("dve", qkn[:, :, 0, :], FG3(cva), FG3(rn), ALU.mult, reads=cvk + [GK(rn)], writes=["qkn_q"])
                            else:
                                tt("dve", qkn[:, :, 3, :], FG3(cva), FG3(rn), ALU.mult, reads=cvk + [GK(rn)], writes=["qkn_k"])
                                for a_ in (1, 2):
                                    stt("dve", qkn[:, :, a_, :], FG3(cva), -1.0, FG3(rn), ALU.mult, ALU.mult, reads=cvk + [GK(rn)], writes=[("qkn_nk", a_)])
                                for j in range(4):
                                    tr(pbb(2)[:, j * 128:(j + 1) * 128], qkn[:, j, 3, :], ident_b[:], reads=["qkn_k", "ident_b"],
                                       writes=["pb2"], inc=(j == 3))
                                tt("dve", h8(kdecG[:]), h8(pbb(2)[:, 0:512]), bc8(tokx[:, 3, :]), ALU.mult, reads=["pb2", "tokx"], writes=["kdecG"])
                        else:
                            for j in range(4):
                                tr(PB[2][:, j * 128:(j + 1) * 128], FG3(cva)[:, j, :], ident_f[:], reads=cvk + ["ident_f"],
                                   writes=["pb2"], inc=(j == 3))
                            tt("dve", h8(FGi(vb)), h8(PB[2][:, :]), bc8(tokx[:, 0, :]), ALU.mult, reads=["pb2", "tokx"], writes=[GK(vb)])


                def st_G_S1(it):
                    r0 = it * 128
                    par = it % 2
                    dbg_tile = (taps is not None and it == (1 if NT > 1 else 0))
                    xat, xk = xa[par], f"xa{par}"
                    hT, hk_ = hTs[par], f"hT{par}"
                    P.tag = "%d:G.S1" % it
                    mm(PB[7][0:8, 0:128], gcols[:, 2, :], ident_f[:], reads=["gcols", "ident_f"], writes=["pb7"], inc=False)
                    mm(PB[7][0:8, 128:256], gcols[:, 1, :], ident_f[:], reads=["gcols", "ident_f"], writes=["pb7"], inc=False)
                    mm(PB[7][0:8, 256:384], gsm[:, 32:40], ident_f[:], reads=["gsm_c", "ident_f"], writes=["pb7"])
                    cp("act", R8[:, :], PB[7][0:8, 0:384], reads=["pb7"], writes=["R8"])
                    for f in range(4):
                        for hp in range(2):
                            h = 2 * f + hp
                            hk = ("HBg", h)
                            q4 = h % 2
                            ps_ = slice(hp * 64, (hp + 1) * 64)
                            bnk = 4 + h % 3
                            mm(PB[3][:, 0:384], qkn[ps_, f, 3, :], qkn[ps_, f, 0:3, :], reads=["qkn_q", "qkn_k", ("qkn_nk", 1), ("qkn_nk", 2)], writes=["pb3"])
                            mm(PB[bnk][:, 0:384], sel8[:, h * 128:(h + 1) * 128], R8[:, :], start=True, stop=False, reads=["sel8", "R8"], writes=[f"pb{bnk}"])
                            mm(PB[bnk][:, 0:384], ident_b[:], negmask3[:], start=False, stop=True, reads=["ident_b", "negmask3"], writes=[f"pb{bnk}"])
                            act(e3[:, q4, 0:256], PB[bnk][:, 0:256], AF.Exp, reads=[f"pb{bnk}", "gsm_c"], writes=[("e3", q4)], bias=gsm[:, 32 + h:33 + h])
                            act(e3[:, q4, 256:384], PB[bnk][:, 256:384], AF.Exp, reads=[f"pb{bnk}", "gcols"], writes=[("e3", q4)], bias=gcols[:, 1, h:h + 1])
                            tt("dve", HBg[:, h, 0:384], PB[3][:, 0:384], e3[:, q4, :], ALU.mult, reads=["pb3", ("e3", q4)], writes=[hk])
                        mm(PB[7][:, 384:512], selp8[:, f * 128:(f + 1) * 128], R8[:, 0:128], reads=["selp8", "R8"], writes=["pb7"])
                        act(FG3(egc)[:, f, :], PB[7][:, 384:512], AF.Exp, reads=["pb7"], writes=[("egc", f)])
                        tt("dve", qtl[:, f * 128:(f + 1) * 128], qkn[:, f, 0, :], FG3(egc)[:, f, :], ALU.mult, reads=["qkn_q", ("egc", f)], writes=[("qtl", f)])

                def st_G_S2(it):
                    r0 = it * 128
                    par = it % 2
                    dbg_tile = (taps is not None and it == (1 if NT > 1 else 0))
                    xat, xk = xa[par], f"xa{par}"
                    hT, hk_ = hTs[par], f"hT{par}"
                    P.tag = "%d:G.S2" % it
                    invert(HBg, "HBg", list(range(8)), 128)

                def st_G_S3(it):
                    r0 = it * 128
                    par = it % 2
                    dbg_tile = (taps is not None and it == (1 if NT > 1 else 0))
                    xat, xk = xa[par], f"xa{par}"
                    hT, hk_ = hTs[par], f"hT{par}"
                    P.tag = "%d:G.S3" % it
                    for f in range(4):
                        mm(PB[7][:, f * 128:(f + 1) * 128], qkn[:, f, 3, :], Sb[:, f, :], reads=["qkn_k", "Sb"], writes=["pb7"], inc=(f == 3))
                    tt("dve", h8(FGi(tmp2)), h8(PB[7][:, :]), bc8(gsm[:, 24:32]), ALU.mult, reads=["pb7", "gsm_c"], writes=[GK(tmp2)])
                    tt("dve", xgG[:], FGi(tmp2), FGi(vb), ALU.add, reads=[GK(tmp2), GK(vb)], writes=["xgG"])
                    for h in range(8):
                        mm(PB[3][:, h * 64:(h + 1) * 64], HBg[:, h, 128:256], xgG[:, h * 64:(h + 1) * 64], reads=[("HBg", h), "xgG"], writes=["pb3"], inc=(h == 7))
                    cp("act", ugG[:], PB[3][:, :], reads=["pb3"], writes=["ugG"])
                    for f in range(4):
                        mm(PB[3][:, f * 128:(f + 1) * 128], qtl[:, f * 128:(f + 1) * 128], Sb[:, f, :], start=True, stop=False,
                           reads=[("qtl", f), "Sb"], writes=["pb3"], inc=False)
                        for hp in range(2):
                            h = 2 * f + hp
                            mm(PB[3][:, h * 64:(h + 1) * 64], HBg[:, h, 0:128], ugG[:, h * 64:(h + 1) * 64], start=False, stop=(hp == 1),
                               reads=[("HBg", h), "ugG"], writes=["pb3"], inc=(h == 7))
                    for f in range(4):
                        mm(PB[7][:, f * 128:(f + 1) * 128], kdecG[:, f * 128:(f + 1) * 128], ugG[:, f * 128:(f + 1) * 128],
                           reads=["kdecG", "ugG"], writes=["pb7"], inc=(f == 3))
                    for f in range(4):
                        for hp in range(2):
                            ps_ = slice(hp * 64, (hp + 1) * 64)
                            stt("dve", Sf[ps_, f, ps_], Sf[ps_, f, ps_], FG3(egc)[ps_, f, 127:128], PB[7][ps_, f * 128 + hp * 64:f * 128 + (hp + 1) * 64],
                                ALU.mult, ALU.add, reads=["pb7", ("egc", f), "Sf"], writes=["Sf"])
                    cp("pool", Sb[:], Sf[:], reads=["Sf"], writes=["Sb"])

                def st_G_epi(it):
                    r0 = it * 128
                    par = it % 2
                    dbg_tile = (taps is not None and it == (1 if NT > 1 else 0))
                    xat, xk = xa[par], f"xa{par}"
                    hT, hk_ = hTs[par], f"hT{par}"
                    P.tag = "%d:G.epi" % it
                    cp("act", FGi(otok), PB[3][:, :], reads=["pb3"], writes=[GK(otok)])
                    if dbg_tile:
                        tap("otok", FGi(otok), [GK(otok)])
                    tt("dve", FGi(tmpf), FGi(otok), FGi(otok), ALU.mult, reads=[GK(otok)], writes=[GK(tmpf)])
                    red(gsm[:, 40:48], h8(FGi(tmpf)), reads=[GK(tmpf)], writes=["gsm_e"])
                    act(gsm[:, 40:48], gsm[:, 40:48], AF.Sqrt, reads=["gsm_e", "cst"], writes=["gsm_e"], scale=1.0 / 64, bias=cst[:, 0:1])
                    recip(gsm[:, 40:48], gsm[:, 40:48], reads=["gsm_e"], writes=["gsm_e"])
                    tt("dve", h8(FGi(tmpf)), h8(FGi(otok)), bc8(gsm[:, 40:48]), ALU.mult, reads=[GK(otok), "gsm_e"], writes=[GK(tmpf)])
                    tt("dve", ytbG[:], FGi(tmpf), FGi(zs), ALU.mult, reads=[GK(tmpf), GK(zs)], writes=["ytbG"])
                    for j in range(4):
                        tr(pbb(7)[:, j * 128:(j + 1) * 128], ytbG[:, j * 128:(j + 1) * 128], ident_b[:], reads=["ytbG", "ident_b"],
                           writes=["pb7"], inc=(j == 3))
                    cp("act", yaT[:], pbb(7)[:, 0:512].rearrange("p (j t) -> p j t", j=4), reads=["pb7"], writes=["yaT"])


                def st_R_prep(it):
                    r0 = it * 128
                    par = it % 2
                    dbg_tile = (taps is not None and it == (1 if NT > 1 else 0))
                    xat, xk = xa[par], f"xa{par}"
                    hT, hk_ = hTs[par], f"hT{par}"
                    P.tag = "%d:R.prep" % it
                    for gi in range(4):
                        nch = 4 if gi < 3 else 3
                        bank = gi % 2
                        ws_ = widths_all[gi * 4: gi * 4 + nch]
                        if gi < 3:
                            for hh_ in range(2):
                                wv, wkey = wblk(("rr", "rk", "rv")[gi] + str(hh_))
                                proj_fm(bank, wv, wkey, 2 * hh_, 2, hT, hk_)
                        else:
                            wv, wkey = wblk("rl")
                            proj_fm(bank, wv, wkey, 0, nch, hT, hk_, ws_)
                        cp("pool", stgR[:, 0:nch, 0:1], pbcar[:, gi * 4:gi * 4 + nch].unsqueeze(2), reads=["pbcar"], writes=["stgR"])
                        if gi < 3:
                            cp("act", stgR[:, 0:nch, 1:129], PB[bank][:, 0:nch * 128].rearrange("p (j t) -> p j t", j=nch),
                               reads=[f"pb{bank}"], writes=["stgR"])
                        else:
                            cp("act", stgR[:, 0:2, 1:129], PB[bank][:, 0:256].rearrange("p (j t) -> p j t", j=2), reads=[f"pb{bank}"], writes=["stgR"])
                            cp("act", stgR[0:32, 2, 1:129], PB[bank][0:32, 256:384], reads=[f"pb{bank}"], writes=["stgR"])
                        cp("pool", pbcar[:, gi * 4:gi * 4 + nch].unsqueeze(2), stgR[:, 0:nch, 128:129], reads=["stgR"], writes=["pbcar"])
                        for j in range(nch):
                            ch = gi * 4 + j
                            w_ = ws_[j]
                            fi = gi if gi < 3 else lg
                            dst = FR3(fi)[0:w_, j, :]
                            dk = ("FRc", fi, j)
                            se_ = "dve"
                            ts(se_, dst, stgR[0:w_, j, 1:129], omuT[0:w_, ch:ch + 1], None, ALU.mult, reads=["stgR", "omuT"], writes=[dk])
                            stt(se_, dst, stgR[0:w_, j, 0:128], muT[0:w_, ch:ch + 1], dst, ALU.mult, ALU.add, reads=["stgR", "muT", dk], writes=[dk])
                    for j in range(4):
                        tr(PB[2][:, j * 128:(j + 1) * 128], FR3(vT)[:, j, :], ident_f[:], reads=ck(vT) + ["ident_f"], writes=["pb2"], inc=(j == 3))
                    cp("act", vtk[:], PB[2][:, :], reads=["pb2"], writes=["vtk"])
                    cp("dve", FRi(vT), PB[2][:, :], reads=["pb2"], writes=ck(vT) + [RK(vT)])
                    act(lorb[0:64, :], lor[0:64, :], AF.Tanh, reads=lgk, writes=["lorb"])
                    cp("dve", lorb[64:128, :], lor[64:128, :], reads=lgk, writes=["lorb"])
                    act(sglb[:, 0, :], glo[:, 0, :], AF.Sigmoid, reads=lgk, writes=["sglb"])
                    act(sglb[0:32, 1, :], glo[0:32, 1, :], AF.Sigmoid, reads=lgk, writes=["sglb"])
                    for j in range(4):
                        mm(PB[0][:, j * 128:(j + 1) * 128], lora[0:64, j * 128:(j + 1) * 128], lorb[0:64, :], reads=["lora", "lorb"], writes=["pb0"])
                    for j in range(4):
                        mm(PB[1][:, j * 128:(j + 1) * 128], lora[64:128, j * 128:(j + 1) * 128], lorb[64:128, :], reads=["lora", "lorb"], writes=["pb1"])
                    for j in range(4):
                        act(FR3(sg)[:, j, :], PB[0][:, j * 128:(j + 1) * 128], AF.Sigmoid, reads=["pb0", "rwp"], writes=[RK(sg)], bias=rwp[:, j:j + 1])
                        act(FR3(asig)[:, j, :], PB[1][:, j * 128:(j + 1) * 128], AF.Sigmoid, reads=["pb1", "rwp"], writes=[RK(asig)], bias=rwp[:, 4 + j:5 + j])
                    mm(PB[2][:, :], sglb[:, 0, :], g2s[:, 0, :], start=True, stop=False, reads=["sglb", "g2s"], writes=["pb2"])
                    mm(PB[2][:, :], sglb[0:32, 1, :], g2s[0:32, 1, :], start=False, stop=True, reads=["sglb", "g2s"], writes=["pb2"])
                    cp("act", FRi(gtok), PB[2][:, :], reads=["pb2"], writes=[RK(gtok)])
                    for j in range(4):
                        ts("dve", FR3(kk)[:, j, :], FR3(kT_)[:, j, :], rwp[:, 8 + j:9 + j], None, ALU.mult, reads=ck(kT_) + ["rwp"], writes=[RK(kk)])
                    act(sqR[:], FR3(kk), AF.Square, reads=[RK(kk)], writes=["sqR"])
                    for j in range(4):
                        mm(PB[2][:, j * 128:(j + 1) * 128], bd[:], sqR[:, j, :], reads=["bd", "sqR"], writes=["pb2"])
                    act(FRi(T1), PB[2][:, :], AF.Sqrt, reads=["pb2", "cst"], writes=[RK(T1)], scale=1.0, bias=cst[:, 0:1])
                    recip(FRi(T1), FRi(T1), reads=[RK(T1)], writes=[RK(T1)])
                    tt("dve", FRi(kk), FRi(kk), FRi(T1), ALU.mult, reads=[RK(kk), RK(T1)], writes=[RK(kk)])
                    for j in range(4):
                        ts("dve", FR3(T1)[:, j, :], FR3(asig)[:, j, :], -1.0, rwp[:, 12 + j:13 + j], ALU.add, ALU.mult, reads=[RK(asig), "rwp"], writes=[RK(T1)])
                    stt("dve", FRi(kT_), FRi(T1), 1.0, FRi(kT_), ALU.add, ALU.mult, reads=[RK(T1)] + ck(kT_), writes=ck(kT_) + [RK(kT_)])
                    tt("dve", FRi(asig), FRi(asig), FRi(kk), ALU.mult, reads=[RK(asig), RK(kk)], writes=[RK(asig)])
                    for j in range(4):
                        ts("dve", FR3(T1)[:, j, :], FR3(rT)[:, j, :], rwp[:, 16 + j:17 + j], None, ALU.mult, reads=ck(rT) + ["rwp"], writes=[RK(T1)])
                    tt("dve", FRi(T1), FRi(T1), FRi(kT_), ALU.mult, reads=[RK(T1), RK(kT_)], writes=[RK(T1)])
                    for j in range(4):
                        mm(PB[2][:, 0:8], FR3(T1)[:, j, :], headsel[:, j * 8:(j + 1) * 8], start=(j == 0), stop=(j == 3), reads=[RK(T1), "headsel"], writes=["pb2"])
                    cp("act", gsr[:, 48:56], PB[2][:, 0:8], reads=["pb2"], writes=["gsr_b"])
                    for j in range(4):
                        P.emit("dve", lambda e, j=j: e.tensor_tensor_scan(out=FR3(cum)[:, j, :], data0=ones_f[:], data1=FR3(sg)[:, j, :], initial=0.0,
                                                                           op0=ALU.mult, op1=ALU.add), reads=["ones_f", RK(sg)], writes=[RK(cum)], cost=420.0)
                    act(FRi(T1), FRi(cum), AF.Exp, reads=[RK(cum)], writes=[RK(T1)], scale=-C05)
                    tt("dve", rat[:, :, 0, :], FR3(rT), FR3(T1), ALU.mult, reads=ck(rT) + [RK(T1)], writes=["rat_r"])
                    tt("dve", FRi(rT), FRi(cum), FRi(sg), ALU.subtract, reads=[RK(cum), RK(sg)], writes=ck(rT) + [RK(rT)])
                    act(FRi(rT), FRi(rT), AF.Exp, reads=[RK(rT)], writes=[RK(rT)], scale=-C05)
                    stt("dve", rat[:, :, 1, :], FR3(kk), -1.0, FR3(rT), ALU.mult, ALU.mult, reads=[RK(kk), RK(rT)], writes=["rat_a"])
                    act(FRi(T1), FRi(cum), AF.Exp, reads=[RK(cum)], writes=[RK(T1)], scale=C05)
                    tt("dve", btl, FR3(asig), FR3(T1), ALU.mult, reads=[RK(asig), RK(T1)], writes=["btl"])
                    tt("dve", ktl, FR3(kT_), FR3(T1), ALU.mult, reads=[RK(kT_), RK(T1)], writes=["ktl"])
                    ts("dve", gsr[:, 56:60], FR3(cum)[:, :, 127], -C05, None, ALU.mult, reads=[RK(cum)], writes=["gsr_c"])
                    for j in range(4):
                        act(FR3(T1)[:, j, :], FR3(cum)[:, j, :], AF.Exp, reads=[RK(cum), "gsr_c"], writes=[RK(T1)], scale=C05, bias=gsr[:, 56 + j:57 + j])
                    act(gsr[:, 60:64], gsr[:, 56:60], AF.Exp, reads=["gsr_c"], writes=["gsr_g"])
                    tt("dve", sqR[:], FR3(asig), FR3(T1), ALU.mult, reads=[RK(asig), RK(T1)], writes=["sqR"])
                    tt("dve", h2[:].rearrange("p (j t) -> p j t", j=4), FR3(kT_), FR3(T1), ALU.mult, reads=[RK(kT_), RK(T1)], writes=["h2"])
                    for j in range(4):
                        tr(pbb(2)[:, j * 128:(j + 1) * 128], sqR[:, j, :], ident_b[:], reads=["sqR", "ident_b"], writes=["pb2"], inc=False)
                    for j in range(4):
                        tr(pbb(2)[:, 512 + j * 128:512 + (j + 1) * 128], h2[:, j * 128:(j + 1) * 128], ident_b[:], reads=["h2", "ident_b"],
                           writes=["pb2"], inc=(j == 3))
                    cp("act", bdtok[:], pbb(2)[:, 0:512], reads=["pb2"], writes=["bdtok"])
                    cp("act", kdtok[:], pbb(2)[:, 512:1024], reads=["pb2"], writes=["kdtok"])


                def st_R_S1(it):
                    r0 = it * 128
                    par = it % 2
                    dbg_tile = (taps is not None and it == (1 if NT > 1 else 0))
                    xat, xk = xa[par], f"xa{par}"
                    hT, hk_ = hTs[par], f"hT{par}"
                    P.tag = "%d:R.S1" % it
                    for f in range(4):
                        for hp in range(2):
                            h = 2 * f + hp
                            hk = ("HBr", h)
                            bnk = 4 + h % 3
                            ps_ = slice(hp * 64, (hp + 1) * 64)
                            mm(PB[3][:, 0:256], ktl[ps_, f, :], rat[ps_, f, :, :], reads=["ktl", "rat_r", "rat_a"], writes=["pb3"], inc=False)
                            mm(PB[3][:, 256:512], btl[ps_, f, :], rat[ps_, f, :, :], reads=["btl", "rat_r", "rat_a"], writes=["pb3"])
                            mm(PB[bnk][:, 0:128], rat[ps_, f, 1, :], btl[ps_, f, :], reads=["btl", "rat_a"], writes=[f"pb{bnk}"])
                            tt("dve", HBr[:, h, 0:512], PB[3][:, :], mask4[:], ALU.mult, reads=["pb3", "mask4"], writes=[hk])
                            tt("dve", HBr[:, h, 512:640], PB[bnk][:, 0:128], masksl[:], ALU.mult, reads=[f"pb{bnk}", "masksl"], writes=[hk])

                def st_R_S2(it):
                    r0 = it * 128
                    par = it % 2
                    dbg_tile = (taps is not None and it == (1 if NT > 1 else 0))
                    xat, xk = xa[par], f"xa{par}"
                    hT, hk_ = hTs[par], f"hT{par}"
                    P.tag = "%d:R.S2" % it
                    invert(HBr, "HBr", list(range(8)), 384)

                def st_R_S3(it):
                    r0 = it * 128
                    par = it % 2
                    dbg_tile = (taps is not None and it == (1 if NT > 1 else 0))
                    xat, xk = xa[par], f"xa{par}"
                    hT, hk_ = hTs[par], f"hT{par}"
                    P.tag = "%d:R.S3" % it
                    for f in range(4):
                        mm(PB[7][:, f * 128:(f + 1) * 128], rat[:, f, 1, :], Mb[:, f, :], start=True, stop=False, reads=["rat_a", "Mb"], writes=["pb7"], inc=False)
                        for hp in range(2):
                            h = 2 * f + hp
                            mm(PB[7][:, h * 64:(h + 1) * 64], HBr[:, h, 128:256], vtk[:, h * 64:(h + 1) * 64], start=False, stop=(hp == 1),
                               reads=[("HBr", h), "vtk"], writes=["pb7"], inc=(h == 7))
                    cp("act", xgR[:], PB[7][:, :], reads=["pb7"], writes=["xgR"])
                    for h in range(8):
                        mm(PB[3][:, h * 64:(h + 1) * 64], HBr[:, h, 384:512], xgR[:, h * 64:(h + 1) * 64], reads=[("HBr", h), "xgR"], writes=["pb3"], inc=(h == 7))
                    cp("act", ugR[:], PB[3][:, :], reads=["pb3"], writes=["ugR"])
                    for f in range(4):
                        mm(PB[3][:, f * 128:(f + 1) * 128], rat[:, f, 0, :], Mb[:, f, :], start=True, stop=False, reads=["rat_r", "Mb"], writes=["pb3"], inc=False)
                        for hp in range(2):
                            h = 2 * f + hp
                            mm(PB[3][:, h * 64:(h + 1) * 64], HBr[:, h, 256:384], ugR[:, h * 64:(h + 1) * 64], start=False, stop=False,
                               reads=[("HBr", h), "ugR"], writes=["pb3"], inc=False)
                            mm(PB[3][:, h * 64:(h + 1) * 64], HBr[:, h, 0:128], vtk[:, h * 64:(h + 1) * 64], start=False, stop=(hp == 1),
                               reads=[("HBr", h), "vtk"], writes=["pb3"], inc=(h == 7))
                    for f in range(4):
                        mm(PB[7][:, f * 128:(f + 1) * 128], bdtok[:, f * 128:(f + 1) * 128], ugR[:, f * 128:(f + 1) * 128], start=True, stop=False,
                           reads=["bdtok", "ugR"], writes=["pb7"], inc=False)
                        mm(PB[7][:, f * 128:(f + 1) * 128], kdtok[:, f * 128:(f + 1) * 128], vtk[:, f * 128:(f + 1) * 128], start=False, stop=True,
                           reads=["kdtok", "vtk"], writes=["pb7"], inc=(f == 3))
                    for f in range(4):
                        for hp in range(2):
                            ps_ = slice(hp * 64, (hp + 1) * 64)
                            stt("dve", Mf[ps_, f, ps_], Mf[ps_, f, ps_], gsr[ps_, 60 + f:61 + f], PB[7][ps_, f * 128 + hp * 64:f * 128 + (hp + 1) * 64],
                                ALU.mult, ALU.add, reads=["pb7", "gsr_g", "Mf"], writes=["Mf"])
                    cp("pool", Mb[:], Mf[:], reads=["Mf"], writes=["Mb"])

                def st_R_epi(it):
                    r0 = it * 128
                    par = it % 2
                    dbg_tile = (taps is not None and it == (1 if NT > 1 else 0))
                    xat, xk = xa[par], f"xa{par}"
                    hT, hk_ = hTs[par], f"hT{par}"
                    P.tag = "%d:R.epi" % it
                    cp("act", FRi(ytk), PB[3][:, :], reads=["pb3"], writes=[RK(ytk)])
                    if dbg_tile:
                        tap("ytok", FRi(ytk), [RK(ytk)])
                    y3, c3, s3 = h8(FRi(ytk)), h8(FRi(yc)), h8(FRi(T1))
                    red(gsr[:, 0:8], y3, reads=[RK(ytk)], writes=["gsr_e"])
                    ts("dve", gsr[:, 0:8], gsr[:, 0:8], -1.0 / 64, None, ALU.mult, reads=["gsr_e"], writes=["gsr_e"])
                    tt("dve", c3, y3, bc8(gsr[:, 0:8]), ALU.add, reads=[RK(ytk), "gsr_e"], writes=[RK(yc)])
                    tt("dve", FRi(T1), FRi(yc), FRi(yc), ALU.mult, reads=[RK(yc)], writes=[RK(T1)])
                    red(gsr[:, 8:16], s3, reads=[RK(T1)], writes=["gsr_f"])
                    act(gsr[:, 8:16], gsr[:, 8:16], AF.Sqrt, reads=["gsr_f", "cst"], writes=["gsr_f"], scale=1.0 / 64, bias=cst[:, 3:4])
                    recip(gsr[:, 8:16], gsr[:, 8:16], reads=["gsr_f"], writes=["gsr_f"])
                    tt("dve", c3, c3, bc8(gsr[:, 8:16]), ALU.mult, reads=[RK(yc), "gsr_f"], writes=[RK(yc)])
                    tt("dve", FRi(yc), FRi(yc), lnx[:, 0:512], ALU.mult, reads=[RK(yc), "lnx"], writes=[RK(yc)])
                    tt("dve", FRi(yc), FRi(yc), lnx[:, 512:1024], ALU.add, reads=[RK(yc), "lnx"], writes=[RK(yc)])
                    tt("dve", s3, h8(FRi(vT)), bc8(gsr[:, 48:56]), ALU.mult, reads=[RK(vT), "gsr_b"], writes=[RK(T1)])
                    tt("dve", FRi(yc), FRi(yc), FRi(T1), ALU.add, reads=[RK(yc), RK(T1)], writes=[RK(yc)])
                    tt("dve", ytbR[:], FRi(yc), FRi(gtok), ALU.mult, reads=[RK(yc), RK(gtok)], writes=["ytbR"])
                    for j in range(4):
                        tr(pbb(7)[:, j * 128:(j + 1) * 128], ytbR[:, j * 128:(j + 1) * 128], ident_b[:], reads=["ytbR", "ident_b"],
                           writes=["pb7"], inc=(j == 3))
                    cp("act", ybT[:], pbb(7)[:, 0:512].rearrange("p (j t) -> p j t", j=4), reads=["pb7"], writes=["ybT"])


                def st_merge(it):
                    r0 = it * 128
                    par = it % 2
                    dbg_tile = (taps is not None and it == (1 if NT > 1 else 0))
                    xat, xk = xa[par], f"xa{par}"
                    hT, hk_ = hTs[par], f"hT{par}"
                    P.tag = "%d:merge" % it
                    sga, sgb, m1, m2 = sg, asig, kk, cum
                    for g in range(2):
                        for bank_, nm_ in ((0, "gla"), (1, "glb")):
                            for hh_ in range(2):
                                wv, wkey = wblk(f"{nm_}{g}{hh_}")
                                for k in range(8):
                                    mm(PB[bank_][:, hh_ * 256:(hh_ + 1) * 256], hT[:, k, :], wv[:, k, :], start=(k == 0), stop=(k == 7),
                                       reads=[wkey, hk_], writes=[f"pb{bank_}"])
                        act(msa[:], PB[0][:, :], AF.Sigmoid, reads=["pb0"], writes=["msa"])
                        act(msb[:], PB[1][:, :], AF.Sigmoid, reads=["pb1"], writes=["msb"])
                        for nm_, yT_, yk_, ms_, msk_, mo_, mok_ in (("wbg", yaT, "yaT", msa, "msa", mm1, "mm1"), ("wbr", ybT, "ybT", msb, "msb", mm2, "mm2")):
                            for hh_ in range(2):
                                wv, wkey = wblk(f"{nm_}{g}{hh_}")
                                for k in range(4):
                                    mm(PB[2][:, hh_ * 256:(hh_ + 1) * 256], yT_[:, k, :], wv[:, k, :], start=(k == 0), stop=(k == 3),
                                       reads=[wkey, yk_], writes=["pb2"])
                            tt("dve", mo_[:], ms_[:], PB[2][:, :], ALU.mult, reads=[msk_, "pb2"], writes=[mok_])
                        tt("dve", mgt[:, g * 512:(g + 1) * 512], mm1[:], mm2[:], ALU.add, reads=["mm1", "mm2"], writes=[("mgt", g)])
                        for j in range(4):
                            tr(pbb(2)[:, j * 128:(j + 1) * 128], mgt[:, g * 512 + j * 128:g * 512 + (j + 1) * 128], ident_b[:], reads=[("mgt", g), "ident_b"],
                               writes=["pb2"], inc=(j == 3))
                        cp("act", mg[:, g * 4:(g + 1) * 4, :], pbb(2)[:, 0:512].rearrange("p (j t) -> p j t", j=4), reads=["pb2"], writes=[("mg", g)])
                    for hf in range(2):
                        bk = hf
                        for hh_ in range(2):
                            wv, wkey = wblk(f"wo{2 * hf + hh_}")
                            for k in range(8):
                                mm(PB[bk][:, hh_ * 256:(hh_ + 1) * 256], mg[:, k, :], wv[:, k, :], start=(k == 0), stop=(k == 7),
                                   reads=[("mg", 0), ("mg", 1), wkey], writes=[f"pb{bk}"])
                        tt("dve", xat[:, hf * 512:(hf + 1) * 512], xat[:, hf * 512:(hf + 1) * 512], PB[bk][:, :], ALU.add,
                           reads=[xk, f"pb{bk}"], writes=[xk])
                    P.dma("sp", x1_d[r0:r0 + 128, :], xat[:], reads=[xk], writes=[("x1_dram", it)], nbytes=524288)

                st_norm(0)
                st_G_prep(0)
                for it in range(NT):
                    st_G_S1(it)
                    st_R_prep(it)
                    st_G_S2(it)
                    st_G_S3(it)
                    st_G_epi(it)
                    st_R_S1(it)
                    if it + 1 < NT:
                        st_norm(it + 1)
                        st_G_prep(it + 1)
                    st_R_S2(it)
                    st_R_S3(it)
                    st_R_epi(it)
                    st_merge(it)
                P.replay()

        GS = 4 if NT % 4 == 0 else (2 if NT % 2 == 0 else 1)
        TG = GS * 128
        with ExitStack() as sB:
            wfi = sbt(sB, "wfi", [128, 8, 2 * D_FF], BF16)
            wfo = sbt(sB, "wfo", [128, 22, D], BF16)
            cfT = sbt(sB, "cfT_sb", [128, 66], F32)
            nf_bc = sbt(sB, "nf_bc_sb", [128, D], F32)
            xs = sbt(sB, "xs", [128, GS, D], F32)
            xnB = sbt(sB, "xnB", [128, D], F32)
            hT2 = sbt(sB, "hT2", [128, 8, TG], BF16)
            hid = sbt(sB, "hid", [128, 22, TG], BF16)
            gbuf = [sbt(sB, f"gbuf{i}", [128, TG + 2], F32) for i in range(2)]
            cacc = [sbt(sB, f"cacc{i}", [128, TG], F32) for i in range(2)]
            carry = sbt(sB, "carryB", [128, 22, 2], F32)
            for k in range(8):
                P.dma("pool", wfi[:, k, :], wfi_d[k * 128:(k + 1) * 128, :], writes=["wfi"])
            P.dma("sp", xnB[:], gscr_d[:, D:2 * D], reads=["gscr"], writes=["xnB"])
            for c in range(22):
                s_ = c % GS if GS > 1 else 0
                P.dma("sp", xs[:, s_, :], wfo_d[c * 128:(c + 1) * 128, :], writes=[("xs", s_)])
                tt("dve", wfo[:, c, :], xs[:, s_, :], xnB[:], ALU.mult, reads=[("xs", s_), "xnB"], writes=["wfo"])
            P.dma("sp", cfT[:], cfT_d, writes=["cfT"])
            P.dma("sp", nf_bc[:], nf_d, writes=["nf_bc"])
            P.emit("pool", lambda e: e.memset(carry[:], 0.0), writes=["carryB"])
            src_d = x_d if mode == "skipA" else x1_d
            for g in range(NT // GS):
                for s in range(GS):
                    r0 = (g * GS + s) * 128
                    P.dma("sp", xs[:, s, :], src_d[r0:r0 + 128, :], writes=[("xs", s)], nbytes=524288)
                    norm_to_featmajor(xs[:, s, :], [("xs", s)], xnB[:], ["xnB"], g2T, sh2T, "g2T", hT2, "hT2", s * 128, 0)
                for c in range(22):
                    gb = gbuf[c % 2]
                    gk = f"gbuf{c % 2}"
                    ca = cacc[c % 2]
                    ck = f"cacc{c % 2}"
                    bg = c % 2
                    for k in range(8):
                        mm(PB[bg][:, 0:TG], wfi[:, k, c * 128:(c + 1) * 128], hT2[:, k, :], start=(k == 0), stop=(k == 7),
                           reads=["wfi", "hT2"], writes=[f"pb{bg}"])
                    for k in range(8):
                        mm(PB[2 + bg][:, 0:TG], wfi[:, k, D_FF + c * 128:D_FF + (c + 1) * 128], hT2[:, k, :], start=(k == 0),
                           stop=(k == 7), reads=["wfi", "hT2"], writes=[f"pb{2 + bg}"])
                    cp("pool", gb[:, 0:2], carry[:, c, :], reads=["carryB"], writes=[gk])
                    cp("act", gb[:, 2:TG + 2], PB[bg][:, 0:TG], reads=[f"pb{bg}"], writes=[gk])
                    cp("pool", carry[:, c, :], gb[:, TG:TG + 2], reads=[gk], writes=["carryB"])
                    ts("dve", ca[:], gb[:, 0:TG], cfT[:, c * 3:c * 3 + 1], None, ALU.mult, reads=[gk, "cfT"], writes=[ck])
                    stt("dve", ca[:], gb[:, 1:TG + 1], cfT[:, c * 3 + 1:c * 3 + 2], ca[:], ALU.mult, ALU.add, reads=[gk, ck, "cfT"], writes=[ck])
                    stt("dve", ca[:], gb[:, 2:TG + 2], cfT[:, c * 3 + 2:c * 3 + 3], ca[:], ALU.mult, ALU.add, reads=[gk, ck, "cfT"], writes=[ck])
                    act(ca[:], ca[:], AF.Silu, reads=[ck], writes=[ck])
                    tt("dve", hid[:, c, :], ca[:], PB[2 + bg][:, 0:TG], ALU.mult, reads=[ck, f"pb{2 + bg}"], writes=[("hid", c)])
                for s in range(GS):
                    r0 = (g * GS + s) * 128
                    for hf in range(2):
                        bk = 4 + hf
                        for c in range(22):
                            mm(PB[bk][:, :], hid[:, c, s * 128:(s + 1) * 128], wfo[:, c, hf * 512:(hf + 1) * 512], start=(c == 0),
                               stop=(c == 21), reads=[("hid", c), "wfo"], writes=[f"pb{bk}"])
                        xsl = xs[:, s, hf * 512:(hf + 1) * 512]
                        tt("dve", xsl, xsl, PB[bk][:, :], ALU.add, reads=[f"pb{bk}", ("xs", s)], writes=[("xs", s)])
                    rms_rstd(xs[:, s, :], [("xs", s)], xnB[:], ["xnB"], 2)
                    stt("dve", xs[:, s, :], xs[:, s, :], small[:, 2:3], nf_bc[:], ALU.mult, ALU.mult,
                        reads=[("xs", s), ("small", 2), "nf_bc"], writes=[("xs", s)])
                    P.dma("sp", y_d[r0:r0 + 128, :], xs[:, s, :], reads=[("xs", s)], writes=[("y_dram", r0)], nbytes=524288)
            P.barrier()
            P.replay()
    return nc


def _prep_inputs(inp, T):
    f = lambda a: np.ascontiguousarray(np.asarray(a, np.float32))
    shared = {
        "w_ada": f(inp["w_ada"][0]),
        "badaT": _fm(inp["b_ada"][0], 48),
        "bgate": _bc(np.concatenate([np.asarray(inp["b_ada"][0][2048:3072]), np.asarray(inp["b_ada"][0][5120:6144])])),
        "n1T": _fm(inp["norm1_w"][0], 8),
        "n2T": _fm(inp["norm2_w"][0], 8),
        "nf_bc": _bc(inp["norm_f_w"]),
        "w_in": f(inp["w_in"][0]),
        "cgT": np.ascontiguousarray(f(inp["conv_gdn"][0]).T.reshape(12, 128, 4).transpose(1, 0, 2).reshape(128, 48)),
        "gpar_bc": _bc(np.concatenate([np.asarray(inp["a_log"][0]), np.asarray(inp["dt_bias"][0])])),
        "onorm_bc": _bc(inp["onorm_gdn"][0]),
        "wbg": f(inp["w_branch_gdn"][0]),
        "wbr": f(inp["w_branch_rwkv"][0]),
        "wout": f(inp["w_out"][0]),
        "muT": _fm(inp["mu_rwkv"][0], 15),
        "rwp": np.ascontiguousarray(np.concatenate(
            [_fm(inp["w0"][0], 4), _fm(inp["a0"][0], 4), _fm(inp["k_k"][0], 4), _fm(inp["k_a"][0], 4),
             _fm(np.asarray(inp["r_k"][0]).reshape(-1), 4)], axis=1)),
        "lnx_bc": _bc(np.concatenate([np.asarray(inp["lnx_w"][0]), np.asarray(inp["lnx_b"][0])])),
        "lora": np.ascontiguousarray(np.concatenate([f(inp["w2"][0]), f(inp["a2"][0])], axis=0)),
        "g2": f(inp["g2"][0]),
        "w_ffn_in": f(inp["w_ffn_in"][0]),
        "cfT": np.ascontiguousarray(f(inp["conv_ffn"][0]).T.reshape(22, 128, 3).transpose(1, 0, 2).reshape(128, 66)),
        "w_ffn_out": f(inp["w_ffn_out"][0]),
    }
    shared.update(_consts())
    maps = []
    x = np.asarray(inp["x"], np.float32)
    c = np.asarray(inp["c"], np.float32)
    for b in range(x.shape[0]):
        m = dict(shared)
        m["x"] = np.ascontiguousarray(x[b, :T])
        m["cT"] = _fm(c[b], 8)
        maps.append(m)
    return maps


_NC_CACHE = {}


def kernel(**inputs):
    x = np.asarray(inputs["x"])
    B, T, _ = x.shape
    maps = _prep_inputs(inputs, T)
    if T not in _NC_CACHE:
        _NC_CACHE[T] = build_nc(T)
    nc = _NC_CACHE[T]
    res = run_bass_kernel_spmd(nc, maps, core_ids=list(range(B)))
    out = np.stack([np.asarray(r["y"], np.float32) for r in res.results], axis=0)
    return out
```
